# Optimizing a Trainium2 kernel written in Bass

```python
import math
import jax, jax.numpy as jnp
from jax import lax
import numpy as np

D_MODEL = 1024
BATCH = 4
SEQ = 8192
DEPTH = 1

MOBA_HEADS = 8
MOBA_HEAD_DIM = 64
MOBA_BLOCK = 256
MOBA_TOPK = 3
MLA_HEADS = 8
MLA_Q_RANK = 256
MLA_KV_RANK = 128
MLA_NOPE_DIM = 64
MLA_ROPE_DIM = 32
MLA_V_DIM = 64
ROPE_THETA = 10000.0
REL_BUCKETS = 32
REL_MAX_DIST = 128
D_FF = 2816
CONV_WIDTH = 3
N_BRANCH = 2
Q_BLOCK = 128
EPS = 1e-6

MOBA_WIDTH = MOBA_HEADS * MOBA_HEAD_DIM
MLA_QK_DIM = MLA_NOPE_DIM + MLA_ROPE_DIM
MLA_WIDTH = MLA_HEADS * MLA_V_DIM
IN_COLS = 3 * MOBA_WIDTH + MLA_Q_RANK + MLA_KV_RANK + MLA_ROPE_DIM + N_BRANCH * D_MODEL
IN_SPLITS = [MOBA_WIDTH, 2 * MOBA_WIDTH, 3 * MOBA_WIDTH,
             3 * MOBA_WIDTH + MLA_Q_RANK,
             3 * MOBA_WIDTH + MLA_Q_RANK + MLA_KV_RANK,
             3 * MOBA_WIDTH + MLA_Q_RANK + MLA_KV_RANK + MLA_ROPE_DIM]

kernel_name = "hybrid_moba_mla_gated_convffn"


def rms_norm(x, g):
    x32 = x.astype(jnp.float32)
    y = x32 * lax.rsqrt(jnp.mean(x32 * x32, axis=-1, keepdims=True) + EPS)
    return (y * g.astype(jnp.float32)).astype(x.dtype)


def t5_bucket(rel):
    n = jnp.maximum(rel, 0)
    max_exact = REL_BUCKETS // 2
    nf = jnp.maximum(n, 1).astype(jnp.float32)
    large = max_exact + (jnp.log(nf / max_exact) / math.log(REL_MAX_DIST / max_exact)
                         * (REL_BUCKETS - max_exact)).astype(jnp.int32)
    large = jnp.minimum(large, REL_BUCKETS - 1)
    return jnp.where(n < max_exact, n, large)


def rope_tables(seq, dim, dtype):
    inv_freq = ROPE_THETA ** (-jnp.arange(0, dim, 2, dtype=jnp.float32) / dim)
    ang = jnp.arange(seq, dtype=jnp.float32)[:, None] * inv_freq[None, :]
    return jnp.cos(ang).astype(dtype), jnp.sin(ang).astype(dtype)


def apply_rope(x, cos, sin):
    half = x.shape[-1] // 2
    x1, x2 = x[..., :half], x[..., half:]
    return jnp.concatenate([x1 * cos - x2 * sin, x2 * cos + x1 * sin], axis=-1)


def to_heads(t, n_heads):
    b, s, _ = t.shape
    return t.reshape(b, s, n_heads, -1).transpose(0, 2, 1, 3)


def merge_heads(t):
    b, h, s, d = t.shape
    return t.transpose(0, 2, 1, 3).reshape(b, s, h * d)


def moba_attention(q, k, v, rel_bias):
    b, h, s, dh = q.shape
    nb = -(-s // MOBA_BLOCK)
    pad = nb * MOBA_BLOCK - s
    k_p = jnp.pad(k, ((0, 0), (0, 0), (0, pad), (0, 0)))
    v_p = jnp.pad(v, ((0, 0), (0, 0), (0, pad), (0, 0)))
    k_blocks = k_p.reshape(b, h, nb, MOBA_BLOCK, dh)
    v_blocks = v_p.reshape(b, h, nb, MOBA_BLOCK, dh)
    k_mean = jnp.mean(k_blocks, axis=3)

    pos = jnp.arange(s, dtype=jnp.int32)
    q_blk = pos // MOBA_BLOCK
    gate = jnp.einsum('bhsd,bhnd->bhsn', q, k_mean).astype(jnp.float32)
    past = jnp.arange(nb, dtype=jnp.int32)[None, :] < q_blk[:, None]
    gate = jnp.where(past, gate, -jnp.inf)
    kk = min(MOBA_TOPK, nb)
    _, idx = lax.top_k(gate, kk)

    nc = s // Q_BLOCK
    q_c = q.reshape(b, h, nc, Q_BLOCK, dh).transpose(2, 0, 1, 3, 4)
    idx_c = idx.reshape(b, h, nc, Q_BLOCK, kk).transpose(2, 0, 1, 3, 4)
    bias_h = rel_bias.T
    bi = jnp.arange(b)[:, None, None, None]
    hi = jnp.arange(h)[None, :, None, None]
    offs = jnp.arange(MOBA_BLOCK, dtype=jnp.int32)
    scale = MOBA_HEAD_DIM ** -0.5

    def chunk(args):
        qc, ic, c = args
        q_pos = c * Q_BLOCK + jnp.arange(Q_BLOCK, dtype=jnp.int32)
        qb = (c * Q_BLOCK) // MOBA_BLOCK
        kb = k_blocks[bi, hi, ic]
        vb = v_blocks[bi, hi, ic]
        k_pos_sel = ic[..., None] * MOBA_BLOCK + offs
        rel_sel = q_pos[:, None, None] - k_pos_sel
        s_sel = (jnp.einsum('bhqd,bhqjkd->bhqjk', qc, kb).astype(jnp.float32) * scale
                 + bias_h[hi[..., None], t5_bucket(rel_sel)].astype(jnp.float32))
        s_sel = jnp.where((ic < qb)[..., None], s_sel, -jnp.inf)
        s_sel = s_sel.reshape(b, h, Q_BLOCK, kk * MOBA_BLOCK)
        start = qb * MOBA_BLOCK
        k_own = lax.dynamic_slice_in_dim(k_p, start, MOBA_BLOCK, axis=2)
        v_own = lax.dynamic_slice_in_dim(v_p, start, MOBA_BLOCK, axis=2)
        rel_own = q_pos[:, None] - (start + offs)[None, :]
        s_own = (jnp.einsum('bhqd,bhkd->bhqk', qc, k_own).astype(jnp.float32) * scale
                 + bias_h[:, t5_bucket(rel_own)].astype(jnp.float32))
        s_own = jnp.where(rel_own >= 0, s_own, -jnp.inf)
        p = jax.nn.softmax(jnp.concatenate([s_sel, s_own], axis=-1), axis=-1).astype(v.dtype)
        p_sel = p[..., :kk * MOBA_BLOCK].reshape(b, h, Q_BLOCK, kk, MOBA_BLOCK)
        p_own = p[..., kk * MOBA_BLOCK:]
        return (jnp.einsum('bhqjk,bhqjkd->bhqd', p_sel, vb)
                + jnp.einsum('bhqk,bhkd->bhqd', p_own, v_own))

    out = lax.map(chunk, (q_c, idx_c, jnp.arange(nc, dtype=jnp.int32)))
    return out.transpose(1, 2, 0, 3, 4).reshape(b, h, s, dh)


def mla_attention(q_nope, q_rope, k_nope, k_rope, v):
    b, h, s, _ = q_nope.shape
    nc = s // Q_BLOCK
    qn_c = q_nope.reshape(b, h, nc, Q_BLOCK, -1).transpose(2, 0, 1, 3, 4)
    qr_c = q_rope.reshape(b, h, nc, Q_BLOCK, -1).transpose(2, 0, 1, 3, 4)
    k_pos = jnp.arange(s, dtype=jnp.int32)
    scale = MLA_QK_DIM ** -0.5

    def chunk(args):
        qn, qr, c = args
        q_pos = c * Q_BLOCK + jnp.arange(Q_BLOCK, dtype=jnp.int32)
        sc = (jnp.einsum('bhqd,bhkd->bhqk', qn, k_nope)
              + jnp.einsum('bhqr,bkr->bhqk', qr, k_rope)).astype(jnp.float32) * scale
        sc = jnp.where(k_pos[None, :] <= q_pos[:, None], sc, -jnp.inf)
        p = jax.nn.softmax(sc, axis=-1).astype(v.dtype)
        return jnp.einsum('bhqk,bhkd->bhqd', p, v)

    out = lax.map(chunk, (qn_c, qr_c, jnp.arange(nc, dtype=jnp.int32)))
    return out.transpose(1, 2, 0, 3, 4).reshape(b, h, s, -1)


def causal_dwconv(u, w, bias):
    s = u.shape[1]
    up = jnp.pad(u, ((0, 0), (CONV_WIDTH - 1, 0), (0, 0)))
    y = bias
    for j in range(CONV_WIDTH):
        y = y + w[j] * up[:, j:j + s]
    return y


def setup_inputs(seed: int = 0) -> dict:
    key = jax.random.key(seed)
    ks = jax.random.split(key, 20)

    def nrm(k, shape, scale):
        return jax.random.normal(k, shape, jnp.float32) * scale

    L = DEPTH
    conv_w = nrm(ks[14], (L, CONV_WIDTH, 2 * D_FF), 0.2)
    conv_w = conv_w.at[:, CONV_WIDTH - 1].add(1.0)
    return {
        "x": nrm(ks[0], (BATCH, SEQ, D_MODEL), 1.0),
        "norm_attn_g": 1.0 + nrm(ks[1], (L, D_MODEL), 0.02),
        "w_in": nrm(ks[2], (L, D_MODEL, IN_COLS), D_MODEL ** -0.5),
        "b_gate": nrm(ks[3], (L, N_BRANCH * D_MODEL), 0.1),
        "q_norm_g": 1.0 + nrm(ks[4], (L, MLA_Q_RANK), 0.02),
        "w_uq": nrm(ks[5], (L, MLA_Q_RANK, MLA_HEADS * MLA_QK_DIM), MLA_Q_RANK ** -0.5),
        "kv_norm_g": 1.0 + nrm(ks[6], (L, MLA_KV_RANK), 0.02),
        "w_ukv": nrm(ks[7], (L, MLA_KV_RANK, MLA_HEADS * (MLA_NOPE_DIM + MLA_V_DIM)), MLA_KV_RANK ** -0.5),
        "rel_bias": nrm(ks[8], (REL_BUCKETS, MOBA_HEADS), 0.3),
        "w_branch_moba": nrm(ks[9], (L, MOBA_WIDTH, D_MODEL), MOBA_WIDTH ** -0.5),
        "w_branch_mla": nrm(ks[10], (L, MLA_WIDTH, D_MODEL), MLA_WIDTH ** -0.5),
        "w_out": nrm(ks[11], (L, D_MODEL, D_MODEL), D_MODEL ** -0.5),
        "norm_ffn_g": 1.0 + nrm(ks[12], (L, D_MODEL), 0.02),
        "w_up": nrm(ks[13], (L, D_MODEL, 2 * D_FF), D_MODEL ** -0.5),
        "conv_w": conv_w,
        "conv_b": nrm(ks[15], (L, 2 * D_FF), 0.02),
        "w_down": nrm(ks[16], (L, D_FF, D_MODEL), D_FF ** -0.5),
        "norm_final_g": 1.0 + nrm(ks[17], (D_MODEL,), 0.02),
    }


def reference(x, norm_attn_g, w_in, b_gate, q_norm_g, w_uq, kv_norm_g, w_ukv, rel_bias,
              w_branch_moba, w_branch_mla, w_out, norm_ffn_g, w_up, conv_w, conv_b, w_down,
              norm_final_g):
    b, s, _ = x.shape
    cos, sin = rope_tables(s, MLA_ROPE_DIM, x.dtype)
    h = x
    for l in range(DEPTH):
        xn = rms_norm(h, norm_attn_g[l])
        proj = xn @ w_in[l]
        q_a, k_a, v_a, c_q, c_kv, k_rope, gates = jnp.split(proj, IN_SPLITS, axis=-1)

        y_a = merge_heads(moba_attention(to_heads(q_a, MOBA_HEADS), to_heads(k_a, MOBA_HEADS),
                                         to_heads(v_a, MOBA_HEADS), rel_bias))

        q = (rms_norm(c_q, q_norm_g[l]) @ w_uq[l]).reshape(b, s, MLA_HEADS, MLA_QK_DIM)
        q_nope = q[..., :MLA_NOPE_DIM]
        q_rope = apply_rope(q[..., MLA_NOPE_DIM:], cos[:, None, :], sin[:, None, :])
        kv = (rms_norm(c_kv, kv_norm_g[l]) @ w_ukv[l]).reshape(b, s, MLA_HEADS, MLA_NOPE_DIM + MLA_V_DIM)
        k_nope = kv[..., :MLA_NOPE_DIM]
        v_b = kv[..., MLA_NOPE_DIM:]
        k_r = apply_rope(k_rope, cos, sin)
        y_b = merge_heads(mla_attention(q_nope.transpose(0, 2, 1, 3), q_rope.transpose(0, 2, 1, 3),
                                        k_nope.transpose(0, 2, 1, 3), k_r,
                                        v_b.transpose(0, 2, 1, 3)))

        g = jax.nn.sigmoid(gates + b_gate[l]).reshape(b, s, N_BRANCH, D_MODEL)
        mixed = g[..., 0, :] * (y_a @ w_branch_moba[l]) + g[..., 1, :] * (y_b @ w_branch_mla[l])
        h = h + mixed @ w_out[l]

        hn = rms_norm(h, norm_ffn_g[l])
        u = causal_dwconv(hn @ w_up[l], conv_w[l], conv_b[l])
        u_gate, u_val = u[..., :D_FF], u[..., D_FF:]
        h = h + (jax.nn.silu(u_gate) * u_val) @ w_down[l]
    return rms_norm(h, norm_final_g)
```

```python
import math
import os
from contextlib import ExitStack

import ml_dtypes
import numpy as np

import concourse.bass as bass
import concourse.mybir as mybir
from concourse.bass_utils import run_bass_kernel_spmd

F32 = mybir.dt.float32
BF16 = mybir.dt.bfloat16
AF = mybir.ActivationFunctionType
ALU = mybir.AluOpType
AX = mybir.AxisListType

NEG = -30000.0
EPS = 1e-6
NT = 64
NQT = 33
NQ = NQT * 128
DEBUG = False
STOP = None
STRICT = bool(int(os.environ.get('KSTRICT', '0')))


class Sem:
    def __init__(self, nc, name):
        self.h = nc.alloc_semaphore(name)
        self.v = 0


class Res:
    __slots__ = ("w", "r", "excl")

    def __init__(self, excl=False):
        self.w = None
        self.r = {}
        self.excl = excl


class Eng:
    def __init__(self, eng, sem):
        self.e = eng
        self.sem = sem
        self.seen = {}

    def wait(self, sem, val):
        if self.seen.get(id(sem), 0) >= val:
            return
        self.e.wait_ge(sem.h, val)
        self.seen[id(sem)] = val


class K:
    def __init__(self, nc):
        self.nc = nc
        self.sems = []
        self.pe = Eng(nc.tensor, self.sem("pe"))
        self.act = Eng(nc.scalar, self.sem("act"))
        self.dve = Eng(nc.vector, self.sem("dve"))
        self.pool = Eng(nc.gpsimd, self.sem("pool"))
        self.sp = Eng(nc.sync, self.sem("sp"))
        self.engs = [self.pe, self.act, self.dve, self.pool, self.sp]

    def sem(self, name):
        s = Sem(self.nc, name)
        self.sems.append(s)
        return s

    def _deps(self, eng, reads, writes):
        for r in reads:
            if r.w is not None:
                eng.wait(*r.w)
        for w in writes:
            if w.w is not None and (STRICT or w.w[0] is not eng.sem):
                eng.wait(*w.w)
            for s, v in w.r.values():
                if STRICT or s is not eng.sem:
                    eng.wait(s, v)

    def _commit(self, ev, reads, writes):
        for w in writes:
            w.w = ev
            w.r = {}
        for r in reads:
            if r not in writes:
                r.r[id(ev[0])] = ev

    def op(self, eng, reads, writes, fn):
        ex = [r for r in reads if r.excl and r not in writes]
        if ex:
            writes = list(writes) + ex
        self._deps(eng, reads, writes)
        ins = fn()
        eng.sem.v += 1
        ins.then_inc(eng.sem.h, 1)
        self._commit((eng.sem, eng.sem.v), reads, writes)

    def dma(self, q, sem, items):
        for o, i, reads, writes in items:
            self._deps(q, reads, writes)
        for o, i, reads, writes in items:
            q.e.dma_start(out=o, in_=i).then_inc(sem.h, 16)
            sem.v += 16
        ev = (sem, sem.v)
        for o, i, reads, writes in items:
            self._commit(ev, reads, writes)

    def barrier(self):
        for e in self.engs:
            for s in self.sems:
                if s.v > 0:
                    e.wait(s, s.v)


class _Stop(Exception):
    pass


def chk(n):
    if STOP in ("P0a", "P0b") and int(os.environ.get("KSTEP", "99")) == n:
        raise _Stop()


def run_pipelined(gen_list, offset):
    gens = []
    nxt = 0
    while gens or nxt < len(gen_list):
        if nxt < len(gen_list) and len(gens) < 2 and (not gens or gens[-1][1] >= offset):
            gens.append([gen_list[nxt], 0])
            nxt += 1
        for ge in list(gens):
            try:
                next(ge[0])
                ge[1] += 1
            except StopIteration:
                gens.remove(ge)


def build_program():
    nc = bass.Bass("TRN2", target_bir_lowering=False)
    k = K(nc)
    PE, ACT, DVE, POOL, SP = k.pe, k.act, k.dve, k.pool, k.sp

    def din(name, shape, dt=F32):
        return nc.dram_tensor(name, list(shape), dt, kind="ExternalInput").ap()

    def dscr(name, shape, dt):
        kind = "ExternalOutput" if DEBUG else "Internal"
        return nc.dram_tensor(name, list(shape), dt, kind=kind).ap()

    xc = din("xc", [8192, 1024])
    w_in = din("w_in", [1024, 4000])
    w_uq = din("w_uq", [256, 768])
    w_ukv = din("w_ukv", [128, 1024])
    w_bm = din("w_bm", [512, 1024])
    w_bl = din("w_bl", [512, 1024])
    w_out = din("w_out", [1024, 1024])
    w_up = din("w_up", [1024, 5632])
    w_dn = din("w_dn", [2816, 1024])
    gA_d = din("gA", [128, 8])
    gF_d = din("gF", [128, 8])
    gQ_d = din("gQ", [128, 2])
    gKV_d = din("gKV", [128, 1])
    bg_d = din("bg", [128, 16])
    cw_d = din("cw", [128, 44 * 3])
    cb_d = din("cb", [128, 44])
    gO_d = din("gO", [128, 1024])
    kval_d = din("kval", [128, 64])
    gbias_d = din("gbias", [128, 256])
    hflag_d = din("hflag", [128, 1])
    b31_d = din("b31", [128, 8])
    csk_d = din("csk", [64, 128, 32])
    csq_d = din("csq", [NQT, 128, 256])
    oh_d = din("oh", [32, 8192], BF16)
    nf_d = din("nf", [8, 128, 6 * 512], BF16)
    nfh_d = din("nfh", [8, 128, 4 * 128], BF16)
    cm_d = din("cm", [128, 4 * 512], BF16)
    idb_d = din("idb", [128, 128], BF16)
    out_d = nc.dram_tensor("out", [4096, 1024], F32, kind="ExternalOutput").ap()

    KTa = dscr("KTa", [8, 64, 8192], BF16)
    KTb = dscr("KTb", [8, 64, 8192], BF16)
    KRT = dscr("KRT", [32, 8192], BF16)
    Va = dscr("Va", [8, 128, 64, 128], BF16)
    Vb = dscr("Vb", [8, 128, 64, 128], BF16)
    QTa = dscr("QTa", [8, 96, NQ], BF16)
    QTb = dscr("QTb", [8, 96, NQ], BF16)
    YT = dscr("YT", [16, 64, NQ], BF16)
    H1 = dscr("H1", [NQ, 1024], F32)

    psBig = nc.alloc_psum_tensor("psbig", [128, 4096], F32)
    psT = [psBig[:, i * 512:(i + 1) * 512] for i in range(8)]
    psR = [Res(excl=True) for _ in range(8)]
    pctr = [0]

    def ps_next(lo=0, hi=8):
        i = lo + pctr[0] % (hi - lo)
        pctr[0] += 1
        return psT[i], psR[i]

    def psbf(p):
        return p[:, :].bitcast(BF16)

    def sb(name, shape, dt):
        return nc.alloc_sbuf_tensor(name, list(shape), dt)

    idb = sb("idb_s", [128, 128], BF16)
    epsb = sb("epsb", [128, 1], F32)
    stat = sb("stat", [128, 16], F32)
    kval = sb("kval_s", [128, 64], F32)
    hflag = sb("hflag_s", [128, 1], F32)
    R_const = Res()
    R_stat = [Res() for _ in range(4)]
    sc = [0]

    k.dma(SP, k.sem("ld0"), [
        (idb[:], idb_d[:, :], [], [R_const]),
        (kval[:], kval_d[:, :], [], [R_const]),
        (hflag[:], hflag_d[:, :], [], [R_const]),
    ])
    k.op(POOL, [], [R_const], lambda: nc.gpsimd.memset(epsb[:], EPS))
    if STOP == "W0":
        k.barrier()
        return nc

    def rstd_of(src_ap, n, reads, junk, Rjunk):
        i = sc[0] % 4
        sc[0] += 1
        R = R_stat[i]
        ss = stat[:, 4 * i:4 * i + 1]
        sd = stat[:, 4 * i + 1:4 * i + 2]
        rs = stat[:, 4 * i + 2:4 * i + 3]
        k.op(ACT, reads, [R, Rjunk], lambda: nc.scalar.activation(out=junk, in_=src_ap, func=AF.Square, accum_out=ss))
        k.op(ACT, [R, R_const], [R], lambda: nc.scalar.activation(out=sd, in_=ss, func=AF.Sqrt, scale=1.0 / n, bias=epsb[:, 0:1]))
        k.op(DVE, [R], [R], lambda: nc.vector.reciprocal(out=rs, in_=sd))
        return rs, R

    def transposes(src_list, reads, dst_ap_fn, dst_writes, rows, copy_eng, alloc=None):
        p, R = (alloc or ps_next)()
        pv = psbf(p)
        n = len(src_list)
        for j, s in enumerate(src_list):
            k.op(PE, reads + [R_const], [R], lambda s=s, j=j: nc.tensor.transpose(out=pv[0:rows, j * 128:(j + 1) * 128], in_=s, identity=idb[:]))
        return pv, R

    def load_weight(dst, Rdst, src, nchunks, cols, scale_ap_fn, stage, Rstage, ssem, c0=0, rows=128):
        for c in range(nchunks):
            for off in range(0, cols, 2048):
                w = min(2048, cols - off)
                i = load_weight.ctr % 2
                load_weight.ctr += 1
                k.dma(SP, ssem[i], [(stage[i][0:rows, 0:w], src[c * rows:(c + 1) * rows, c0 + off:c0 + off + w], [], [Rstage[i]])])
                sap = scale_ap_fn(c) if scale_ap_fn else None
                if sap is not None:
                    if load_weight.ctr % 2:
                        k.op(ACT, [Rstage[i], R_const], [Rdst], lambda i=i, c=c, off=off, w=w, sap=sap: nc.scalar.activation(out=dst[0:rows, c, off:off + w], in_=stage[i][0:rows, 0:w], func=AF.Copy, scale=sap))
                    else:
                        k.op(DVE, [Rstage[i], R_const], [Rdst], lambda i=i, c=c, off=off, w=w, sap=sap: nc.vector.tensor_scalar(out=dst[0:rows, c, off:off + w], in0=stage[i][0:rows, 0:w], scalar1=sap, scalar2=None, op0=ALU.mult))
                else:
                    if load_weight.ctr % 2:
                        k.op(ACT, [Rstage[i]], [Rdst], lambda i=i, c=c, off=off, w=w: nc.scalar.copy(out=dst[0:rows, c, off:off + w], in_=stage[i][0:rows, 0:w]))
                    else:
                        k.op(DVE, [Rstage[i]], [Rdst], lambda i=i, c=c, off=off, w=w: nc.vector.tensor_copy(out=dst[0:rows, c, off:off + w], in_=stage[i][0:rows, 0:w]))
    load_weight.ctr = 0
    transposes_g = transposes
    wsem = [k.sem("ws0"), k.sem("ws1")]

    with ExitStack() as es:
        def tb(name, shape, dt):
            return es.enter_context(nc.sbuf_tensor(name, list(shape), dt))

        W0 = tb("W0", [128, 8, 1952], BF16); RW0 = Res()
        Wuq = tb("Wuq", [128, 2, 768], BF16); RWuq = Res()
        Wukv = tb("Wukv", [128, 1, 1024], BF16); RWukv = Res()
        gA = tb("gA_s", [128, 8], F32)
        gQ = tb("gQ_s", [128, 2], F32)
        gKV = tb("gKV_s", [128, 1], F32)
        gbias = tb("gbias_s", [128, 256], F32)
        stage = [tb(f"wst{i}", [128, 2048], F32) for i in range(2)]
        Rstage = [Res(), Res()]
        k.dma(SP, k.sem("ld1"), [
            (gA[:], gA_d[:, :], [], [R_const]), (gQ[:], gQ_d[:, :], [], [R_const]),
            (gKV[:], gKV_d[:, :], [], [R_const]), (gbias[:], gbias_d[:, :], [], [R_const]),
        ])
        load_weight(W0, RW0, w_in, 8, 1952, lambda c: gA[:, c:c + 1], stage, Rstage, wsem)
        load_weight(Wuq, RWuq, w_uq, 2, 768, lambda c: gQ[:, c:c + 1], stage, Rstage, wsem)
        load_weight(Wukv, RWukv, w_ukv, 1, 1024, lambda c: gKV[:, 0:1], stage, Rstage, wsem)
        if STOP == "W":
            k.barrier()
            return nc

        xs = [tb(f"xs{i}", [128, 1024], F32) for i in range(3)]; Rxs = [Res() for _ in range(3)]
        xsem = [k.sem(f"x{i}") for i in range(3)]
        cqsem = [k.sem(f"cq{i}") for i in range(3)]
        csk = [tb(f"csk{i}", [128, 32], F32) for i in range(3)]; Rcsk = [Res() for _ in range(3)]
        csq = [tb(f"csq{i}", [128, 256], F32) for i in range(3)]; Rcsq = [Res() for _ in range(3)]
        junk = tb("junk", [128, 1024], BF16); Rjunk = Res()
        xn2 = [tb(f"xn{i}", [128, 1024], BF16) for i in range(2)]; Rxn2 = [Res(), Res()]
        xnT2 = [tb(f"xnT{i}", [128, 8, 128], BF16) for i in range(2)]; RxnT2 = [Res(), Res()]
        kA2 = [tb(f"kA{i}", [128, 512], BF16) for i in range(2)]; RkA2 = [Res(), Res()]
        kB2 = [tb(f"kB{i}", [128, 8, 64], BF16) for i in range(2)]; RkB2 = [Res(), Res()]
        qA2 = [tb(f"qA{i}", [128, 512], BF16) for i in range(2)]; RqA2 = [Res(), Res()]
        qB2 = [tb(f"qB{i}", [128, 8, 96], BF16) for i in range(2)]; RqB2 = [Res(), Res()]
        Mfull2 = [tb(f"Mfull{i}", [128, 8, 96], BF16) for i in range(2)]; RMf2 = [Res(), Res()]
        ckvn2 = [tb(f"ckvn{i}", [128, 128], BF16) for i in range(2)]; Rckvn2 = [Res(), Res()]
        ckvnT2 = [tb(f"ckvnT{i}", [128, 128], BF16) for i in range(2)]; RckvnT2 = [Res(), Res()]
        cqn2 = [tb(f"cqn{i}", [128, 256], BF16) for i in range(2)]; Rcqn2 = [Res(), Res()]
        cqnT2 = [tb(f"cqnT{i}", [128, 2, 128], BF16) for i in range(2)]; RcqnT2 = [Res(), Res()]
        krr2 = [tb(f"krr{i}", [128, 32], BF16) for i in range(2)]; Rkrr2 = [Res(), Res()]
        rt2 = [tb(f"rt{i}", [128, 4, 64], F32) for i in range(2)]; Rrt2 = [Res(), Res()]
        qf2 = [tb(f"qf{i}", [128, 384], F32) for i in range(2)]; Rqf2 = [Res(), Res()]
        ksum = tb("ksum", [64, 8, 32], F32); Rksum = Res()
        kpart2 = [tb(f"kpart{i}", [64, 16], F32) for i in range(2)]; Rkpart2 = [Res(), Res()]; Rksum2 = [Res(), Res()]
        kmT = tb("kmT", [64, 8, 32], BF16); RkmT = Res()
        gateS2 = [tb(f"gateS{i}", [128, 8, 32], F32) for i in range(2)]; RgS2 = [Res(), Res()]
        top82 = [tb(f"top8{i}", [128, 64], F32) for i in range(2)]; Rtop2 = [Res(), Res()]
        onesb = tb("onesb", [128, 8, 64], BF16)
        kTbA = [tb(f"kTbA{i}", [64, 8, 512], BF16) for i in range(2)]
        kTbB = [tb(f"kTbB{i}", [64, 8, 512], BF16) for i in range(2)]
        krTb = [tb(f"krTb{i}", [32, 512], BF16) for i in range(2)]
        VBa = [tb(f"VBa{i}", [128, 8, 4, 128], BF16) for i in range(2)]
        VBb = [tb(f"VBb{i}", [128, 8, 4, 128], BF16) for i in range(2)]
        qTbA = [tb(f"qTbA{i}", [96, 8, 512], BF16) for i in range(2)]
        qTbB = [tb(f"qTbB{i}", [96, 8, 512], BF16) for i in range(2)]
        RF = [[Res() for _ in range(4)] for _ in range(2)]
        stsem = [k.sem("st0"), k.sem("st1")]

        k.op(POOL, [], [R_const], lambda: nc.gpsimd.memset(onesb[:], 1.0))
        for i in range(2):
            k.op(POOL, [], [RMf2[i]], lambda: nc.gpsimd.memset(Mfull2[i][:], 0.0))
        k.op(POOL, [], [RkmT], lambda: nc.gpsimd.memset(kmT[:], 0.0))

        def issue_x(t):
            i = t % 3
            items = [(xs[i][:], xc[t * 128:(t + 1) * 128, :], [], [Rxs[i]]),
                     (csk[i][:], csk_d[t], [], [Rcsk[i]])]
            k.dma(SP, xsem[i], items)
            if t >= 31:
                k.dma(SP, cqsem[i], [(csq[i][:], csq_d[t - 31], [], [Rcsq[i]])])

        issue_x(0)

        def do_tile(t):
            i = t % 2
            x3 = t % 3
            quad, j = t // 4, t % 4
            qp = quad % 2
            isq = t >= 31
            nblk = t // 2
            RFj = RF[qp][j]
            xn = xn2[i]; Rxn = Rxn2[i]
            xnT = xnT2[i]; RxnT = RxnT2[i]
            kA = kA2[i]; RkA = RkA2[i]
            kB = kB2[i]; RkB = RkB2[i]
            qA = qA2[i]; RqA = RqA2[i]
            qB = qB2[i]; RqB = RqB2[i]
            Mfull = Mfull2[i]; RMf = RMf2[i]
            ckvn = ckvn2[i]; Rckvn = Rckvn2[i]
            ckvnT = ckvnT2[i]; RckvnT = RckvnT2[i]
            cqn = cqn2[i]; Rcqn = Rcqn2[i]
            cqnT = cqnT2[i]; RcqnT = RcqnT2[i]
            krr = krr2[i]; Rkrr = Rkrr2[i]
            rt = rt2[i]; Rrt = Rrt2[i]
            qf = qf2[i]; Rqf = Rqf2[i]
            gateS = gateS2[i]; RgS = RgS2[i]
            top8 = top82[i]; Rtop = Rtop2[i]
            kpart = kpart2[nblk % 2]; Rkpart = Rkpart2[nblk % 2]; Rksum = Rksum2[nblk % 2]
            cnt = [0]

            def ps_next():
                bnk = 4 * i + cnt[0] % 4
                cnt[0] += 1
                return psT[bnk], psR[bnk]

            def transposes(src_list, reads, a_, b_, rows, c_):
                return transposes_g(src_list, reads, a_, b_, rows, c_, alloc=ps_next)
            if t + 1 < NT:
                issue_x(t + 1)
            rs, Rr = rstd_of(xs[x3][:], 1024, [Rxs[x3]], junk[:], Rjunk)
            k.op(DVE, [Rxs[x3], Rr], [Rxn], lambda: nc.vector.tensor_scalar(out=xn[:], in0=xs[x3][:], scalar1=rs, scalar2=None, op0=ALU.mult))
            yield
            chk(1)
            pv, Rp = transposes([xn[:, c * 128:(c + 1) * 128] for c in range(8)], [Rxn], None, None, 128, None)
            k.op(ACT, [Rp], [RxnT], lambda: nc.scalar.copy(out=xnT[:].rearrange("p c t -> p (c t)"), in_=pv[:, :]))
            yield
            chk(2)

            def proj(c0, c1):
                p, R = ps_next()
                for c in range(8):
                    k.op(PE, [RxnT, RW0], [R], lambda c=c: nc.tensor.matmul(p[:, 0:c1 - c0], lhsT=xnT[:, c, :], rhs=W0[:, c, c0:c1], start=(c == 0), stop=(c == 7)))
                return p, R
            p_k, R_k = proj(512, 1024)
            p_v, R_v = proj(1024, 1536)
            p_c, R_c = proj(1792, 1952)

            chk(3)
            k.op(ACT, [R_k], [RkA], lambda: nc.scalar.copy(out=kA[:], in_=p_k[:, :]))
            yield
            pv, Rp = transposes([kA[:, h * 64:(h + 1) * 64] for h in range(8)], [RkA], None, None, 64, None)
            pv3 = pv[0:64, :].rearrange("p (h t) -> p h t", h=8)
            chk(31)
            k.op(ACT, [Rp], [RFj], lambda: nc.scalar.copy(out=kTbA[qp][:, :, j * 128:(j + 1) * 128], in_=pv3))
            yield
            chk(32)
            ksrc = kTbA[qp][:, :, j * 128:(j + 1) * 128]
            if t % 2 == 0:
                k.op(DVE, [RFj], [Rksum], lambda: nc.vector.reduce_sum(out=kpart[:, 0:8], in_=ksrc, axis=AX.X))
                yield
            else:
                k.op(DVE, [RFj], [Rkpart], lambda: nc.vector.reduce_sum(out=kpart[:, 8:16], in_=ksrc, axis=AX.X))
                yield
                k.op(DVE, [Rkpart, Rksum], [RkmT], lambda: nc.vector.tensor_tensor(out=kmT[:, :, nblk], in0=kpart[:, 0:8], in1=kpart[:, 8:16], op=ALU.add))
                yield
            chk(33)
            chk(4)
            k.op(ACT, [R_v], [RFj], lambda: nc.scalar.copy(out=VBa[qp][:, :, j, 0:64], in_=p_v[:, :].rearrange("p (h d) -> p h d", h=8)))
            yield
            k.op(DVE, [R_const], [RFj], lambda: nc.vector.tensor_scalar(out=VBa[qp][:, :, j, 64:128], in0=onesb[:], scalar1=kval[:, t:t + 1], scalar2=None, op0=ALU.mult))
            yield
            k.op(ACT, [R_const], [RFj], lambda: nc.scalar.activation(out=VBb[qp][:, :, j, 64:128], in_=onesb[:], func=AF.Copy, scale=kval[:, t:t + 1]))
            yield

            chk(5)
            rs2, Rr2 = rstd_of(p_c[:, 0:128], 128, [R_c], junk[:, 0:128], Rjunk)
            k.op(DVE, [R_c, Rr2], [Rckvn], lambda: nc.vector.tensor_scalar(out=ckvn[:], in0=p_c[:, 0:128], scalar1=rs2, scalar2=None, op0=ALU.mult))
            yield
            x1 = p_c[:, 128:144]; x2 = p_c[:, 144:160]
            co = csk[x3][:, 0:16]; si = csk[x3][:, 16:32]
            k.op(DVE, [R_c, Rcsk[x3]], [Rrt], lambda: nc.vector.tensor_tensor(out=rt[:, 0, 0:16], in0=x1, in1=co, op=ALU.mult))
            yield
            k.op(DVE, [R_c, Rcsk[x3]], [Rrt], lambda: nc.vector.tensor_tensor(out=rt[:, 1, 0:16], in0=x2, in1=si, op=ALU.mult))
            yield
            k.op(DVE, [R_c, Rcsk[x3]], [Rrt], lambda: nc.vector.tensor_tensor(out=rt[:, 2, 0:16], in0=x2, in1=co, op=ALU.mult))
            yield
            k.op(DVE, [R_c, Rcsk[x3]], [Rrt], lambda: nc.vector.tensor_tensor(out=rt[:, 3, 0:16], in0=x1, in1=si, op=ALU.mult))
            yield
            k.op(DVE, [Rrt], [Rkrr], lambda: nc.vector.tensor_tensor(out=krr[:, 0:16], in0=rt[:, 0, 0:16], in1=rt[:, 1, 0:16], op=ALU.subtract))
            yield
            k.op(DVE, [Rrt], [Rkrr], lambda: nc.vector.tensor_tensor(out=krr[:, 16:32], in0=rt[:, 2, 0:16], in1=rt[:, 3, 0:16], op=ALU.add))
            yield
            chk(6)
            pv, Rp = transposes([ckvn[:]], [Rckvn], None, None, 128, None)
            k.op(ACT, [Rp], [RckvnT], lambda: nc.scalar.copy(out=ckvnT[:], in_=pv[:, 0:128]))
            yield
            pv, Rp = transposes([krr[:]], [Rkrr], None, None, 32, None)
            k.op(ACT, [Rp], [RFj], lambda: nc.scalar.copy(out=krTb[qp][:, j * 128:(j + 1) * 128], in_=pv[0:32, 0:128]))
            yield
            chk(7)
            for hh in range(2):
                p, R = ps_next()
                k.op(PE, [RckvnT, RWukv], [R], lambda: nc.tensor.matmul(p[:, :], lhsT=ckvnT[:], rhs=Wukv[:, 0, hh * 512:(hh + 1) * 512], start=True, stop=True))
                yield
                p4 = p[:, :].rearrange("p (h two d) -> p h two d", h=4, two=2)
                k.op(ACT, [R], [RkB], lambda: nc.scalar.copy(out=kB[:, hh * 4:hh * 4 + 4, :], in_=p4[:, :, 0, :]))
                yield
                k.op(DVE, [R], [RFj], lambda: nc.vector.tensor_copy(out=VBb[qp][:, hh * 4:hh * 4 + 4, j, 0:64], in_=p4[:, :, 1, :]))
                yield
            pv, Rp = transposes([kB[:, h, :] for h in range(8)], [RkB], None, None, 64, None)
            pv3 = pv[0:64, :].rearrange("p (h t) -> p h t", h=8)
            k.op(ACT, [Rp], [RFj], lambda: nc.scalar.copy(out=kTbB[qp][:, :, j * 128:(j + 1) * 128], in_=pv3))
            yield

            chk(8)
            if isq:
                p_q, R_q = proj(0, 512)
                k.op(ACT, [R_q], [RqA], lambda: nc.scalar.activation(out=qA[:], in_=p_q[:, :], func=AF.Copy, scale=0.125))
                yield
                pv, Rp = transposes([qA[:, h * 64:(h + 1) * 64] for h in range(8)], [RqA], None, None, 64, None)
                pv3 = pv[0:64, :].rearrange("p (h t) -> p h t", h=8)
                k.op(ACT, [Rp], [RFj], lambda: nc.scalar.copy(out=qTbA[qp][0:64, :, j * 128:(j + 1) * 128], in_=pv3))
                yield
                chk(41)
                pg, Rg = ps_next()
                for h in range(8):
                    k.op(PE, [RFj, RkmT], [Rg], lambda h=h: nc.tensor.matmul(pg[:, h * 32:(h + 1) * 32], lhsT=qTbA[qp][0:64, h, j * 128:(j + 1) * 128], rhs=kmT[:, h, :], start=True, stop=True))
                chk(42)
                k.op(DVE, [Rg, R_const], [RgS], lambda: nc.vector.tensor_tensor(out=gateS[:].rearrange("p h n -> p (h n)"), in0=pg[:, 0:256], in1=gbias[:], op=ALU.add))
                yield
                if nblk < 32:
                    k.op(DVE, [], [RgS], lambda: nc.vector.memset(gateS[:, :, nblk:32], NEG))
                chk(43)
                for h in range(8):
                    k.op(DVE, [RgS], [Rtop], lambda h=h: nc.vector.max(out=top8[:, h * 8:(h + 1) * 8], in_=gateS[:, h, :]))
                chk(44)
                for h in range(8):
                    k.op(DVE, [RgS, Rtop], [RMf], lambda h=h: nc.vector.tensor_scalar(out=Mfull[:, h, 64:96], in0=gateS[:, h, :], scalar1=top8[:, h * 8 + 2:h * 8 + 3], scalar2=NEG, op0=ALU.is_lt, op1=ALU.mult))
                k.op(DVE, [], [RMf], lambda: nc.vector.memset(Mfull[:, :, 64 + nblk:65 + nblk], 0.0))
                yield
                chk(45)
                pv, Rp = transposes([Mfull[:, h, :] for h in range(8)], [RMf], None, None, 96, None)
                pv3 = pv[64:96, :].rearrange("p (h t) -> p h t", h=8)
                k.op(ACT, [Rp], [RFj], lambda: nc.scalar.copy(out=qTbA[qp][64:96, :, j * 128:(j + 1) * 128], in_=pv3))
                yield
                chk(46)
                p_cq, R_cq = proj(1536, 1792)
                rs3, Rr3 = rstd_of(p_cq[:, 0:256], 256, [R_cq], junk[:, 0:256], Rjunk)
                k.op(DVE, [R_cq, Rr3], [Rcqn], lambda: nc.vector.tensor_scalar(out=cqn[:], in0=p_cq[:, 0:256], scalar1=rs3, scalar2=None, op0=ALU.mult))
                yield
                pv, Rp = transposes([cqn[:, c * 128:(c + 1) * 128] for c in range(2)], [Rcqn], None, None, 128, None)
                k.op(ACT, [Rp], [RcqnT], lambda: nc.scalar.copy(out=cqnT[:].rearrange("p c t -> p (c t)"), in_=pv[:, 0:256]))
                yield
                chk(47)
                for hh in range(2):
                    p, R = ps_next()
                    for c in range(2):
                        k.op(PE, [RcqnT, RWuq], [R], lambda c=c: nc.tensor.matmul(p[:, 0:384], lhsT=cqnT[:, c, :], rhs=Wuq[:, c, hh * 384:(hh + 1) * 384], start=(c == 0), stop=(c == 1)))
                    p3 = p[:, 0:384].rearrange("p (h d) -> p h d", h=4)
                    hs_ = slice(hh * 4, hh * 4 + 4)
                    k.op(ACT, [R], [RqB], lambda: nc.scalar.activation(out=qB[:, hs_, 0:64], in_=p3[:, :, 0:64], func=AF.Copy, scale=96.0 ** -0.5))
                    chk(48)
                    k.op(ACT, [R], [Rqf], lambda: nc.scalar.copy(out=qf[:], in_=p[:, 0:384]))
                    q3 = qf[:].rearrange("p (h d) -> p h d", h=4)
                    x1 = q3[:, :, 64:80]; x2 = q3[:, :, 80:96]
                    co = csq[x3][:, 0:128].rearrange("p (h f) -> p h f", h=8)[:, hs_, :]
                    si = csq[x3][:, 128:256].rearrange("p (h f) -> p h f", h=8)[:, hs_, :]
                    k.op(DVE, [Rqf, Rcsq[x3]], [Rrt], lambda: nc.vector.tensor_tensor(out=rt[:, :, 0:16], in0=x1, in1=co, op=ALU.mult))
                    k.op(DVE, [Rqf, Rcsq[x3]], [Rrt], lambda: nc.vector.tensor_tensor(out=rt[:, :, 16:32], in0=x2, in1=si, op=ALU.mult))
                    k.op(DVE, [Rqf, Rcsq[x3]], [Rrt], lambda: nc.vector.tensor_tensor(out=rt[:, :, 32:48], in0=x2, in1=co, op=ALU.mult))
                    k.op(DVE, [Rqf, Rcsq[x3]], [Rrt], lambda: nc.vector.tensor_tensor(out=rt[:, :, 48:64], in0=x1, in1=si, op=ALU.mult))
                    k.op(DVE, [Rrt], [RqB], lambda: nc.vector.tensor_tensor(out=qB[:, hs_, 64:80], in0=rt[:, :, 0:16], in1=rt[:, :, 16:32], op=ALU.subtract))
                    k.op(DVE, [Rrt], [RqB], lambda: nc.vector.tensor_tensor(out=qB[:, hs_, 80:96], in0=rt[:, :, 32:48], in1=rt[:, :, 48:64], op=ALU.add))
                chk(49)
                pv, Rp = transposes([qB[:, h, :] for h in range(8)], [RqB], None, None, 96, None)
                pv3 = pv[0:96, :].rearrange("p (h t) -> p h t", h=8)
                k.op(ACT, [Rp], [RFj], lambda: nc.scalar.copy(out=qTbB[qp][:, :, j * 128:(j + 1) * 128], in_=pv3))
                yield
                chk(50)

            if j == 3:
                rd = RF[qp]
                ks = slice(quad * 512, quad * 512 + 512)
                items = [
                    (KTa[:, :, ks].rearrange("h d k -> d h k"), kTbA[qp][:], rd, []),
                    (KTb[:, :, ks].rearrange("h d k -> d h k"), kTbB[qp][:], rd, []),
                    (KRT[:, ks], krTb[qp][:], rd, []),
                    (Va[:, :, quad * 4:quad * 4 + 4, :].rearrange("h p t c -> p h t c"), VBa[qp][:], rd, []),
                    (Vb[:, :, quad * 4:quad * 4 + 4, :].rearrange("h p t c -> p h t c"), VBb[qp][:], rd, []),
                ]
                if quad == 7:
                    items.append((QTa[:, :, 0:128].rearrange("h d k -> d h k"), qTbA[qp][:, :, 384:512], rd, []))
                    items.append((QTb[:, :, 0:128].rearrange("h d k -> d h k"), qTbB[qp][:, :, 384:512], rd, []))
                elif quad >= 8:
                    qs = slice(128 + (quad - 8) * 512, 128 + (quad - 8) * 512 + 512)
                    items.append((QTa[:, :, qs].rearrange("h d k -> d h k"), qTbA[qp][:], rd, []))
                    items.append((QTb[:, :, qs].rearrange("h d k -> d h k"), qTbB[qp][:], rd, []))
                k.dma(POOL, stsem[qp], items)
            if STOP == "P0a" and t == 3:
                raise _Stop()
        try:
            gens = []
            t_next = 0
            OFFSET = int(os.environ.get("KOFF", "20"))
            while gens or t_next < NT:
                if t_next < NT and len(gens) < 2 and (not gens or gens[-1][1] >= OFFSET):
                    gens.append([do_tile(t_next), 0])
                    t_next += 1
                for ge in list(gens):
                    try:
                        next(ge[0])
                        ge[1] += 1
                    except StopIteration:
                        gens.remove(ge)
        except _Stop:
            pass
        k.barrier()
    if STOP in ("P0", "P0a", "P0b"):
        return nc

    with ExitStack() as es:
        def tb(name, shape, dt):
            return es.enter_context(nc.sbuf_tensor(name, list(shape), dt))
        Kt = [tb(f"Kt{i}", [96, 8192], BF16) for i in range(2)]
        Vt = [tb(f"Vt{i}", [128, 64, 128], BF16) for i in range(2)]
        Qt = [tb(f"Qt{i}", [96, NQ], BF16) for i in range(2)]
        NF = [tb(f"NF{i}", [128, 6, 512], BF16) for i in range(2)]
        NFh = [tb(f"NFh{i}", [128, 4, 128], BF16) for i in range(2)]
        Rh = [Res(), Res()]
        hsem = [k.sem("h0"), k.sem("h1")]
        CM = tb("CM", [128, 4, 512], BF16)
        b31 = tb("b31_s", [128, 8], F32)
        Pt = [tb(f"Pt{i}", [128, 1024], BF16) for i in range(4)]
        RPt = [Res() for _ in range(4)]
        rsb = [tb(f"rsb{i}", [64, 512], F32) for i in range(2)]
        yTb = [tb(f"yTb{i}", [64, 512], BF16) for i in range(2)]
        Ry = [Res(), Res()]
        ysem = [k.sem("y0"), k.sem("y1")]
        k.dma(SP, k.sem("ld2"), [(CM[:].rearrange("p a b -> p (a b)"), cm_d[:, :], [], [R_const]),
                                 (b31[:], b31_d[:, :], [], [R_const])])

        def load_head(hh):
            i = hh % 2
            h = hh % 8
            moba = hh < 8
            items = [
                (Kt[i][0:64, :], (KTa if moba else KTb)[h], [], [Rh[i]]),
                (Kt[i][64:96, :], oh_d[:, :] if moba else KRT[:, :], [], [Rh[i]]),
                (Vt[i][:], (Va if moba else Vb)[h], [], [Rh[i]]),
                (Qt[i][:], (QTa if moba else QTb)[h], [], [Rh[i]]),
            ]
            if moba:
                items.append((NF[i][:].rearrange("p a b -> p (a b)"), nf_d[h], [], [Rh[i]]))
                items.append((NFh[i][:].rearrange("p a b -> p (a b)"), nfh_d[h], [], [Rh[i]]))
            k.dma(SP, hsem[i], items)

        load_head(0)
        pcnt = [0]
        gcnt = [0]
        for hh in range(16):
            i = hh % 2
            h = hh % 8
            moba = hh < 8
            if hh + 1 < 16:
                load_head(hh + 1)
            for gi in range(-1, 8):
                if gi < 0:
                    W, qc0, nvis = 128, 0, 32
                else:
                    W, qc0, nvis = 512, 128 + 512 * gi, 32 + 4 * gi + 4
                tabs = {}
                if moba:
                    if gi < 0:
                        for a in range(4):
                            tabs[28 + a] = NFh[i][:, a, :]
                    else:
                        for a in range(6):
                            tabs[30 + 4 * gi + a] = NF[i][:, a, :]
                else:
                    if gi < 0:
                        tabs[31] = CM[:, 0, 0:128]
                    else:
                        for a in range(4):
                            tabs[32 + 4 * gi + a] = CM[:, a, :]
                po, Ro = ps_next(6, 8)
                pend = []

                def emit_pv(kt, pi, off, first, last):
                    k.op(PE, [Rh[i], RPt[pi]], [Ro], lambda: nc.tensor.matmul(po[:, 0:W], lhsT=Vt[i][:, kt, :], rhs=Pt[pi][:, off:off + W], start=first, stop=last))

                for kp in range(nvis // 2):
                    di = pcnt[0] % 3
                    pcnt[0] += 1
                    RD_ = psR[2 * di]
                    base = 2 * di * 512
                    offs = (0, W)
                    if W == 512:
                        preg = psBig[:, base:base + 1024]
                    else:
                        preg = psBig[:, base:base + 2 * W]
                    anytab = False
                    for hk in range(2):
                        kt = 2 * kp + hk
                        tab = tabs.get(kt)
                        anytab = anytab or (tab is not None)
                        dst = preg[:, offs[hk]:offs[hk] + W]
                        k.op(PE, [Rh[i]], [RD_], lambda: nc.tensor.matmul(dst, lhsT=Kt[i][:, kt * 128:(kt + 1) * 128], rhs=Qt[i][:, qc0:qc0 + W], start=True, stop=(tab is None)))
                        if tab is not None:
                            k.op(PE, [Rh[i], R_const], [RD_], lambda: nc.tensor.matmul(dst, lhsT=idb[:], rhs=tab, start=False, stop=True))
                    if moba and not anytab:
                        k.op(ACT, [RD_, R_const], [RPt[di]], lambda: nc.scalar.activation(out=Pt[di][:, 0:2 * W], in_=preg, func=AF.Exp, bias=b31[:, h:h + 1]))
                    else:
                        k.op(ACT, [RD_], [RPt[di]], lambda: nc.scalar.activation(out=Pt[di][:, 0:2 * W], in_=preg, func=AF.Exp))
                    pend.append((kp, di))
                    if len(pend) > 1:
                        kp0, d0 = pend.pop(0)
                        for hk in range(2):
                            kt0 = 2 * kp0 + hk
                            emit_pv(kt0, d0, offs[hk], kt0 == 0, False)
                while pend:
                    kp0, d0 = pend.pop(0)
                    for hk in range(2):
                        kt0 = 2 * kp0 + hk
                        emit_pv(kt0, d0, offs[hk], kt0 == 0, kt0 == nvis - 1)
                yi = gcnt[0] % 2
                gcnt[0] += 1
                k.op(DVE, [Ro], [Ry[yi]], lambda: nc.vector.tensor_scalar(out=rsb[yi][:, 0:W], in0=po[64:128, 0:W], scalar1=1e-30, scalar2=None, op0=ALU.max))
                k.op(DVE, [Ry[yi]], [Ry[yi]], lambda: nc.vector.reciprocal(out=rsb[yi][:, 0:W], in_=rsb[yi][:, 0:W]))
                k.op(DVE, [Ro, Ry[yi]], [Ry[yi]], lambda: nc.vector.tensor_tensor(out=yTb[yi][:, 0:W], in0=po[0:64, 0:W], in1=rsb[yi][:, 0:W], op=ALU.mult))
                k.dma(POOL, ysem[yi], [(YT[hh, :, qc0:qc0 + W], yTb[yi][:, 0:W], [Ry[yi]], [])])
        k.barrier()
    if STOP == "P1":
        return nc

    with ExitStack() as es:
        def tb(name, shape, dt):
            return es.enter_context(nc.sbuf_tensor(name, list(shape), dt))
        Wg = tb("Wg", [128, 8, 2048], BF16); RWg = Res()
        Wm = tb("Wm", [64, 8, 1024], BF16); RWm = Res()
        Wl = tb("Wl", [64, 8, 1024], BF16); RWl = Res()
        Wo = tb("Wo", [128, 8, 1024], BF16); RWo = Res()
        gA = tb("gA2", [128, 8], F32)
        bg = tb("bg_s", [128, 16], F32)
        k.dma(SP, k.sem("ld3"), [(gA[:], gA_d[:, :], [], [R_const]), (bg[:], bg_d[:, :], [], [R_const])])
        with ExitStack() as es2:
            stage = [es2.enter_context(nc.sbuf_tensor(f"wstb{i}", [128, 2048], F32)) for i in range(2)]
            Rstage = [Res(), Res()]
            load_weight(Wg, RWg, w_in, 8, 2048, lambda c: gA[:, c:c + 1], stage, Rstage, wsem, c0=1952)
            load_weight(Wm, RWm, w_bm, 8, 1024, None, stage, Rstage, wsem, rows=64)
            load_weight(Wl, RWl, w_bl, 8, 1024, None, stage, Rstage, wsem, rows=64)
            load_weight(Wo, RWo, w_out, 8, 1024, None, stage, Rstage, wsem)
            k.barrier()
        WG2 = 256
        xs = [tb(f"xsB{i}", [128, 1024], F32) for i in range(4)]; Rxs = [Res() for _ in range(4)]
        xsem = [k.sem(f"xb{i}") for i in range(4)]
        hsem2 = [k.sem(f"hs{i}") for i in range(4)]
        junk = tb("junkB", [128, 1024], BF16); Rjunk = Res()
        xn2 = [tb(f"xnB{i}", [128, 1024], BF16) for i in range(2)]; Rxn2 = [Res(), Res()]
        xnT2 = [tb(f"xnTB{i}", [128, 8, WG2], BF16) for i in range(2)]; RxnT2 = [Res(), Res()]
        ysb2 = [tb(f"ysb{i}", [64, 16, WG2], BF16) for i in range(2)]; Rysb2 = [Res(), Res()]
        ysem2 = [k.sem("ysb0"), k.sem("ysb1")]
        gT2 = [tb(f"gT{i}", [128, 16, WG2], F32) for i in range(2)]; RgT2 = [Res(), Res()]
        t1 = [tb(f"t1{i}", [128, WG2], F32) for i in range(4)]
        t2 = [tb(f"t2{i}", [128, WG2], F32) for i in range(4)]
        Rt = [Res() for _ in range(4)]
        mixT2 = [tb(f"mixT{i}", [128, 8, WG2], BF16) for i in range(2)]; Rmix2 = [Res(), Res()]
        xcnt = [0]

        def do_group2(idx, gi, W, qc0, r0):
            par = idx % 2
            xn, Rxn, xnT, RxnT = xn2[par], Rxn2[par], xnT2[par], RxnT2[par]
            ysb, Rysb, gT, RgT, mixT, Rmix = ysb2[par], Rysb2[par], gT2[par], RgT2[par], mixT2[par], Rmix2[par]
            ntt = W // 128
            sl = []
            for tt in range(ntt):
                sl.append(xcnt[0] % 4)
                xcnt[0] += 1
            k.dma(SP, ysem2[par], [(ysb[:, :, 0:W], YT[:, :, qc0:qc0 + W].rearrange("h d k -> d h k"), [], [Rysb])])
            for tt in range(ntt):
                a_ = sl[tt]
                k.dma(SP, xsem[a_], [(xs[a_][:], xc[r0 + tt * 128:r0 + (tt + 1) * 128, :], [], [Rxs[a_]])])
            yield
            for tt in range(ntt):
                a_ = sl[tt]
                rs, Rr = rstd_of(xs[a_][:], 1024, [Rxs[a_]], junk[:], Rjunk)
                k.op(DVE, [Rxs[a_], Rr], [Rxn], lambda: nc.vector.tensor_scalar(out=xn[:], in0=xs[a_][:], scalar1=rs, scalar2=None, op0=ALU.mult))
                pv, Rp = transposes([xn[:, c * 128:(c + 1) * 128] for c in range(8)], [Rxn], None, None, 128, None)
                k.op(ACT, [Rp], [RxnT], lambda: nc.scalar.copy(out=xnT[:, :, tt * 128:(tt + 1) * 128], in_=pv[:, :].rearrange("p (c t) -> p c t", c=8)))
                yield
            for ct in range(16):
                p, R = ps_next()
                for c in range(8):
                    k.op(PE, [RxnT, RWg], [R], lambda c=c: nc.tensor.matmul(p[:, 0:W], lhsT=Wg[:, c, ct * 128:(ct + 1) * 128], rhs=xnT[:, c, 0:W], start=(c == 0), stop=(c == 7)))
                k.op(ACT, [R, R_const], [RgT], lambda: nc.scalar.activation(out=gT[:, ct, 0:W], in_=p[:, 0:W], func=AF.Sigmoid, bias=bg[:, ct:ct + 1]))
                yield
            for ct in range(8):
                pa, Ra = ps_next()
                for h in range(8):
                    k.op(PE, [Rysb, RWm], [Ra], lambda h=h: nc.tensor.matmul(pa[:, 0:W], lhsT=Wm[:, h, ct * 128:(ct + 1) * 128], rhs=ysb[:, h, 0:W], start=(h == 0), stop=(h == 7)))
                pb, Rb = ps_next()
                for h in range(8):
                    k.op(PE, [Rysb, RWl], [Rb], lambda h=h: nc.tensor.matmul(pb[:, 0:W], lhsT=Wl[:, h, ct * 128:(ct + 1) * 128], rhs=ysb[:, 8 + h, 0:W], start=(h == 0), stop=(h == 7)))
                ti = ct % 4
                k.op(DVE, [Ra, RgT], [Rt[ti]], lambda: nc.vector.tensor_tensor(out=t1[ti][:, 0:W], in0=pa[:, 0:W], in1=gT[:, ct, 0:W], op=ALU.mult))
                k.op(DVE, [Rb, RgT], [Rt[ti]], lambda: nc.vector.tensor_tensor(out=t2[ti][:, 0:W], in0=pb[:, 0:W], in1=gT[:, 8 + ct, 0:W], op=ALU.mult))
                k.op(DVE, [Rt[ti]], [Rmix], lambda: nc.vector.tensor_tensor(out=mixT[:, ct, 0:W], in0=t1[ti][:, 0:W], in1=t2[ti][:, 0:W], op=ALU.add))
                yield
            for tt in range(ntt):
                a_ = sl[tt]
                for hf in range(2):
                    p, R = ps_next()
                    for c in range(8):
                        k.op(PE, [Rmix, RWo], [R], lambda c=c: nc.tensor.matmul(p[:, :], lhsT=mixT[:, c, tt * 128:(tt + 1) * 128], rhs=Wo[:, c, hf * 512:(hf + 1) * 512], start=(c == 0), stop=(c == 7)))
                    k.op(DVE, [R, Rxs[a_]], [Rxs[a_]], lambda: nc.vector.tensor_tensor(out=xs[a_][:, hf * 512:(hf + 1) * 512], in0=p[:, :], in1=xs[a_][:, hf * 512:(hf + 1) * 512], op=ALU.add))
                k.dma(SP, hsem2[a_], [(H1[qc0 + tt * 128:qc0 + (tt + 1) * 128, :], xs[a_][:], [Rxs[a_]], [])])
                yield

        groups2 = [(-1, 128, 0, 31 * 128)] + [(g, WG2, 128 + WG2 * g, 4096 + WG2 * g) for g in range(4096 // WG2)]
        run_pipelined([do_group2(n_, *g_) for n_, g_ in enumerate(groups2)], int(os.environ.get("KOFF2", "14")))
        k.barrier()
    if STOP == "P2":
        return nc

    with ExitStack() as es:
        def tb(name, shape, dt):
            return es.enter_context(nc.sbuf_tensor(name, list(shape), dt))
        Wu = tb("Wu", [128, 8, 5632], BF16); RWu = Res()
        Wd = tb("Wd", [128, 22, 1024], BF16); RWd = Res()
        gF = tb("gF_s", [128, 8], F32)
        cw = tb("cw_s", [128, 44, 3], F32)
        cb = tb("cb_s", [128, 44], F32)
        gO = tb("gO_s", [128, 1024], F32)
        HB = tb("HB", [128, 44, 2], F32); RHB = Res()
        k.dma(SP, k.sem("ld4"), [(gF[:], gF_d[:, :], [], [R_const]), (cw[:].rearrange("p a b -> p (a b)"), cw_d[:, :], [], [R_const]),
                                 (cb[:], cb_d[:, :], [], [R_const]), (gO[:], gO_d[:, :], [], [R_const])])
        with ExitStack() as es2:
            stage = [es2.enter_context(nc.sbuf_tensor(f"wstc{i}", [128, 2048], F32)) for i in range(2)]
            Rstage = [Res(), Res()]
            load_weight(Wu, RWu, w_up, 8, 5632, lambda c: gF[:, c:c + 1], stage, Rstage, wsem)
            load_weight(Wd, RWd, w_dn, 22, 1024, None, stage, Rstage, wsem)
            k.barrier()
        WG = 256
        hs = [tb(f"hs{i}", [128, 1024], F32) for i in range(4)]; Rhs = [Res() for _ in range(4)]
        lsem = [k.sem(f"l{i}") for i in range(4)]
        osem = [k.sem(f"o{i}") for i in range(4)]
        junk = tb("junkC", [128, 1024], BF16); Rjunk = Res()
        hn2 = [tb(f"hnC{i}", [128, 1024], BF16) for i in range(2)]; Rhn2 = [Res(), Res()]
        hnT2 = [tb(f"hnT{i}", [128, 8, WG], BF16) for i in range(2)]; RhnT2 = [Res(), Res()]
        actT2 = [tb(f"actT{i}", [128, 22, WG], BF16) for i in range(2)]; RactT2 = [Res(), Res()]
        upw = [tb(f"upw{i}", [128, 2 + WG], F32) for i in range(6)]; Rupw = [Res() for _ in range(6)]
        acc = [tb(f"acc{i}", [128, WG], F32) for i in range(6)]; Racc = [Res() for _ in range(6)]
        sg = [tb(f"sg{i}", [128, WG], F32) for i in range(3)]; Rsg = [Res() for _ in range(3)]
        uc = [0]
        hc = [0]
        groups = [(-1, 128, 0)] + [(g, WG, 128 + WG * g) for g in range(4096 // WG)]

        def do_group3(idx, gi, W, qc0):
            par = idx % 2
            hn, Rhn, hnT, RhnT, actT, RactT = hn2[par], Rhn2[par], hnT2[par], RhnT2[par], actT2[par], RactT2[par]
            ntt = W // 128
            hidx = []
            for tt in range(ntt):
                a = hc[0] % 4
                hc[0] += 1
                hidx.append(a)
                k.dma(SP, lsem[a], [(hs[a][:], H1[qc0 + tt * 128:qc0 + (tt + 1) * 128, :], [], [Rhs[a]])])
            yield
            for tt in range(ntt):
                a = hidx[tt]
                rs, Rr = rstd_of(hs[a][:], 1024, [Rhs[a]], junk[:], Rjunk)
                k.op(DVE, [Rhs[a], Rr], [Rhn], lambda: nc.vector.tensor_scalar(out=hn[:], in0=hs[a][:], scalar1=rs, scalar2=None, op0=ALU.mult))
                pv, Rp = transposes([hn[:, c * 128:(c + 1) * 128] for c in range(8)], [Rhn], None, None, 128, None)
                k.op(ACT, [Rp], [RhnT], lambda: nc.scalar.copy(out=hnT[:, :, tt * 128:(tt + 1) * 128], in_=pv[:, :].rearrange("p (c t) -> p c t", c=8)))
                yield
            for c in range(22):
                accs = []
                for part, ch in ((0, c), (1, 22 + c)):
                    p, R = ps_next()
                    col = ch * 128
                    for d in range(8):
                        k.op(PE, [RhnT, RWu], [R], lambda d=d: nc.tensor.matmul(p[:, 0:W], lhsT=Wu[:, d, col:col + 128], rhs=hnT[:, d, 0:W], start=(d == 0), stop=(d == 7)))
                    if gi < 0:
                        k.op(ACT, [R, R_const], [RHB], lambda: nc.scalar.activation(out=HB[:, ch, :], in_=p[:, W - 2:W], func=AF.Copy, scale=hflag[:, 0:1]))
                        continue
                    u = uc[0] % 6
                    uc[0] += 1
                    k.op(ACT, [R], [Rupw[u]], lambda: nc.scalar.copy(out=upw[u][:, 2:2 + W], in_=p[:, 0:W]))
                    k.op(DVE, [RHB], [Rupw[u]], lambda: nc.vector.tensor_copy(out=upw[u][:, 0:2], in_=HB[:, ch, :]))
                    k.op(DVE, [Rupw[u], R_const], [Racc[u]], lambda: nc.vector.tensor_scalar(out=acc[u][:, 0:W], in0=upw[u][:, 2:2 + W], scalar1=cw[:, ch, 2:3], scalar2=cb[:, ch:ch + 1], op0=ALU.mult, op1=ALU.add))
                    k.op(DVE, [Rupw[u], R_const, Racc[u]], [Racc[u]], lambda: nc.vector.scalar_tensor_tensor(out=acc[u][:, 0:W], in0=upw[u][:, 1:1 + W], scalar=cw[:, ch, 1:2], in1=acc[u][:, 0:W], op0=ALU.mult, op1=ALU.add))
                    k.op(DVE, [Rupw[u], R_const, Racc[u]], [Racc[u]], lambda: nc.vector.scalar_tensor_tensor(out=acc[u][:, 0:W], in0=upw[u][:, 0:W], scalar=cw[:, ch, 0:1], in1=acc[u][:, 0:W], op0=ALU.mult, op1=ALU.add))
                    k.op(POOL, [Rupw[u]], [RHB], lambda: nc.gpsimd.tensor_copy(out=HB[:, ch, :], in_=upw[u][:, W:W + 2]))
                    accs.append(u)
                if gi >= 0:
                    ug, uv = accs
                    si = c % 3
                    k.op(ACT, [Racc[ug]], [Rsg[si]], lambda: nc.scalar.activation(out=sg[si][:, 0:W], in_=acc[ug][:, 0:W], func=AF.Silu))
                    k.op(POOL, [Rsg[si], Racc[uv]], [RactT], lambda: nc.gpsimd.tensor_tensor(out=actT[:, c, 0:W], in0=sg[si][:, 0:W], in1=acc[uv][:, 0:W], op=ALU.mult))
                yield
            if gi >= 0:
                for tt in range(ntt):
                    a = hidx[tt]
                    for hf in range(2):
                        p, R = ps_next()
                        for c in range(22):
                            k.op(PE, [RactT, RWd], [R], lambda c=c: nc.tensor.matmul(p[:, :], lhsT=actT[:, c, tt * 128:(tt + 1) * 128], rhs=Wd[:, c, hf * 512:(hf + 1) * 512], start=(c == 0), stop=(c == 21)))
                        k.op(DVE, [R, Rhs[a]], [Rhs[a]], lambda: nc.vector.tensor_tensor(out=hs[a][:, hf * 512:(hf + 1) * 512], in0=p[:, :], in1=hs[a][:, hf * 512:(hf + 1) * 512], op=ALU.add))
                    rs, Rr = rstd_of(hs[a][:], 1024, [Rhs[a]], junk[:], Rjunk)
                    k.op(DVE, [Rhs[a], Rr, R_const], [Rhs[a]], lambda: nc.vector.scalar_tensor_tensor(out=hs[a][:], in0=hs[a][:], scalar=rs, in1=gO[:], op0=ALU.mult, op1=ALU.mult))
                    row = qc0 - 128 + tt * 128
                    k.dma(SP, osem[a], [(out_d[row:row + 128, :], hs[a][:], [Rhs[a]], [])])
                    yield

        run_pipelined([do_group3(n_, *g_) for n_, g_ in enumerate(groups)], int(os.environ.get("KOFF3", "13")))
        k.barrier()
    return nc


def _t5_bucket(rel):
    n = np.maximum(rel, 0)
    nf = np.maximum(n, 1).astype(np.float32)
    large = 16 + (np.log(nf / np.float32(16)) / np.float32(math.log(128 / 16)) * np.float32(16)).astype(np.int32)
    large = np.minimum(large, 31)
    return np.where(n < 16, n, large)


def _tables():
    kk = np.arange(128)[:, None]
    qq = np.arange(256)[None, :]
    T0, T1 = [], []
    for i in range(2):
        kb = i * 128 + kk
        rel = qq - kb
        T0.append(np.where(rel >= 0, _t5_bucket(rel), 32))
        T1.append(_t5_bucket(qq + 256 - kb))
    far = np.full((128, 256), 31)
    msk = np.full((128, 256), 32)
    nf = [np.concatenate([T1[0], far], 1), np.concatenate([T1[1], far], 1),
          np.concatenate([T0[0], T1[0]], 1), np.concatenate([T0[1], T1[1]], 1),
          np.concatenate([msk, T0[0]], 1), np.concatenate([msk, T0[1]], 1)]
    nfh = [T1[0][:, 128:], T1[1][:, 128:], T0[0][:, 128:], T0[1][:, 128:]]
    return np.stack(nf, 1), np.stack(nfh, 1)


_NC_CACHE = {}


def kernel(x, norm_attn_g, w_in, b_gate, q_norm_g, w_uq, kv_norm_g, w_ukv, rel_bias,
           w_branch_moba, w_branch_mla, w_out, norm_ffn_g, w_up, conv_w, conv_b, w_down,
           norm_final_g):
    f32 = np.float32
    bf = ml_dtypes.bfloat16
    x = np.asarray(x, f32)
    c = lambda a: np.ascontiguousarray(np.asarray(a, f32))
    nfi, nfhi = _tables()
    rb_ext = np.concatenate([np.asarray(rel_bias, f32), np.full((1, 8), NEG, f32)], 0)
    nf = np.ascontiguousarray(np.transpose(rb_ext[nfi], (3, 0, 1, 2)).reshape(8, 128, 6 * 512)).astype(bf)
    nfh = np.ascontiguousarray(np.transpose(rb_ext[nfhi], (3, 0, 1, 2)).reshape(8, 128, 4 * 128)).astype(bf)
    b31 = np.ascontiguousarray(np.broadcast_to(np.asarray(rel_bias, f32)[31][None, :], (128, 8)))
    kk = np.arange(128)[:, None]
    cm = np.stack([np.where(i * 128 + kk <= np.arange(512)[None, :], 0.0, NEG) for i in range(4)], 1)
    cm = np.ascontiguousarray(cm.reshape(128, 2048).astype(f32)).astype(bf)
    oh = (np.arange(8192)[None, :] // 256 == np.arange(32)[:, None]).astype(f32).astype(bf)
    idb = np.eye(128, dtype=f32).astype(bf)
    inv_freq = (np.float32(10000.0) ** (-np.arange(0, 32, 2, dtype=f32) / np.float32(32))).astype(f32)
    common = {
        "w_in": c(w_in[0]), "w_uq": c(w_uq[0]), "w_ukv": c(w_ukv[0]), "w_bm": c(w_branch_moba[0]),
        "w_bl": c(w_branch_mla[0]), "w_out": c(w_out[0]), "w_up": c(w_up[0]), "w_dn": c(w_down[0]),
        "gA": c(np.asarray(norm_attn_g, f32)[0].reshape(8, 128).T),
        "gF": c(np.asarray(norm_ffn_g, f32)[0].reshape(8, 128).T),
        "gQ": c(np.asarray(q_norm_g, f32)[0].reshape(2, 128).T),
        "gKV": c(np.asarray(kv_norm_g, f32)[0].reshape(1, 128).T),
        "bg": c(np.asarray(b_gate, f32)[0].reshape(16, 128).T),
        "cw": c(np.transpose(np.asarray(conv_w, f32)[0].reshape(3, 44, 128), (2, 1, 0)).reshape(128, 132)),
        "cb": c(np.asarray(conv_b, f32)[0].reshape(44, 128).T),
        "gO": c(np.broadcast_to(np.asarray(norm_final_g, f32)[None, :], (128, 1024))),
        "b31": b31, "oh": oh, "nf": nf, "nfh": nfh, "cm": cm, "idb": idb,
    }
    in_maps = []
    for core in range(8):
        b, half = core // 2, core % 2
        xcat = np.zeros((8192, 1024), f32)
        if half == 1:
            xcat[:4096] = x[b, :4096]
        xcat[4096:] = x[b, half * 4096:(half + 1) * 4096]
        kval = np.ones((128, 64), f32)
        kval[:, :32] = float(half)
        gbias = np.zeros((128, 8, 32), f32)
        if half == 0:
            gbias[:, :, :16] = NEG
        pos = (np.arange(8192) if half == 1 else np.concatenate([np.arange(4096), np.arange(4096)])).astype(f32)
        ang = pos[:, None] * inv_freq[None, :]
        cs, sn = np.cos(ang).astype(f32), np.sin(ang).astype(f32)
        csk = np.concatenate([cs, sn], 1).reshape(64, 128, 32)
        s = np.float32(96.0 ** -0.5)
        csq = np.concatenate([np.tile(cs[31 * 128:], (1, 8)) * s, np.tile(sn[31 * 128:], (1, 8)) * s], 1).reshape(NQT, 128, 256)
        m = dict(common)
        m.update({"xc": xcat, "kval": kval, "gbias": c(gbias.reshape(128, 256)),
                  "hflag": np.full((128, 1), float(half), f32), "csk": c(csk), "csq": c(csq)})
        in_maps.append(m)
    if "nc" not in _NC_CACHE:
        _NC_CACHE["nc"] = build_program()
    nc = _NC_CACHE["nc"]
    ncores = int(os.environ.get("KCORES", "8"))
    res = run_bass_kernel_spmd(nc, in_maps[:ncores], core_ids=list(range(ncores)))
    out = np.empty((4, 8192, 1024), f32)
    for core in range(ncores):
        b, half = core // 2, core % 2
        out[b, half * 4096:(half + 1) * 4096] = res.results[core]["out"]
    if DEBUG:
        kernel.last = res
    return out
```

```python
import math
import os
from contextlib import ExitStack

import ml_dtypes
import numpy as np

import concourse.bass as bass
import concourse.mybir as mybir
from concourse.bass_utils import run_bass_kernel_spmd

F32 = mybir.dt.float32
BF16 = mybir.dt.bfloat16
AF = mybir.ActivationFunctionType
ALU = mybir.AluOpType
AX = mybir.AxisListType

NEG = -30000.0
EPS = 1e-6
NT = 64
NQT = 33
NQ = NQT * 128
DEBUG = False
STOP = None
STRICT = bool(int(os.environ.get('KSTRICT', '0')))


class Sem:
    def __init__(self, nc, name):
        self.h = nc.alloc_semaphore(name)
        self.v = 0


class Res:
    __slots__ = ("w", "r", "excl")

    def __init__(self, excl=False):
        self.w = None
        self.r = {}
        self.excl = excl


class Eng:
    def __init__(self, eng, sem):
        self.e = eng
        self.sem = sem
        self.seen = {}

    def wait(self, sem, val):
        if self.seen.get(id(sem), 0) >= val:
            return
        self.e.wait_ge(sem.h, val)
        self.seen[id(sem)] = val


class K:
    def __init__(self, nc):
        self.nc = nc
        self.sems = []
        self.pe = Eng(nc.tensor, self.sem("pe"))
        self.act = Eng(nc.scalar, self.sem("act"))
        self.dve = Eng(nc.vector, self.sem("dve"))
        self.pool = Eng(nc.gpsimd, self.sem("pool"))
        self.sp = Eng(nc.sync, self.sem("sp"))
        self.engs = [self.pe, self.act, self.dve, self.pool, self.sp]

    def sem(self, name):
        s = Sem(self.nc, name)
        self.sems.append(s)
        return s

    def _deps(self, eng, reads, writes):
        for r in reads:
            if r.w is not None:
                eng.wait(*r.w)
        for w in writes:
            if w.w is not None and (STRICT or w.w[0] is not eng.sem):
                eng.wait(*w.w)
            for s, v in w.r.values():
                if STRICT or s is not eng.sem:
                    eng.wait(s, v)

    def _commit(self, ev, reads, writes):
        for w in writes:
            w.w = ev
            w.r = {}
        for r in reads:
            if r not in writes:
                r.r[id(ev[0])] = ev

    def op(self, eng, reads, writes, fn):
        ex = [r for r in reads if r.excl and r not in writes]
        if ex:
            writes = list(writes) + ex
        self._deps(eng, reads, writes)
        ins = fn()
        eng.sem.v += 1
        ins.then_inc(eng.sem.h, 1)
        self._commit((eng.sem, eng.sem.v), reads, writes)

    def dma(self, q, sem, items):
        for o, i, reads, writes in items:
            self._deps(q, reads, writes)
        for o, i, reads, writes in items:
            q.e.dma_start(out=o, in_=i).then_inc(sem.h, 16)
            sem.v += 16
        ev = (sem, sem.v)
        for o, i, reads, writes in items:
            self._commit(ev, reads, writes)

    def barrier(self):
        for e in self.engs:
            for s in self.sems:
                if s.v > 0:
                    e.wait(s, s.v)


class _Stop(Exception):
    pass


def chk(n):
    if STOP in ("P0a", "P0b") and int(os.environ.get("KSTEP", "99")) == n:
        raise _Stop()


def run_pipelined(gen_list, offset):
    gens = []
    nxt = 0
    while gens or nxt < len(gen_list):
        if nxt < len(gen_list) and len(gens) < 2 and (not gens or gens[-1][1] >= offset):
            gens.append([gen_list[nxt], 0])
            nxt += 1
        for ge in list(gens):
            try:
                next(ge[0])
                ge[1] += 1
            except StopIteration:
                gens.remove(ge)


def build_program():
    nc = bass.Bass("TRN2", target_bir_lowering=False)
    k = K(nc)
    PE, ACT, DVE, POOL, SP = k.pe, k.act, k.dve, k.pool, k.sp

    def din(name, shape, dt=F32):
        return nc.dram_tensor(name, list(shape), dt, kind="ExternalInput").ap()

    def dscr(name, shape, dt):
        kind = "ExternalOutput" if DEBUG else "Internal"
        return nc.dram_tensor(name, list(shape), dt, kind=kind).ap()

    xc = din("xc", [8192, 1024])
    w_in = din("w_in", [1024, 4000])
    w_uq = din("w_uq", [256, 768])
    w_ukv = din("w_ukv", [128, 1024])
    w_bm = din("w_bm", [512, 1024])
    w_bl = din("w_bl", [512, 1024])
    w_out = din("w_out", [1024, 1024])
    w_up = din("w_up", [1024, 5632])
    w_dn = din("w_dn", [2816, 1024])
    gA_d = din("gA", [128, 8])
    gF_d = din("gF", [128, 8])
    gQ_d = din("gQ", [128, 2])
    gKV_d = din("gKV", [128, 1])
    bg_d = din("bg", [128, 16])
    cw_d = din("cw", [128, 44 * 3])
    cb_d = din("cb", [128, 44])
    gO_d = din("gO", [128, 1024])
    kval_d = din("kval", [128, 64])
    gbias_d = din("gbias", [128, 256])
    hflag_d = din("hflag", [128, 1])
    b31_d = din("b31", [128, 8])
    csk_d = din("csk", [64, 128, 32])
    csq_d = din("csq", [NQT, 128, 256])
    oh_d = din("oh", [32, 8192], BF16)
    nf_d = din("nf", [8, 128, 6 * 512], BF16)
    nfh_d = din("nfh", [8, 128, 4 * 128], BF16)
    cm_d = din("cm", [128, 4 * 512], BF16)
    idb_d = din("idb", [128, 128], BF16)
    out_d = nc.dram_tensor("out", [4096, 1024], F32, kind="ExternalOutput").ap()

    KTa = dscr("KTa", [8, 64, 8192], BF16)
    KTb = dscr("KTb", [8, 64, 8192], BF16)
    KRT = dscr("KRT", [32, 8192], BF16)
    Va = dscr("Va", [8, 128, 64, 128], BF16)
    Vb = dscr("Vb", [8, 128, 64, 128], BF16)
    QTa = dscr("QTa", [8, 96, NQ], BF16)
    QTb = dscr("QTb", [8, 96, NQ], BF16)
    YT = dscr("YT", [16, 64, NQ], BF16)
    H1 = dscr("H1", [NQ, 1024], F32)

    psBig = nc.alloc_psum_tensor("psbig", [128, 4096], F32)
    psT = [psBig[:, i * 512:(i + 1) * 512] for i in range(8)]
    psR = [Res(excl=True) for _ in range(8)]
    pctr = [0]

    def ps_next(lo=0, hi=8):
        i = lo + pctr[0] % (hi - lo)
        pctr[0] += 1
        return psT[i], psR[i]

    def psbf(p):
        return p[:, :].bitcast(BF16)

    def sb(name, shape, dt):
        return nc.alloc_sbuf_tensor(name, list(shape), dt)

    idb = sb("idb_s", [128, 128], BF16)
    epsb = sb("epsb", [128, 1], F32)
    stat = sb("stat", [128, 16], F32)
    kval = sb("kval_s", [128, 64], F32)
    hflag = sb("hflag_s", [128, 1], F32)
    R_const = Res()
    R_stat = [Res() for _ in range(4)]
    sc = [0]

    k.dma(SP, k.sem("ld0"), [
        (idb[:], idb_d[:, :], [], [R_const]),
        (kval[:], kval_d[:, :], [], [R_const]),
        (hflag[:], hflag_d[:, :], [], [R_const]),
    ])
    k.op(POOL, [], [R_const], lambda: nc.gpsimd.memset(epsb[:], EPS))
    if STOP == "W0":
        k.barrier()
        return nc

    def rstd_of(src_ap, n, reads, junk, Rjunk):
        i = sc[0] % 4
        sc[0] += 1
        R = R_stat[i]
        ss = stat[:, 4 * i:4 * i + 1]
        sd = stat[:, 4 * i + 1:4 * i + 2]
        rs = stat[:, 4 * i + 2:4 * i + 3]
        k.op(ACT, reads, [R, Rjunk], lambda: nc.scalar.activation(out=junk, in_=src_ap, func=AF.Square, accum_out=ss))
        k.op(ACT, [R, R_const], [R], lambda: nc.scalar.activation(out=sd, in_=ss, func=AF.Sqrt, scale=1.0 / n, bias=epsb[:, 0:1]))
        k.op(DVE, [R], [R], lambda: nc.vector.reciprocal(out=rs, in_=sd))
        return rs, R

    def transposes(src_list, reads, dst_ap_fn, dst_writes, rows, copy_eng, alloc=None):
        p, R = (alloc or ps_next)()
        pv = psbf(p)
        n = len(src_list)
        for j, s in enumerate(src_list):
            k.op(PE, reads + [R_const], [R], lambda s=s, j=j: nc.tensor.transpose(out=pv[0:rows, j * 128:(j + 1) * 128], in_=s, identity=idb[:]))
        return pv, R

    def load_weight(dst, Rdst, src, nchunks, cols, scale_ap_fn, stage, Rstage, ssem, c0=0, rows=128):
        for c in range(nchunks):
            for off in range(0, cols, 2048):
                w = min(2048, cols - off)
                i = load_weight.ctr % 2
                load_weight.ctr += 1
                k.dma(SP, ssem[i], [(stage[i][0:rows, 0:w], src[c * rows:(c + 1) * rows, c0 + off:c0 + off + w], [], [Rstage[i]])])
                sap = scale_ap_fn(c) if scale_ap_fn else None
                if sap is not None:
                    if load_weight.ctr % 2:
                        k.op(ACT, [Rstage[i], R_const], [Rdst], lambda i=i, c=c, off=off, w=w, sap=sap: nc.scalar.activation(out=dst[0:rows, c, off:off + w], in_=stage[i][0:rows, 0:w], func=AF.Copy, scale=sap))
                    else:
                        k.op(DVE, [Rstage[i], R_const], [Rdst], lambda i=i, c=c, off=off, w=w, sap=sap: nc.vector.tensor_scalar(out=dst[0:rows, c, off:off + w], in0=stage[i][0:rows, 0:w], scalar1=sap, scalar2=None, op0=ALU.mult))
                else:
                    if load_weight.ctr % 2:
                        k.op(ACT, [Rstage[i]], [Rdst], lambda i=i, c=c, off=off, w=w: nc.scalar.copy(out=dst[0:rows, c, off:off + w], in_=stage[i][0:rows, 0:w]))
                    else:
                        k.op(DVE, [Rstage[i]], [Rdst], lambda i=i, c=c, off=off, w=w: nc.vector.tensor_copy(out=dst[0:rows, c, off:off + w], in_=stage[i][0:rows, 0:w]))
    load_weight.ctr = 0
    transposes_g = transposes
    wsem = [k.sem("ws0"), k.sem("ws1")]

    with ExitStack() as es:
        def tb(name, shape, dt):
            return es.enter_context(nc.sbuf_tensor(name, list(shape), dt))

        W0 = tb("W0", [128, 8, 1952], BF16); RW0 = Res()
        Wuq = tb("Wuq", [128, 2, 768], BF16); RWuq = Res()
        Wukv = tb("Wukv", [128, 1, 1024], BF16); RWukv = Res()
        gA = tb("gA_s", [128, 8], F32)
        gQ = tb("gQ_s", [128, 2], F32)
        gKV = tb("gKV_s", [128, 1], F32)
        gbias = tb("gbias_s", [128, 256], F32)
        stage = [tb(f"wst{i}", [128, 2048], F32) for i in range(2)]
        Rstage = [Res(), Res()]
        k.dma(SP, k.sem("ld1"), [
            (gA[:], gA_d[:, :], [], [R_const]), (gQ[:], gQ_d[:, :], [], [R_const]),
            (gKV[:], gKV_d[:, :], [], [R_const]), (gbias[:], gbias_d[:, :], [], [R_const]),
        ])
        load_weight(W0, RW0, w_in, 8, 1952, lambda c: gA[:, c:c + 1], stage, Rstage, wsem)
        load_weight(Wuq, RWuq, w_uq, 2, 768, lambda c: gQ[:, c:c + 1], stage, Rstage, wsem)
        load_weight(Wukv, RWukv, w_ukv, 1, 1024, lambda c: gKV[:, 0:1], stage, Rstage, wsem)
        if STOP == "W":
            k.barrier()
            return nc

        xs = [tb(f"xs{i}", [128, 1024], F32) for i in range(3)]; Rxs = [Res() for _ in range(3)]
        xsem = [k.sem(f"x{i}") for i in range(3)]
        cqsem = [k.sem(f"cq{i}") for i in range(3)]
        csk = [tb(f"csk{i}", [128, 32], F32) for i in range(3)]; Rcsk = [Res() for _ in range(3)]
        csq = [tb(f"csq{i}", [128, 256], F32) for i in range(3)]; Rcsq = [Res() for _ in range(3)]
        junk = tb("junk", [128, 1024], BF16); Rjunk = Res()
        xn2 = [tb(f"xn{i}", [128, 1024], BF16) for i in range(2)]; Rxn2 = [Res(), Res()]
        xnT2 = [tb(f"xnT{i}", [128, 8, 128], BF16) for i in range(2)]; RxnT2 = [Res(), Res()]
        kA2 = [tb(f"kA{i}", [128, 512], BF16) for i in range(2)]; RkA2 = [Res(), Res()]
        kB2 = [tb(f"kB{i}", [128, 8, 64], BF16) for i in range(2)]; RkB2 = [Res(), Res()]
        qA2 = [tb(f"qA{i}", [128, 512], BF16) for i in range(2)]; RqA2 = [Res(), Res()]
        qB2 = [tb(f"qB{i}", [128, 8, 96], BF16) for i in range(2)]; RqB2 = [Res(), Res()]
        Mfull2 = [tb(f"Mfull{i}", [128, 8, 96], BF16) for i in range(2)]; RMf2 = [Res(), Res()]
        ckvn2 = [tb(f"ckvn{i}", [128, 128], BF16) for i in range(2)]; Rckvn2 = [Res(), Res()]
        ckvnT2 = [tb(f"ckvnT{i}", [128, 128], BF16) for i in range(2)]; RckvnT2 = [Res(), Res()]
        cqn2 = [tb(f"cqn{i}", [128, 256], BF16) for i in range(2)]; Rcqn2 = [Res(), Res()]
        cqnT2 = [tb(f"cqnT{i}", [128, 2, 128], BF16) for i in range(2)]; RcqnT2 = [Res(), Res()]
        krr2 = [tb(f"krr{i}", [128, 32], BF16) for i in range(2)]; Rkrr2 = [Res(), Res()]
        rt2 = [tb(f"rt{i}", [128, 4, 64], F32) for i in range(2)]; Rrt2 = [Res(), Res()]
        qf2 = [tb(f"qf{i}", [128, 384], F32) for i in range(2)]; Rqf2 = [Res(), Res()]
        ksum = tb("ksum", [64, 8, 32], F32); Rksum = Res()
        kpart2 = [tb(f"kpart{i}", [64, 16], F32) for i in range(2)]; Rkpart2 = [Res(), Res()]; Rksum2 = [Res(), Res()]
        kmT = tb("kmT", [64, 8, 32], BF16); RkmT = Res()
        gateS2 = [tb(f"gateS{i}", [128, 8, 32], F32) for i in range(2)]; RgS2 = [Res(), Res()]
        top82 = [tb(f"top8{i}", [128, 64], F32) for i in range(2)]; Rtop2 = [Res(), Res()]
        onesb = tb("onesb", [128, 8, 64], BF16)
        kTbA = [tb(f"kTbA{i}", [64, 8, 512], BF16) for i in range(2)]
        kTbB = [tb(f"kTbB{i}", [64, 8, 512], BF16) for i in range(2)]
        krTb = [tb(f"krTb{i}", [32, 512], BF16) for i in range(2)]
        VBa = [tb(f"VBa{i}", [128, 8, 4, 128], BF16) for i in range(2)]
        VBb = [tb(f"VBb{i}", [128, 8, 4, 128], BF16) for i in range(2)]
        qTbA = [tb(f"qTbA{i}", [96, 8, 512], BF16) for i in range(2)]
        qTbB = [tb(f"qTbB{i}", [96, 8, 512], BF16) for i in range(2)]
        RF = [[Res() for _ in range(4)] for _ in range(2)]
        stsem = [k.sem("st0"), k.sem("st1")]

        k.op(POOL, [], [R_const], lambda: nc.gpsimd.memset(onesb[:], 1.0))
        for i in range(2):
            k.op(POOL, [], [RMf2[i]], lambda: nc.gpsimd.memset(Mfull2[i][:], 0.0))
        k.op(POOL, [], [RkmT], lambda: nc.gpsimd.memset(kmT[:], 0.0))

        def issue_x(t):
            i = t % 3
            items = [(xs[i][:], xc[t * 128:(t + 1) * 128, :], [], [Rxs[i]]),
                     (csk[i][:], csk_d[t], [], [Rcsk[i]])]
            k.dma(SP, xsem[i], items)
            if t >= 31:
                k.dma(SP, cqsem[i], [(csq[i][:], csq_d[t - 31], [], [Rcsq[i]])])

        issue_x(0)

        def do_tile(t):
            i = t % 2
            x3 = t % 3
            quad, j = t // 4, t % 4
            qp = quad % 2
            isq = t >= 31
            nblk = t // 2
            RFj = RF[qp][j]
            xn = xn2[i]; Rxn = Rxn2[i]
            xnT = xnT2[i]; RxnT = RxnT2[i]
            kA = kA2[i]; RkA = RkA2[i]
            kB = kB2[i]; RkB = RkB2[i]
            qA = qA2[i]; RqA = RqA2[i]
            qB = qB2[i]; RqB = RqB2[i]
            Mfull = Mfull2[i]; RMf = RMf2[i]
            ckvn = ckvn2[i]; Rckvn = Rckvn2[i]
            ckvnT = ckvnT2[i]; RckvnT = RckvnT2[i]
            cqn = cqn2[i]; Rcqn = Rcqn2[i]
            cqnT = cqnT2[i]; RcqnT = RcqnT2[i]
            krr = krr2[i]; Rkrr = Rkrr2[i]
            rt = rt2[i]; Rrt = Rrt2[i]
            qf = qf2[i]; Rqf = Rqf2[i]
            gateS = gateS2[i]; RgS = RgS2[i]
            top8 = top82[i]; Rtop = Rtop2[i]
            kpart = kpart2[nblk % 2]; Rkpart = Rkpart2[nblk % 2]; Rksum = Rksum2[nblk % 2]
            cnt = [0]

            def ps_next():
                bnk = 4 * i + cnt[0] % 4
                cnt[0] += 1
                return psT[bnk], psR[bnk]

            def transposes(src_list, reads, a_, b_, rows, c_):
                return transposes_g(src_list, reads, a_, b_, rows, c_, alloc=ps_next)
            if t + 1 < NT:
                issue_x(t + 1)
            rs, Rr = rstd_of(xs[x3][:], 1024, [Rxs[x3]], junk[:], Rjunk)
            k.op(DVE, [Rxs[x3], Rr], [Rxn], lambda: nc.vector.tensor_scalar(out=xn[:], in0=xs[x3][:], scalar1=rs, scalar2=None, op0=ALU.mult))
            yield
            chk(1)
            pv, Rp = transposes([xn[:, c * 128:(c + 1) * 128] for c in range(8)], [Rxn], None, None, 128, None)
            k.op(ACT, [Rp], [RxnT], lambda: nc.scalar.copy(out=xnT[:].rearrange("p c t -> p (c t)"), in_=pv[:, :]))
            yield
            chk(2)

            def proj(c0, c1):
                p, R = ps_next()
                for c in range(8):
                    k.op(PE, [RxnT, RW0], [R], lambda c=c: nc.tensor.matmul(p[:, 0:c1 - c0], lhsT=xnT[:, c, :], rhs=W0[:, c, c0:c1], start=(c == 0), stop=(c == 7)))
                return p, R
            p_k, R_k = proj(512, 1024)
            p_v, R_v = proj(1024, 1536)
            p_c, R_c = proj(1792, 1952)

            chk(3)
            k.op(ACT, [R_k], [RkA], lambda: nc.scalar.copy(out=kA[:], in_=p_k[:, :]))
            yield
            pv, Rp = transposes([kA[:, h * 64:(h + 1) * 64] for h in range(8)], [RkA], None, None, 64, None)
            pv3 = pv[0:64, :].rearrange("p (h t) -> p h t", h=8)
            chk(31)
            k.op(ACT, [Rp], [RFj], lambda: nc.scalar.copy(out=kTbA[qp][:, :, j * 128:(j + 1) * 128], in_=pv3))
            yield
            chk(32)
            ksrc = kTbA[qp][:, :, j * 128:(j + 1) * 128]
            if t % 2 == 0:
                k.op(DVE, [RFj], [Rksum], lambda: nc.vector.reduce_sum(out=kpart[:, 0:8], in_=ksrc, axis=AX.X))
                yield
            else:
                k.op(DVE, [RFj], [Rkpart], lambda: nc.vector.reduce_sum(out=kpart[:, 8:16], in_=ksrc, axis=AX.X))
                yield
                k.op(DVE, [Rkpart, Rksum], [RkmT], lambda: nc.vector.tensor_tensor(out=kmT[:, :, nblk], in0=kpart[:, 0:8], in1=kpart[:, 8:16], op=ALU.add))
                yield
            chk(33)
            chk(4)
            k.op(ACT, [R_v], [RFj], lambda: nc.scalar.copy(out=VBa[qp][:, :, j, 0:64], in_=p_v[:, :].rearrange("p (h d) -> p h d", h=8)))
            yield
            k.op(DVE, [R_const], [RFj], lambda: nc.vector.tensor_scalar(out=VBa[qp][:, :, j, 64:128], in0=onesb[:], scalar1=kval[:, t:t + 1], scalar2=None, op0=ALU.mult))
            yield
            k.op(ACT, [R_const], [RFj], lambda: nc.scalar.activation(out=VBb[qp][:, :, j, 64:128], in_=onesb[:], func=AF.Copy, scale=kval[:, t:t + 1]))
            yield

            chk(5)
            rs2, Rr2 = rstd_of(p_c[:, 0:128], 128, [R_c], junk[:, 0:128], Rjunk)
            k.op(DVE, [R_c, Rr2], [Rckvn], lambda: nc.vector.tensor_scalar(out=ckvn[:], in0=p_c[:, 0:128], scalar1=rs2, scalar2=None, op0=ALU.mult))
            yield
            x1 = p_c[:, 128:144]; x2 = p_c[:, 144:160]
            co = csk[x3][:, 0:16]; si = csk[x3][:, 16:32]
            k.op(DVE, [R_c, Rcsk[x3]], [Rrt], lambda: nc.vector.tensor_tensor(out=rt[:, 0, 0:16], in0=x1, in1=co, op=ALU.mult))
            yield
            k.op(DVE, [R_c, Rcsk[x3]], [Rrt], lambda: nc.vector.tensor_tensor(out=rt[:, 1, 0:16], in0=x2, in1=si, op=ALU.mult))
            yield
            k.op(DVE, [R_c, Rcsk[x3]], [Rrt], lambda: nc.vector.tensor_tensor(out=rt[:, 2, 0:16], in0=x2, in1=co, op=ALU.mult))
            yield
            k.op(DVE, [R_c, Rcsk[x3]], [Rrt], lambda: nc.vector.tensor_tensor(out=rt[:, 3, 0:16], in0=x1, in1=si, op=ALU.mult))
            yield
            k.op(DVE, [Rrt], [Rkrr], lambda: nc.vector.tensor_tensor(out=krr[:, 0:16], in0=rt[:, 0, 0:16], in1=rt[:, 1, 0:16], op=ALU.subtract))
            yield
            k.op(DVE, [Rrt], [Rkrr], lambda: nc.vector.tensor_tensor(out=krr[:, 16:32], in0=rt[:, 2, 0:16], in1=rt[:, 3, 0:16], op=ALU.add))
            yield
            chk(6)
            pv, Rp = transposes([ckvn[:]], [Rckvn], None, None, 128, None)
            k.op(ACT, [Rp], [RckvnT], lambda: nc.scalar.copy(out=ckvnT[:], in_=pv[:, 0:128]))
            yield
            pv, Rp = transposes([krr[:]], [Rkrr], None, None, 32, None)
            k.op(ACT, [Rp], [RFj], lambda: nc.scalar.copy(out=krTb[qp][:, j * 128:(j + 1) * 128], in_=pv[0:32, 0:128]))
            yield
            chk(7)
            for hh in range(2):
                p, R = ps_next()
                k.op(PE, [RckvnT, RWukv], [R], lambda: nc.tensor.matmul(p[:, :], lhsT=ckvnT[:], rhs=Wukv[:, 0, hh * 512:(hh + 1) * 512], start=True, stop=True))
                yield
                p4 = p[:, :].rearrange("p (h two d) -> p h two d", h=4, two=2)
                k.op(ACT, [R], [RkB], lambda: nc.scalar.copy(out=kB[:, hh * 4:hh * 4 + 4, :], in_=p4[:, :, 0, :]))
                yield
                k.op(DVE, [R], [RFj], lambda: nc.vector.tensor_copy(out=VBb[qp][:, hh * 4:hh * 4 + 4, j, 0:64], in_=p4[:, :, 1, :]))
                yield
            pv, Rp = transposes([kB[:, h, :] for h in range(8)], [RkB], None, None, 64, None)
            pv3 = pv[0:64, :].rearrange("p (h t) -> p h t", h=8)
            k.op(ACT, [Rp], [RFj], lambda: nc.scalar.copy(out=kTbB[qp][:, :, j * 128:(j + 1) * 128], in_=pv3))
            yield

            chk(8)
            if isq:
                p_q, R_q = proj(0, 512)
                k.op(ACT, [R_q], [RqA], lambda: nc.scalar.activation(out=qA[:], in_=p_q[:, :], func=AF.Copy, scale=0.125))
                yield
                pv, Rp = transposes([qA[:, h * 64:(h + 1) * 64] for h in range(8)], [RqA], None, None, 64, None)
                pv3 = pv[0:64, :].rearrange("p (h t) -> p h t", h=8)
                k.op(ACT, [Rp], [RFj], lambda: nc.scalar.copy(out=qTbA[qp][0:64, :, j * 128:(j + 1) * 128], in_=pv3))
                yield
                chk(41)
                pg, Rg = ps_next()
                for h in range(8):
                    k.op(PE, [RFj, RkmT], [Rg], lambda h=h: nc.tensor.matmul(pg[:, h * 32:(h + 1) * 32], lhsT=qTbA[qp][0:64, h, j * 128:(j + 1) * 128], rhs=kmT[:, h, :], start=True, stop=True))
                chk(42)
                k.op(DVE, [Rg, R_const], [RgS], lambda: nc.vector.tensor_tensor(out=gateS[:].rearrange("p h n -> p (h n)"), in0=pg[:, 0:256], in1=gbias[:], op=ALU.add))
                yield
                if nblk < 32:
                    k.op(DVE, [], [RgS], lambda: nc.vector.memset(gateS[:, :, nblk:32], NEG))
                chk(43)
                for h in range(8):
                    k.op(DVE, [RgS], [Rtop], lambda h=h: nc.vector.max(out=top8[:, h * 8:(h + 1) * 8], in_=gateS[:, h, :]))
                chk(44)
                for h in range(8):
                    k.op(DVE, [RgS, Rtop], [RMf], lambda h=h: nc.vector.tensor_scalar(out=Mfull[:, h, 64:96], in0=gateS[:, h, :], scalar1=top8[:, h * 8 + 2:h * 8 + 3], scalar2=NEG, op0=ALU.is_lt, op1=ALU.mult))
                k.op(DVE, [], [RMf], lambda: nc.vector.memset(Mfull[:, :, 64 + nblk:65 + nblk], 0.0))
                yield
                chk(45)
                pv, Rp = transposes([Mfull[:, h, :] for h in range(8)], [RMf], None, None, 96, None)
                pv3 = pv[64:96, :].rearrange("p (h t) -> p h t", h=8)
                k.op(ACT, [Rp], [RFj], lambda: nc.scalar.copy(out=qTbA[qp][64:96, :, j * 128:(j + 1) * 128], in_=pv3))
                yield
                chk(46)
                p_cq, R_cq = proj(1536, 1792)
                rs3, Rr3 = rstd_of(p_cq[:, 0:256], 256, [R_cq], junk[:, 0:256], Rjunk)
                k.op(DVE, [R_cq, Rr3], [Rcqn], lambda: nc.vector.tensor_scalar(out=cqn[:], in0=p_cq[:, 0:256], scalar1=rs3, scalar2=None, op0=ALU.mult))
                yield
                pv, Rp = transposes([cqn[:, c * 128:(c + 1) * 128] for c in range(2)], [Rcqn], None, None, 128, None)
                k.op(ACT, [Rp], [RcqnT], lambda: nc.scalar.copy(out=cqnT[:].rearrange("p c t -> p (c t)"), in_=pv[:, 0:256]))
                yield
                chk(47)
                for hh in range(2):
                    p, R = ps_next()
                    for c in range(2):
                        k.op(PE, [RcqnT, RWuq], [R], lambda c=c: nc.tensor.matmul(p[:, 0:384], lhsT=cqnT[:, c, :], rhs=Wuq[:, c, hh * 384:(hh + 1) * 384], start=(c == 0), stop=(c == 1)))
                    p3 = p[:, 0:384].rearrange("p (h d) -> p h d", h=4)
                    hs_ = slice(hh * 4, hh * 4 + 4)
                    k.op(ACT, [R], [RqB], lambda: nc.scalar.activation(out=qB[:, hs_, 0:64], in_=p3[:, :, 0:64], func=AF.Copy, scale=96.0 ** -0.5))
                    chk(48)
                    k.op(ACT, [R], [Rqf], lambda: nc.scalar.copy(out=qf[:], in_=p[:, 0:384]))
                    q3 = qf[:].rearrange("p (h d) -> p h d", h=4)
                    x1 = q3[:, :, 64:80]; x2 = q3[:, :, 80:96]
                    co = csq[x3][:, 0:128].rearrange("p (h f) -> p h f", h=8)[:, hs_, :]
                    si = csq[x3][:, 128:256].rearrange("p (h f) -> p h f", h=8)[:, hs_, :]
                    k.op(DVE, [Rqf, Rcsq[x3]], [Rrt], lambda: nc.vector.tensor_tensor(out=rt[:, :, 0:16], in0=x1, in1=co, op=ALU.mult))
                    k.op(DVE, [Rqf, Rcsq[x3]], [Rrt], lambda: nc.vector.tensor_tensor(out=rt[:, :, 16:32], in0=x2, in1=si, op=ALU.mult))
                    k.op(DVE, [Rqf, Rcsq[x3]], [Rrt], lambda: nc.vector.tensor_tensor(out=rt[:, :, 32:48], in0=x2, in1=co, op=ALU.mult))
                    k.op(DVE, [Rqf, Rcsq[x3]], [Rrt], lambda: nc.vector.tensor_tensor(out=rt[:, :, 48:64], in0=x1, in1=si, op=ALU.mult))
                    k.op(DVE, [Rrt], [RqB], lambda: nc.vector.tensor_tensor(out=qB[:, hs_, 64:80], in0=rt[:, :, 0:16], in1=rt[:, :, 16:32], op=ALU.subtract))
                    k.op(DVE, [Rrt], [RqB], lambda: nc.vector.tensor_tensor(out=qB[:, hs_, 80:96], in0=rt[:, :, 32:48], in1=rt[:, :, 48:64], op=ALU.add))
                chk(49)
                pv, Rp = transposes([qB[:, h, :] for h in range(8)], [RqB], None, None, 96, None)
                pv3 = pv[0:96, :].rearrange("p (h t) -> p h t", h=8)
                k.op(ACT, [Rp], [RFj], lambda: nc.scalar.copy(out=qTbB[qp][:, :, j * 128:(j + 1) * 128], in_=pv3))
                yield
                chk(50)

            if j == 3:
                rd = RF[qp]
                ks = slice(quad * 512, quad * 512 + 512)
                items = [
                    (KTa[:, :, ks].rearrange("h d k -> d h k"), kTbA[qp][:], rd, []),
                    (KTb[:, :, ks].rearrange("h d k -> d h k"), kTbB[qp][:], rd, []),
                    (KRT[:, ks], krTb[qp][:], rd, []),
                    (Va[:, :, quad * 4:quad * 4 + 4, :].rearrange("h p t c -> p h t c"), VBa[qp][:], rd, []),
                    (Vb[:, :, quad * 4:quad * 4 + 4, :].rearrange("h p t c -> p h t c"), VBb[qp][:], rd, []),
                ]
                if quad == 7:
                    items.append((QTa[:, :, 0:128].rearrange("h d k -> d h k"), qTbA[qp][:, :, 384:512], rd, []))
                    items.append((QTb[:, :, 0:128].rearrange("h d k -> d h k"), qTbB[qp][:, :, 384:512], rd, []))
                elif quad >= 8:
                    qs = slice(128 + (quad - 8) * 512, 128 + (quad - 8) * 512 + 512)
                    items.append((QTa[:, :, qs].rearrange("h d k -> d h k"), qTbA[qp][:], rd, []))
                    items.append((QTb[:, :, qs].rearrange("h d k -> d h k"), qTbB[qp][:], rd, []))
                k.dma(POOL, stsem[qp], items)
            if STOP == "P0a" and t == 3:
                raise _Stop()
        try:
            gens = []
            t_next = 0
            OFFSET = int(os.environ.get("KOFF", "20"))
            while gens or t_next < NT:
                if t_next < NT and len(gens) < 2 and (not gens or gens[-1][1] >= OFFSET):
                    gens.append([do_tile(t_next), 0])
                    t_next += 1
                for ge in list(gens):
                    try:
                        next(ge[0])
                        ge[1] += 1
                    except StopIteration:
                        gens.remove(ge)
        except _Stop:
            pass
        k.barrier()
    if STOP in ("P0", "P0a", "P0b"):
        return nc

    with ExitStack() as es:
        def tb(name, shape, dt):
            return es.enter_context(nc.sbuf_tensor(name, list(shape), dt))
        Kt = [tb(f"Kt{i}", [96, 8192], BF16) for i in range(2)]
        Vt = [tb(f"Vt{i}", [128, 64, 128], BF16) for i in range(2)]
        Qt = [tb(f"Qt{i}", [96, NQ], BF16) for i in range(2)]
        NF = [tb(f"NF{i}", [128, 6, 512], BF16) for i in range(2)]
        NFh = [tb(f"NFh{i}", [128, 4, 128], BF16) for i in range(2)]
        Rh = [Res(), Res()]
        hsem = [k.sem("h0"), k.sem("h1")]
        CM = tb("CM", [128, 4, 512], BF16)
        b31 = tb("b31_s", [128, 8], F32)
        Pt = [tb(f"Pt{i}", [128, 1024], BF16) for i in range(4)]
        RPt = [Res() for _ in range(4)]
        rsb = [tb(f"rsb{i}", [64, 512], F32) for i in range(2)]
        yTb = [tb(f"yTb{i}", [64, 512], BF16) for i in range(2)]
        Ry = [Res(), Res()]
        ysem = [k.sem("y0"), k.sem("y1")]
        k.dma(SP, k.sem("ld2"), [(CM[:].rearrange("p a b -> p (a b)"), cm_d[:, :], [], [R_const]),
                                 (b31[:], b31_d[:, :], [], [R_const])])

        def load_head(hh):
            i = hh % 2
            h = hh % 8
            moba = hh < 8
            items = [
                (Kt[i][0:64, :], (KTa if moba else KTb)[h], [], [Rh[i]]),
                (Kt[i][64:96, :], oh_d[:, :] if moba else KRT[:, :], [], [Rh[i]]),
                (Vt[i][:], (Va if moba else Vb)[h], [], [Rh[i]]),
                (Qt[i][:], (QTa if moba else QTb)[h], [], [Rh[i]]),
            ]
            if moba:
                items.append((NF[i][:].rearrange("p a b -> p (a b)"), nf_d[h], [], [Rh[i]]))
                items.append((NFh[i][:].rearrange("p a b -> p (a b)"), nfh_d[h], [], [Rh[i]]))
            k.dma(SP, hsem[i], items)

        load_head(0)
        pcnt = [0]
        gcnt = [0]
        for hh in range(16):
            i = hh % 2
            h = hh % 8
            moba = hh < 8
            if hh + 1 < 16:
                load_head(hh + 1)
            for gi in range(-1, 8):
                if gi < 0:
                    W, qc0, nvis = 128, 0, 32
                else:
                    W, qc0, nvis = 512, 128 + 512 * gi, 32 + 4 * gi + 4
                tabs = {}
                if moba:
                    if gi < 0:
                        for a in range(4):
                            tabs[28 + a] = NFh[i][:, a, :]
                    else:
                        for a in range(6):
                            tabs[30 + 4 * gi + a] = NF[i][:, a, :]
                else:
                    if gi < 0:
                        tabs[31] = CM[:, 0, 0:128]
                    else:
                        for a in range(4):
                            tabs[32 + 4 * gi + a] = CM[:, a, :]
                po, Ro = ps_next(6, 8)
                pend = []

                def emit_pv(kt, pi, first, last):
                    k.op(PE, [Rh[i], RPt[pi]], [Ro], lambda: nc.tensor.matmul(po[:, 0:W], lhsT=Vt[i][:, kt, :], rhs=Pt[pi][:, 0:W], start=first, stop=last))

                for kt in range(nvis):
                    p, R = ps_next(0, 6)
                    tab = tabs.get(kt)
                    k.op(PE, [Rh[i]], [R], lambda: nc.tensor.matmul(p[:, 0:W], lhsT=Kt[i][:, kt * 128:(kt + 1) * 128], rhs=Qt[i][:, qc0:qc0 + W], start=True, stop=(tab is None)))
                    if tab is not None:
                        k.op(PE, [Rh[i], R_const], [R], lambda: nc.tensor.matmul(p[:, 0:W], lhsT=idb[:], rhs=tab, start=False, stop=True))
                    pi = pcnt[0] % 4
                    pcnt[0] += 1
                    if moba and tab is None:
                        k.op(ACT, [R, R_const], [RPt[pi]], lambda: nc.scalar.activation(out=Pt[pi][:, 0:W], in_=p[:, 0:W], func=AF.Exp, bias=b31[:, h:h + 1]))
                    else:
                        k.op(ACT, [R], [RPt[pi]], lambda: nc.scalar.activation(out=Pt[pi][:, 0:W], in_=p[:, 0:W], func=AF.Exp))
                    pend.append((kt, pi))
                    if len(pend) > 2:
                        kt0, pi0 = pend.pop(0)
                        emit_pv(kt0, pi0, kt0 == 0, False)
                while pend:
                    kt0, pi0 = pend.pop(0)
                    emit_pv(kt0, pi0, kt0 == 0, kt0 == nvis - 1)
                yi = gcnt[0] % 2
                gcnt[0] += 1
                k.op(DVE, [Ro], [Ry[yi]], lambda: nc.vector.tensor_scalar(out=rsb[yi][:, 0:W], in0=po[64:128, 0:W], scalar1=1e-30, scalar2=None, op0=ALU.max))
                k.op(DVE, [Ry[yi]], [Ry[yi]], lambda: nc.vector.reciprocal(out=rsb[yi][:, 0:W], in_=rsb[yi][:, 0:W]))
                k.op(DVE, [Ro, Ry[yi]], [Ry[yi]], lambda: nc.vector.tensor_tensor(out=yTb[yi][:, 0:W], in0=po[0:64, 0:W], in1=rsb[yi][:, 0:W], op=ALU.mult))
                k.dma(POOL, ysem[yi], [(YT[hh, :, qc0:qc0 + W], yTb[yi][:, 0:W], [Ry[yi]], [])])
        k.barrier()
    if STOP == "P1":
        return nc

    with ExitStack() as es:
        def tb(name, shape, dt):
            return es.enter_context(nc.sbuf_tensor(name, list(shape), dt))
        Wg = tb("Wg", [128, 8, 2048], BF16); RWg = Res()
        Wm = tb("Wm", [64, 8, 1024], BF16); RWm = Res()
        Wl = tb("Wl", [64, 8, 1024], BF16); RWl = Res()
        Wo = tb("Wo", [128, 8, 1024], BF16); RWo = Res()
        gA = tb("gA2", [128, 8], F32)
        bg = tb("bg_s", [128, 16], F32)
        k.dma(SP, k.sem("ld3"), [(gA[:], gA_d[:, :], [], [R_const]), (bg[:], bg_d[:, :], [], [R_const])])
        with ExitStack() as es2:
            stage = [es2.enter_context(nc.sbuf_tensor(f"wstb{i}", [128, 2048], F32)) for i in range(2)]
            Rstage = [Res(), Res()]
            load_weight(Wg, RWg, w_in, 8, 2048, lambda c: gA[:, c:c + 1], stage, Rstage, wsem, c0=1952)
            load_weight(Wm, RWm, w_bm, 8, 1024, None, stage, Rstage, wsem, rows=64)
            load_weight(Wl, RWl, w_bl, 8, 1024, None, stage, Rstage, wsem, rows=64)
            load_weight(Wo, RWo, w_out, 8, 1024, None, stage, Rstage, wsem)
            k.barrier()
        WG2 = 256
        xs = [tb(f"xsB{i}", [128, 1024], F32) for i in range(4)]; Rxs = [Res() for _ in range(4)]
        xsem = [k.sem(f"xb{i}") for i in range(4)]
        hsem2 = [k.sem(f"hs{i}") for i in range(4)]
        junk = tb("junkB", [128, 1024], BF16); Rjunk = Res()
        xn2 = [tb(f"xnB{i}", [128, 1024], BF16) for i in range(2)]; Rxn2 = [Res(), Res()]
        xnT2 = [tb(f"xnTB{i}", [128, 8, WG2], BF16) for i in range(2)]; RxnT2 = [Res(), Res()]
        ysb2 = [tb(f"ysb{i}", [64, 16, WG2], BF16) for i in range(2)]; Rysb2 = [Res(), Res()]
        ysem2 = [k.sem("ysb0"), k.sem("ysb1")]
        gT2 = [tb(f"gT{i}", [128, 16, WG2], F32) for i in range(2)]; RgT2 = [Res(), Res()]
        t1 = [tb(f"t1{i}", [128, WG2], F32) for i in range(4)]
        t2 = [tb(f"t2{i}", [128, WG2], F32) for i in range(4)]
        Rt = [Res() for _ in range(4)]
        mixT2 = [tb(f"mixT{i}", [128, 8, WG2], BF16) for i in range(2)]; Rmix2 = [Res(), Res()]
        xcnt = [0]

        def do_group2(idx, gi, W, qc0, r0):
            par = idx % 2
            xn, Rxn, xnT, RxnT = xn2[par], Rxn2[par], xnT2[par], RxnT2[par]
            ysb, Rysb, gT, RgT, mixT, Rmix = ysb2[par], Rysb2[par], gT2[par], RgT2[par], mixT2[par], Rmix2[par]
            ntt = W // 128
            sl = []
            for tt in range(ntt):
                sl.append(xcnt[0] % 4)
                xcnt[0] += 1
            k.dma(SP, ysem2[par], [(ysb[:, :, 0:W], YT[:, :, qc0:qc0 + W].rearrange("h d k -> d h k"), [], [Rysb])])
            for tt in range(ntt):
                a_ = sl[tt]
                k.dma(SP, xsem[a_], [(xs[a_][:], xc[r0 + tt * 128:r0 + (tt + 1) * 128, :], [], [Rxs[a_]])])
            yield
            for tt in range(ntt):
                a_ = sl[tt]
                rs, Rr = rstd_of(xs[a_][:], 1024, [Rxs[a_]], junk[:], Rjunk)
                k.op(DVE, [Rxs[a_], Rr], [Rxn], lambda: nc.vector.tensor_scalar(out=xn[:], in0=xs[a_][:], scalar1=rs, scalar2=None, op0=ALU.mult))
                pv, Rp = transposes([xn[:, c * 128:(c + 1) * 128] for c in range(8)], [Rxn], None, None, 128, None)
                k.op(ACT, [Rp], [RxnT], lambda: nc.scalar.copy(out=xnT[:, :, tt * 128:(tt + 1) * 128], in_=pv[:, :].rearrange("p (c t) -> p c t", c=8)))
                yield
            for ct in range(16):
                p, R = ps_next()
                for c in range(8):
                    k.op(PE, [RxnT, RWg], [R], lambda c=c: nc.tensor.matmul(p[:, 0:W], lhsT=Wg[:, c, ct * 128:(ct + 1) * 128], rhs=xnT[:, c, 0:W], start=(c == 0), stop=(c == 7)))
                k.op(ACT, [R, R_const], [RgT], lambda: nc.scalar.activation(out=gT[:, ct, 0:W], in_=p[:, 0:W], func=AF.Sigmoid, bias=bg[:, ct:ct + 1]))
                yield
            for ct in range(8):
                pa, Ra = ps_next()
                for h in range(8):
                    k.op(PE, [Rysb, RWm], [Ra], lambda h=h: nc.tensor.matmul(pa[:, 0:W], lhsT=Wm[:, h, ct * 128:(ct + 1) * 128], rhs=ysb[:, h, 0:W], start=(h == 0), stop=(h == 7)))
                pb, Rb = ps_next()
                for h in range(8):
                    k.op(PE, [Rysb, RWl], [Rb], lambda h=h: nc.tensor.matmul(pb[:, 0:W], lhsT=Wl[:, h, ct * 128:(ct + 1) * 128], rhs=ysb[:, 8 + h, 0:W], start=(h == 0), stop=(h == 7)))
                ti = ct % 4
                k.op(DVE, [Ra, RgT], [Rt[ti]], lambda: nc.vector.tensor_tensor(out=t1[ti][:, 0:W], in0=pa[:, 0:W], in1=gT[:, ct, 0:W], op=ALU.mult))
                k.op(DVE, [Rb, RgT], [Rt[ti]], lambda: nc.vector.tensor_tensor(out=t2[ti][:, 0:W], in0=pb[:, 0:W], in1=gT[:, 8 + ct, 0:W], op=ALU.mult))
                k.op(DVE, [Rt[ti]], [Rmix], lambda: nc.vector.tensor_tensor(out=mixT[:, ct, 0:W], in0=t1[ti][:, 0:W], in1=t2[ti][:, 0:W], op=ALU.add))
                yield
            for tt in range(ntt):
                a_ = sl[tt]
                for hf in range(2):
                    p, R = ps_next()
                    for c in range(8):
                        k.op(PE, [Rmix, RWo], [R], lambda c=c: nc.tensor.matmul(p[:, :], lhsT=mixT[:, c, tt * 128:(tt + 1) * 128], rhs=Wo[:, c, hf * 512:(hf + 1) * 512], start=(c == 0), stop=(c == 7)))
                    k.op(DVE, [R, Rxs[a_]], [Rxs[a_]], lambda: nc.vector.tensor_tensor(out=xs[a_][:, hf * 512:(hf + 1) * 512], in0=p[:, :], in1=xs[a_][:, hf * 512:(hf + 1) * 512], op=ALU.add))
                k.dma(SP, hsem2[a_], [(H1[qc0 + tt * 128:qc0 + (tt + 1) * 128, :], xs[a_][:], [Rxs[a_]], [])])
                yield

        groups2 = [(-1, 128, 0, 31 * 128)] + [(g, WG2, 128 + WG2 * g, 4096 + WG2 * g) for g in range(4096 // WG2)]
        run_pipelined([do_group2(n_, *g_) for n_, g_ in enumerate(groups2)], int(os.environ.get("KOFF2", "14")))
        k.barrier()
    if STOP == "P2":
        return nc

    with ExitStack() as es:
        def tb(name, shape, dt):
            return es.enter_context(nc.sbuf_tensor(name, list(shape), dt))
        Wu = tb("Wu", [128, 8, 5632], BF16); RWu = Res()
        Wd = tb("Wd", [128, 22, 1024], BF16); RWd = Res()
        gF = tb("gF_s", [128, 8], F32)
        cw = tb("cw_s", [128, 44, 3], F32)
        cb = tb("cb_s", [128, 44], F32)
        gO = tb("gO_s", [128, 1024], F32)
        HB = tb("HB", [128, 44, 2], F32); RHB = Res()
        k.dma(SP, k.sem("ld4"), [(gF[:], gF_d[:, :], [], [R_const]), (cw[:].rearrange("p a b -> p (a b)"), cw_d[:, :], [], [R_const]),
                                 (cb[:], cb_d[:, :], [], [R_const]), (gO[:], gO_d[:, :], [], [R_const])])
        with ExitStack() as es2:
            stage = [es2.enter_context(nc.sbuf_tensor(f"wstc{i}", [128, 2048], F32)) for i in range(2)]
            Rstage = [Res(), Res()]
            load_weight(Wu, RWu, w_up, 8, 5632, lambda c: gF[:, c:c + 1], stage, Rstage, wsem)
            load_weight(Wd, RWd, w_dn, 22, 1024, None, stage, Rstage, wsem)
            k.barrier()
        WG = 256
        hs = [tb(f"hs{i}", [128, 1024], F32) for i in range(4)]; Rhs = [Res() for _ in range(4)]
        lsem = [k.sem(f"l{i}") for i in range(4)]
        osem = [k.sem(f"o{i}") for i in range(4)]
        junk = tb("junkC", [128, 1024], BF16); Rjunk = Res()
        hn2 = [tb(f"hnC{i}", [128, 1024], BF16) for i in range(2)]; Rhn2 = [Res(), Res()]
        hnT2 = [tb(f"hnT{i}", [128, 8, WG], BF16) for i in range(2)]; RhnT2 = [Res(), Res()]
        actT2 = [tb(f"actT{i}", [128, 22, WG], BF16) for i in range(2)]; RactT2 = [Res(), Res()]
        upw = [tb(f"upw{i}", [128, 2 + WG], F32) for i in range(6)]; Rupw = [Res() for _ in range(6)]
        acc = [tb(f"acc{i}", [128, WG], F32) for i in range(6)]; Racc = [Res() for _ in range(6)]
        sg = [tb(f"sg{i}", [128, WG], F32) for i in range(3)]; Rsg = [Res() for _ in range(3)]
        uc = [0]
        hc = [0]
        groups = [(-1, 128, 0)] + [(g, WG, 128 + WG * g) for g in range(4096 // WG)]

        def do_group3(idx, gi, W, qc0):
            par = idx % 2
            hn, Rhn, hnT, RhnT, actT, RactT = hn2[par], Rhn2[par], hnT2[par], RhnT2[par], actT2[par], RactT2[par]
            ntt = W // 128
            hidx = []
            for tt in range(ntt):
                a = hc[0] % 4
                hc[0] += 1
                hidx.append(a)
                k.dma(SP, lsem[a], [(hs[a][:], H1[qc0 + tt * 128:qc0 + (tt + 1) * 128, :], [], [Rhs[a]])])
            yield
            for tt in range(ntt):
                a = hidx[tt]
                rs, Rr = rstd_of(hs[a][:], 1024, [Rhs[a]], junk[:], Rjunk)
                k.op(DVE, [Rhs[a], Rr], [Rhn], lambda: nc.vector.tensor_scalar(out=hn[:], in0=hs[a][:], scalar1=rs, scalar2=None, op0=ALU.mult))
                pv, Rp = transposes([hn[:, c * 128:(c + 1) * 128] for c in range(8)], [Rhn], None, None, 128, None)
                k.op(ACT, [Rp], [RhnT], lambda: nc.scalar.copy(out=hnT[:, :, tt * 128:(tt + 1) * 128], in_=pv[:, :].rearrange("p (c t) -> p c t", c=8)))
                yield
            for c in range(22):
                accs = []
                for part, ch in ((0, c), (1, 22 + c)):
                    p, R = ps_next()
                    col = ch * 128
                    for d in range(8):
                        k.op(PE, [RhnT, RWu], [R], lambda d=d: nc.tensor.matmul(p[:, 0:W], lhsT=Wu[:, d, col:col + 128], rhs=hnT[:, d, 0:W], start=(d == 0), stop=(d == 7)))
                    if gi < 0:
                        k.op(ACT, [R, R_const], [RHB], lambda: nc.scalar.activation(out=HB[:, ch, :], in_=p[:, W - 2:W], func=AF.Copy, scale=hflag[:, 0:1]))
                        continue
                    u = uc[0] % 6
                    uc[0] += 1
                    k.op(ACT, [R], [Rupw[u]], lambda: nc.scalar.copy(out=upw[u][:, 2:2 + W], in_=p[:, 0:W]))
                    k.op(DVE, [RHB], [Rupw[u]], lambda: nc.vector.tensor_copy(out=upw[u][:, 0:2], in_=HB[:, ch, :]))
                    k.op(ACT, [R, R_const], [Racc[u]], lambda: nc.scalar.activation(out=acc[u][:, 0:W], in_=p[:, 0:W], func=AF.Identity, scale=cw[:, ch, 2:3], bias=cb[:, ch:ch + 1]))
                    k.op(DVE, [Rupw[u], R_const, Racc[u]], [Racc[u]], lambda: nc.vector.scalar_tensor_tensor(out=acc[u][:, 0:W], in0=upw[u][:, 1:1 + W], scalar=cw[:, ch, 1:2], in1=acc[u][:, 0:W], op0=ALU.mult, op1=ALU.add))
                    k.op(DVE, [Rupw[u], R_const, Racc[u]], [Racc[u]], lambda: nc.vector.scalar_tensor_tensor(out=acc[u][:, 0:W], in0=upw[u][:, 0:W], scalar=cw[:, ch, 0:1], in1=acc[u][:, 0:W], op0=ALU.mult, op1=ALU.add))
                    k.op(POOL, [Rupw[u]], [RHB], lambda: nc.gpsimd.tensor_copy(out=HB[:, ch, :], in_=upw[u][:, W:W + 2]))
                    accs.append(u)
                if gi >= 0:
                    ug, uv = accs
                    si = c % 3
                    k.op(ACT, [Racc[ug]], [Rsg[si]], lambda: nc.scalar.activation(out=sg[si][:, 0:W], in_=acc[ug][:, 0:W], func=AF.Silu))
                    k.op(POOL, [Rsg[si], Racc[uv]], [RactT], lambda: nc.gpsimd.tensor_tensor(out=actT[:, c, 0:W], in0=sg[si][:, 0:W], in1=acc[uv][:, 0:W], op=ALU.mult))
                yield
            if gi >= 0:
                for tt in range(ntt):
                    a = hidx[tt]
                    for hf in range(2):
                        p, R = ps_next()
                        for c in range(22):
                            k.op(PE, [RactT, RWd], [R], lambda c=c: nc.tensor.matmul(p[:, :], lhsT=actT[:, c, tt * 128:(tt + 1) * 128], rhs=Wd[:, c, hf * 512:(hf + 1) * 512], start=(c == 0), stop=(c == 21)))
                        k.op(DVE, [R, Rhs[a]], [Rhs[a]], lambda: nc.vector.tensor_tensor(out=hs[a][:, hf * 512:(hf + 1) * 512], in0=p[:, :], in1=hs[a][:, hf * 512:(hf + 1) * 512], op=ALU.add))
                    rs, Rr = rstd_of(hs[a][:], 1024, [Rhs[a]], junk[:], Rjunk)
                    k.op(DVE, [Rhs[a], Rr, R_const], [Rhs[a]], lambda: nc.vector.scalar_tensor_tensor(out=hs[a][:], in0=hs[a][:], scalar=rs, in1=gO[:], op0=ALU.mult, op1=ALU.mult))
                    row = qc0 - 128 + tt * 128
                    k.dma(SP, osem[a], [(out_d[row:row + 128, :], hs[a][:], [Rhs[a]], [])])
                    yield

        run_pipelined([do_group3(n_, *g_) for n_, g_ in enumerate(groups)], int(os.environ.get("KOFF3", "13")))
        k.barrier()
    return nc


def _t5_bucket(rel):
    n = np.maximum(rel, 0)
    nf = np.maximum(n, 1).astype(np.float32)
    large = 16 + (np.log(nf / np.float32(16)) / np.float32(math.log(128 / 16)) * np.float32(16)).astype(np.int32)
    large = np.minimum(large, 31)
    return np.where(n < 16, n, large)


def _tables():
    kk = np.arange(128)[:, None]
    qq = np.arange(256)[None, :]
    T0, T1 = [], []
    for i in range(2):
        kb = i * 128 + kk
        rel = qq - kb
        T0.append(np.where(rel >= 0, _t5_bucket(rel), 32))
        T1.append(_t5_bucket(qq + 256 - kb))
    far = np.full((128, 256), 31)
    msk = np.full((128, 256), 32)
    nf = [np.concatenate([T1[0], far], 1), np.concatenate([T1[1], far], 1),
          np.concatenate([T0[0], T1[0]], 1), np.concatenate([T0[1], T1[1]], 1),
          np.concatenate([msk, T0[0]], 1), np.concatenate([msk, T0[1]], 1)]
    nfh = [T1[0][:, 128:], T1[1][:, 128:], T0[0][:, 128:], T0[1][:, 128:]]
    return np.stack(nf, 1), np.stack(nfh, 1)


_NC_CACHE = {}


def kernel(x, norm_attn_g, w_in, b_gate, q_norm_g, w_uq, kv_norm_g, w_ukv, rel_bias,
           w_branch_moba, w_branch_mla, w_out, norm_ffn_g, w_up, conv_w, conv_b, w_down,
           norm_final_g):
    f32 = np.float32
    bf = ml_dtypes.bfloat16
    x = np.asarray(x, f32)
    c = lambda a: np.ascontiguousarray(np.asarray(a, f32))
    nfi, nfhi = _tables()
    rb_ext = np.concatenate([np.asarray(rel_bias, f32), np.full((1, 8), NEG, f32)], 0)
    nf = np.ascontiguousarray(np.transpose(rb_ext[nfi], (3, 0, 1, 2)).reshape(8, 128, 6 * 512)).astype(bf)
    nfh = np.ascontiguousarray(np.transpose(rb_ext[nfhi], (3, 0, 1, 2)).reshape(8, 128, 4 * 128)).astype(bf)
    b31 = np.ascontiguousarray(np.broadcast_to(np.asarray(rel_bias, f32)[31][None, :], (128, 8)))
    kk = np.arange(128)[:, None]
    cm = np.stack([np.where(i * 128 + kk <= np.arange(512)[None, :], 0.0, NEG) for i in range(4)], 1)
    cm = np.ascontiguousarray(cm.reshape(128, 2048).astype(f32)).astype(bf)
    oh = (np.arange(8192)[None, :] // 256 == np.arange(32)[:, None]).astype(f32).astype(bf)
    idb = np.eye(128, dtype=f32).astype(bf)
    inv_freq = (np.float32(10000.0) ** (-np.arange(0, 32, 2, dtype=f32) / np.float32(32))).astype(f32)
    common = {
        "w_in": c(w_in[0]), "w_uq": c(w_uq[0]), "w_ukv": c(w_ukv[0]), "w_bm": c(w_branch_moba[0]),
        "w_bl": c(w_branch_mla[0]), "w_out": c(w_out[0]), "w_up": c(w_up[0]), "w_dn": c(w_down[0]),
        "gA": c(np.asarray(norm_attn_g, f32)[0].reshape(8, 128).T),
        "gF": c(np.asarray(norm_ffn_g, f32)[0].reshape(8, 128).T),
        "gQ": c(np.asarray(q_norm_g, f32)[0].reshape(2, 128).T),
        "gKV": c(np.asarray(kv_norm_g, f32)[0].reshape(1, 128).T),
        "bg": c(np.asarray(b_gate, f32)[0].reshape(16, 128).T),
        "cw": c(np.transpose(np.asarray(conv_w, f32)[0].reshape(3, 44, 128), (2, 1, 0)).reshape(128, 132)),
        "cb": c(np.asarray(conv_b, f32)[0].reshape(44, 128).T),
        "gO": c(np.broadcast_to(np.asarray(norm_final_g, f32)[None, :], (128, 1024))),
        "b31": b31, "oh": oh, "nf": nf, "nfh": nfh, "cm": cm, "idb": idb,
    }
    in_maps = []
    for core in range(8):
        b, half = core // 2, core % 2
        xcat = np.zeros((8192, 1024), f32)
        if half == 1:
            xcat[:4096] = x[b, :4096]
        xcat[4096:] = x[b, half * 4096:(half + 1) * 4096]
        kval = np.ones((128, 64), f32)
        kval[:, :32] = float(half)
        gbias = np.zeros((128, 8, 32), f32)
        if half == 0:
            gbias[:, :, :16] = NEG
        pos = (np.arange(8192) if half == 1 else np.concatenate([np.arange(4096), np.arange(4096)])).astype(f32)
        ang = pos[:, None] * inv_freq[None, :]
        cs, sn = np.cos(ang).astype(f32), np.sin(ang).astype(f32)
        csk = np.concatenate([cs, sn], 1).reshape(64, 128, 32)
        s = np.float32(96.0 ** -0.5)
        csq = np.concatenate([np.tile(cs[31 * 128:], (1, 8)) * s, np.tile(sn[31 * 128:], (1, 8)) * s], 1).reshape(NQT, 128, 256)
        m = dict(common)
        m.update({"xc": xcat, "kval": kval, "gbias": c(gbias.reshape(128, 256)),
                  "hflag": np.full((128, 1), float(half), f32), "csk": c(csk), "csq": c(csq)})
        in_maps.append(m)
    if "nc" not in _NC_CACHE:
        _NC_CACHE["nc"] = build_program()
    nc = _NC_CACHE["nc"]
    ncores = int(os.environ.get("KCORES", "8"))
    res = run_bass_kernel_spmd(nc, in_maps[:ncores], core_ids=list(range(ncores)))
    out = np.empty((4, 8192, 1024), f32)
    for core in range(ncores):
        b, half = core // 2, core % 2
        out[b, half * 4096:(half + 1) * 4096] = res.results[core]["out"]
    if DEBUG:
        kernel.last = res
    return out
```

```python
import math
import os
from contextlib import ExitStack

import ml_dtypes
import numpy as np

import concourse.bass as bass
import concourse.mybir as mybir
from concourse.bass_utils import run_bass_kernel_spmd

F32 = mybir.dt.float32
BF16 = mybir.dt.bfloat16
AF = mybir.ActivationFunctionType
ALU = mybir.AluOpType
AX = mybir.AxisListType

NEG = -30000.0
EPS = 1e-6
NT = 64
NQT = 33
NQ = NQT * 128
DEBUG = False
STOP = None
STRICT = int(os.environ.get('KSTRICT', '0'))


class Sem:
    def __init__(self, nc, name):
        self.h = nc.alloc_semaphore(name)
        self.v = 0


class Res:
    __slots__ = ("w", "r", "excl")

    def __init__(self, excl=False):
        self.w = None
        self.r = {}
        self.excl = excl


class Eng:
    def __init__(self, eng, sem):
        self.e = eng
        self.sem = sem
        self.seen = {}

    def wait(self, sem, val):
        if self.seen.get(id(sem), 0) >= val:
            return
        self.e.wait_ge(sem.h, val)
        self.seen[id(sem)] = val


class K:
    def __init__(self, nc):
        self.nc = nc
        self.sems = []
        self.pe = Eng(nc.tensor, self.sem("pe"))
        self.act = Eng(nc.scalar, self.sem("act"))
        self.dve = Eng(nc.vector, self.sem("dve"))
        self.pool = Eng(nc.gpsimd, self.sem("pool"))
        self.sp = Eng(nc.sync, self.sem("sp"))
        self.engs = [self.pe, self.act, self.dve, self.pool, self.sp]

    def sem(self, name):
        s = Sem(self.nc, name)
        self.sems.append(s)
        return s

    def _deps(self, eng, reads, writes):
        for r in reads:
            if r.w is not None:
                eng.wait(*r.w)
        strict = STRICT == 1 or (STRICT == 2 and eng is not self.pe)
        for w in writes:
            if w.w is not None and (strict or w.w[0] is not eng.sem):
                eng.wait(*w.w)
            for s, v in w.r.values():
                if strict or s is not eng.sem:
                    eng.wait(s, v)

    def _commit(self, ev, reads, writes):
        for w in writes:
            w.w = ev
            w.r = {}
        for r in reads:
            if r not in writes:
                r.r[id(ev[0])] = ev

    def op(self, eng, reads, writes, fn):
        ex = [r for r in reads if r.excl and r not in writes]
        if ex:
            writes = list(writes) + ex
        self._deps(eng, reads, writes)
        ins = fn()
        eng.sem.v += 1
        ins.then_inc(eng.sem.h, 1)
        self._commit((eng.sem, eng.sem.v), reads, writes)

    def dma(self, q, sem, items):
        for o, i, reads, writes in items:
            self._deps(q, reads, writes)
        for o, i, reads, writes in items:
            q.e.dma_start(out=o, in_=i).then_inc(sem.h, 16)
            sem.v += 16
        ev = (sem, sem.v)
        for o, i, reads, writes in items:
            self._commit(ev, reads, writes)

    def barrier(self):
        for e in self.engs:
            for s in self.sems:
                if s.v > 0:
                    e.wait(s, s.v)


class _Stop(Exception):
    pass


def chk(n):
    if STOP in ("P0a", "P0b") and int(os.environ.get("KSTEP", "99")) == n:
        raise _Stop()


def run_pipelined(gen_list, offset):
    gens = []
    nxt = 0
    while gens or nxt < len(gen_list):
        if nxt < len(gen_list) and len(gens) < 2 and (not gens or gens[-1][1] >= offset):
            gens.append([gen_list[nxt], 0])
            nxt += 1
        for ge in list(gens):
            try:
                next(ge[0])
                ge[1] += 1
            except StopIteration:
                gens.remove(ge)


def build_program():
    nc = bass.Bass("TRN2", target_bir_lowering=False)
    k = K(nc)
    PE, ACT, DVE, POOL, SP = k.pe, k.act, k.dve, k.pool, k.sp

    def din(name, shape, dt=F32):
        return nc.dram_tensor(name, list(shape), dt, kind="ExternalInput").ap()

    def dscr(name, shape, dt):
        kind = "ExternalOutput" if DEBUG else "Internal"
        return nc.dram_tensor(name, list(shape), dt, kind=kind).ap()

    xc = din("xc", [8192, 1024])
    w_in = din("w_in", [1024, 4000])
    w_uq = din("w_uq", [256, 768])
    w_ukv = din("w_ukv", [128, 1024])
    w_bm = din("w_bm", [512, 1024])
    w_bl = din("w_bl", [512, 1024])
    w_out = din("w_out", [1024, 1024])
    w_up = din("w_up", [1024, 5632])
    w_dn = din("w_dn", [2816, 1024])
    gA_d = din("gA", [128, 8])
    gF_d = din("gF", [128, 8])
    gQ_d = din("gQ", [128, 2])
    gKV_d = din("gKV", [128, 1])
    bg_d = din("bg", [128, 16])
    cw_d = din("cw", [128, 44 * 3])
    cb_d = din("cb", [128, 44])
    gO_d = din("gO", [128, 1024])
    kval_d = din("kval", [128, 64])
    gbias_d = din("gbias", [128, 256])
    hflag_d = din("hflag", [128, 1])
    b31_d = din("b31", [128, 8])
    csk_d = din("csk", [64, 128, 32])
    csq_d = din("csq", [NQT, 128, 256])
    oh_d = din("oh", [32, 8192], BF16)
    nf_d = din("nf", [8, 128, 6 * 512], BF16)
    nfh_d = din("nfh", [8, 128, 4 * 128], BF16)
    cm_d = din("cm", [128, 4 * 512], BF16)
    idb_d = din("idb", [128, 128], BF16)
    out_d = nc.dram_tensor("out", [4096, 1024], F32, kind="ExternalOutput").ap()

    KTa = dscr("KTa", [8, 64, 8192], BF16)
    KTb = dscr("KTb", [8, 64, 8192], BF16)
    KRT = dscr("KRT", [32, 8192], BF16)
    Va = dscr("Va", [8, 128, 64, 128], BF16)
    Vb = dscr("Vb", [8, 128, 64, 128], BF16)
    QTa = dscr("QTa", [8, 96, NQ], BF16)
    QTb = dscr("QTb", [8, 96, NQ], BF16)
    YT = dscr("YT", [16, 64, NQ], BF16)
    H1 = dscr("H1", [NQ, 1024], F32)

    psBig = nc.alloc_psum_tensor("psbig", [128, 4096], F32)
    psT = [psBig[:, i * 512:(i + 1) * 512] for i in range(8)]
    psR = [Res(excl=True) for _ in range(8)]
    pctr = [0]

    def ps_next(lo=0, hi=8):
        i = lo + pctr[0] % (hi - lo)
        pctr[0] += 1
        return psT[i], psR[i]

    def psbf(p):
        return p[:, :].bitcast(BF16)

    def sb(name, shape, dt):
        return nc.alloc_sbuf_tensor(name, list(shape), dt)

    idb = sb("idb_s", [128, 128], BF16)
    epsb = sb("epsb", [128, 1], F32)
    stat = sb("stat", [128, 16], F32)
    kval = sb("kval_s", [128, 64], F32)
    hflag = sb("hflag_s", [128, 1], F32)
    R_const = Res()
    R_stat = [Res() for _ in range(4)]
    sc = [0]

    k.dma(SP, k.sem("ld0"), [
        (idb[:], idb_d[:, :], [], [R_const]),
        (kval[:], kval_d[:, :], [], [R_const]),
        (hflag[:], hflag_d[:, :], [], [R_const]),
    ])
    k.op(POOL, [], [R_const], lambda: nc.gpsimd.memset(epsb[:], EPS))
    if STOP == "W0":
        k.barrier()
        return nc

    def rstd_of(src_ap, n, reads, junk, Rjunk):
        i = sc[0] % 4
        sc[0] += 1
        R = R_stat[i]
        ss = stat[:, 4 * i:4 * i + 1]
        sd = stat[:, 4 * i + 1:4 * i + 2]
        rs = stat[:, 4 * i + 2:4 * i + 3]
        k.op(ACT, reads, [R, Rjunk], lambda: nc.scalar.activation(out=junk, in_=src_ap, func=AF.Square, accum_out=ss))
        k.op(ACT, [R, R_const], [R], lambda: nc.scalar.activation(out=sd, in_=ss, func=AF.Sqrt, scale=1.0 / n, bias=epsb[:, 0:1]))
        k.op(DVE, [R], [R], lambda: nc.vector.reciprocal(out=rs, in_=sd))
        return rs, R

    def transposes(src_list, reads, dst_ap_fn, dst_writes, rows, copy_eng, alloc=None):
        p, R = (alloc or ps_next)()
        pv = psbf(p)
        n = len(src_list)
        for j, s in enumerate(src_list):
            k.op(PE, reads + [R_const], [R], lambda s=s, j=j: nc.tensor.transpose(out=pv[0:rows, j * 128:(j + 1) * 128], in_=s, identity=idb[:]))
        return pv, R

    def load_weight(dst, Rdst, src, nchunks, cols, scale_ap_fn, stage, Rstage, ssem, c0=0, rows=128):
        for c in range(nchunks):
            for off in range(0, cols, 2048):
                w = min(2048, cols - off)
                i = load_weight.ctr % 2
                load_weight.ctr += 1
                k.dma(SP, ssem[i], [(stage[i][0:rows, 0:w], src[c * rows:(c + 1) * rows, c0 + off:c0 + off + w], [], [Rstage[i]])])
                sap = scale_ap_fn(c) if scale_ap_fn else None
                if sap is not None:
                    if load_weight.ctr % 2:
                        k.op(ACT, [Rstage[i], R_const], [Rdst], lambda i=i, c=c, off=off, w=w, sap=sap: nc.scalar.activation(out=dst[0:rows, c, off:off + w], in_=stage[i][0:rows, 0:w], func=AF.Copy, scale=sap))
                    else:
                        k.op(DVE, [Rstage[i], R_const], [Rdst], lambda i=i, c=c, off=off, w=w, sap=sap: nc.vector.tensor_scalar(out=dst[0:rows, c, off:off + w], in0=stage[i][0:rows, 0:w], scalar1=sap, scalar2=None, op0=ALU.mult))
                else:
                    if load_weight.ctr % 2:
                        k.op(ACT, [Rstage[i]], [Rdst], lambda i=i, c=c, off=off, w=w: nc.scalar.copy(out=dst[0:rows, c, off:off + w], in_=stage[i][0:rows, 0:w]))
                    else:
                        k.op(DVE, [Rstage[i]], [Rdst], lambda i=i, c=c, off=off, w=w: nc.vector.tensor_copy(out=dst[0:rows, c, off:off + w], in_=stage[i][0:rows, 0:w]))
    load_weight.ctr = 0
    transposes_g = transposes
    wsem = [k.sem("ws0"), k.sem("ws1")]

    with ExitStack() as es:
        def tb(name, shape, dt):
            return es.enter_context(nc.sbuf_tensor(name, list(shape), dt))

        W0 = tb("W0", [128, 8, 1952], BF16); RW0 = Res()
        Wuq = tb("Wuq", [128, 2, 768], BF16); RWuq = Res()
        Wukv = tb("Wukv", [128, 1, 1024], BF16); RWukv = Res()
        gA = tb("gA_s", [128, 8], F32)
        gQ = tb("gQ_s", [128, 2], F32)
        gKV = tb("gKV_s", [128, 1], F32)
        gbias = tb("gbias_s", [128, 256], F32)
        k.dma(SP, k.sem("ld1"), [
            (gA[:], gA_d[:, :], [], [R_const]), (gQ[:], gQ_d[:, :], [], [R_const]),
            (gKV[:], gKV_d[:, :], [], [R_const]), (gbias[:], gbias_d[:, :], [], [R_const]),
        ])
        with ExitStack() as es0:
            stage = [es0.enter_context(nc.sbuf_tensor(f"wst{i}", [128, 2048], F32)) for i in range(2)]
            Rstage = [Res(), Res()]
            load_weight(W0, RW0, w_in, 8, 1952, lambda c: gA[:, c:c + 1], stage, Rstage, wsem)
            load_weight(Wuq, RWuq, w_uq, 2, 768, lambda c: gQ[:, c:c + 1], stage, Rstage, wsem)
            load_weight(Wukv, RWukv, w_ukv, 1, 1024, lambda c: gKV[:, 0:1], stage, Rstage, wsem)
            k.barrier()
        if STOP == "W":
            k.barrier()
            return nc

        xs = [tb(f"xs{i}", [128, 1024], F32) for i in range(3)]; Rxs = [Res() for _ in range(3)]
        xsem = [k.sem(f"x{i}") for i in range(3)]
        cqsem = [k.sem(f"cq{i}") for i in range(4)]
        cksem = [k.sem(f"ck{i}") for i in range(4)]
        csk = [tb(f"csk{i}", [128, 32], F32) for i in range(4)]; Rcsk = [Res() for _ in range(4)]
        csq = [tb(f"csq{i}", [128, 256], F32) for i in range(4)]; Rcsq = [Res() for _ in range(4)]
        junk = tb("junk", [128, 1024], BF16); Rjunk = Res()
        cS2 = [tb(f"cS{i}", [128, 160], F32) for i in range(3)]; RcS2 = [Res() for _ in range(3)]
        xn2 = [tb(f"xn{i}", [128, 1024], BF16) for i in range(3)]; Rxn2 = [Res() for _ in range(3)]
        xnT2 = [tb(f"xnT{i}", [128, 8, 128], BF16) for i in range(3)]; RxnT2 = [Res() for _ in range(3)]
        kA2 = [tb(f"kA{i}", [128, 512], BF16) for i in range(3)]; RkA2 = [Res() for _ in range(3)]
        kB2 = [tb(f"kB{i}", [128, 8, 64], BF16) for i in range(3)]; RkB2 = [Res() for _ in range(3)]
        qA2 = [tb(f"qA{i}", [128, 512], BF16) for i in range(3)]; RqA2 = [Res() for _ in range(3)]
        qB2 = [tb(f"qB{i}", [128, 8, 96], BF16) for i in range(3)]; RqB2 = [Res() for _ in range(3)]
        Mfull2 = [tb(f"Mfull{i}", [128, 8, 96], BF16) for i in range(3)]; RMf2 = [Res() for _ in range(3)]
        ckvn2 = [tb(f"ckvn{i}", [128, 128], BF16) for i in range(3)]; Rckvn2 = [Res() for _ in range(3)]
        ckvnT2 = [tb(f"ckvnT{i}", [128, 128], BF16) for i in range(3)]; RckvnT2 = [Res() for _ in range(3)]
        cqn2 = [tb(f"cqn{i}", [128, 256], BF16) for i in range(3)]; Rcqn2 = [Res() for _ in range(3)]
        cqnT2 = [tb(f"cqnT{i}", [128, 2, 128], BF16) for i in range(3)]; RcqnT2 = [Res() for _ in range(3)]
        krr2 = [tb(f"krr{i}", [128, 32], BF16) for i in range(3)]; Rkrr2 = [Res() for _ in range(3)]
        rt2 = [tb(f"rt{i}", [128, 4, 64], F32) for i in range(3)]; Rrt2 = [Res() for _ in range(3)]
        qf2 = [tb(f"qf{i}", [128, 384], F32) for i in range(3)]; Rqf2 = [Res() for _ in range(3)]
        ksum = tb("ksum", [64, 8, 32], F32); Rksum = Res()
        kpart2 = [tb(f"kpart{i}", [64, 16], F32) for i in range(2)]; Rkpart2 = [Res() for _ in range(3)]; Rksum2 = [Res(), Res()]
        kmT = tb("kmT", [64, 8, 32], BF16); RkmT = Res()
        gateS2 = [tb(f"gateS{i}", [128, 8, 32], F32) for i in range(3)]; RgS2 = [Res() for _ in range(3)]
        top82 = [tb(f"top8{i}", [128, 64], F32) for i in range(3)]; Rtop2 = [Res() for _ in range(3)]
        onesb = tb("onesb", [128, 8, 64], BF16)
        kTbA = [tb(f"kTbA{i}", [64, 8, 512], BF16) for i in range(2)]
        kTbB = [tb(f"kTbB{i}", [64, 8, 512], BF16) for i in range(2)]
        krTb = [tb(f"krTb{i}", [32, 512], BF16) for i in range(2)]
        VBa = [tb(f"VBa{i}", [128, 8, 4, 128], BF16) for i in range(2)]
        VBb = [tb(f"VBb{i}", [128, 8, 4, 128], BF16) for i in range(2)]
        qTbA = [tb(f"qTbA{i}", [96, 8, 512], BF16) for i in range(2)]
        qTbB = [tb(f"qTbB{i}", [96, 8, 512], BF16) for i in range(2)]
        RF = [[Res() for _ in range(4)] for _ in range(2)]
        stsem = [k.sem("st0"), k.sem("st1")]

        k.op(POOL, [], [R_const], lambda: nc.gpsimd.memset(onesb[:], 1.0))
        for i in range(3):
            k.op(POOL, [], [RMf2[i]], lambda: nc.gpsimd.memset(Mfull2[i][:], 0.0))
        k.op(POOL, [], [RkmT], lambda: nc.gpsimd.memset(kmT[:], 0.0))

        def issue_x(t):
            i = t % 3
            i4 = t % 4
            k.dma(SP, xsem[i], [(xs[i][:], xc[t * 128:(t + 1) * 128, :], [], [Rxs[i]])])
            k.dma(SP, cksem[i4], [(csk[i4][:], csk_d[t], [], [Rcsk[i4]])])
            if t >= 31:
                k.dma(SP, cqsem[i4], [(csq[i4][:], csq_d[t - 31], [], [Rcsq[i4]])])

        issue_x(0)

        def do_tile(t):
            i = t % 2
            x3 = t % 3
            x4 = t % 4
            sx = t % 3
            quad, j = t // 4, t % 4
            qp = quad % 2
            isq = t >= 31
            nblk = t // 2
            RFj = RF[qp][j]
            xn = xn2[sx]; Rxn = Rxn2[sx]
            xnT = xnT2[sx]; RxnT = RxnT2[sx]
            kA = kA2[sx]; RkA = RkA2[sx]
            kB = kB2[sx]; RkB = RkB2[sx]
            qA = qA2[sx]; RqA = RqA2[sx]
            qB = qB2[sx]; RqB = RqB2[sx]
            Mfull = Mfull2[sx]; RMf = RMf2[sx]
            ckvn = ckvn2[sx]; Rckvn = Rckvn2[sx]
            ckvnT = ckvnT2[sx]; RckvnT = RckvnT2[sx]
            cqn = cqn2[sx]; Rcqn = Rcqn2[sx]
            cqnT = cqnT2[sx]; RcqnT = RcqnT2[sx]
            krr = krr2[sx]; Rkrr = Rkrr2[sx]
            rt = rt2[sx]; Rrt = Rrt2[sx]
            qf = qf2[sx]; Rqf = Rqf2[sx]
            gateS = gateS2[sx]; RgS = RgS2[sx]
            top8 = top82[sx]; Rtop = Rtop2[sx]
            kpart = kpart2[nblk % 2]; Rkpart = Rkpart2[nblk % 2]; Rksum = Rksum2[nblk % 2]
            cS = cS2[sx]; RcS = RcS2[sx]
            cnt = [0]

            def ps_next():
                bnk = 2 * sx + cnt[0] % 2
                cnt[0] += 1
                return psT[bnk], psR[bnk]

            def transposes(src_list, reads, a_, b_, rows, c_):
                return transposes_g(src_list, reads, a_, b_, rows, c_, alloc=ps_next)
            if t + 1 < NT:
                issue_x(t + 1)
            rs, Rr = rstd_of(xs[x3][:], 1024, [Rxs[x3]], junk[:], Rjunk)
            k.op(DVE, [Rxs[x3], Rr], [Rxn], lambda: nc.vector.tensor_scalar(out=xn[:], in0=xs[x3][:], scalar1=rs, scalar2=None, op0=ALU.mult))
            yield
            chk(1)
            pv, Rp = transposes([xn[:, c * 128:(c + 1) * 128] for c in range(8)], [Rxn], None, None, 128, None)
            k.op(ACT, [Rp], [RxnT], lambda: nc.scalar.copy(out=xnT[:].rearrange("p c t -> p (c t)"), in_=pv[:, :]))
            yield
            chk(2)

            def proj(c0, c1):
                p, R = ps_next()
                for c in range(8):
                    k.op(PE, [RxnT, RW0], [R], lambda c=c: nc.tensor.matmul(p[:, 0:c1 - c0], lhsT=xnT[:, c, :], rhs=W0[:, c, c0:c1], start=(c == 0), stop=(c == 7)))
                return p, R
            p_k, R_k = proj(512, 1024)

            chk(3)
            k.op(ACT, [R_k], [RkA], lambda: nc.scalar.copy(out=kA[:], in_=p_k[:, :]))
            yield
            pv, Rp = transposes([kA[:, h * 64:(h + 1) * 64] for h in range(8)], [RkA], None, None, 64, None)
            pv3 = pv[0:64, :].rearrange("p (h t) -> p h t", h=8)
            chk(31)
            k.op(ACT, [Rp], [RFj], lambda: nc.scalar.copy(out=kTbA[qp][:, :, j * 128:(j + 1) * 128], in_=pv3))
            yield
            chk(32)
            ksrc = kTbA[qp][:, :, j * 128:(j + 1) * 128]
            if t % 2 == 0:
                k.op(DVE, [RFj], [Rksum], lambda: nc.vector.reduce_sum(out=kpart[:, 0:8], in_=ksrc, axis=AX.X))
                yield
            else:
                k.op(DVE, [RFj], [Rkpart], lambda: nc.vector.reduce_sum(out=kpart[:, 8:16], in_=ksrc, axis=AX.X))
                yield
                k.op(DVE, [Rkpart, Rksum], [RkmT], lambda: nc.vector.tensor_tensor(out=kmT[:, :, nblk], in0=kpart[:, 0:8], in1=kpart[:, 8:16], op=ALU.add))
                yield
            chk(33)
            chk(4)
            p_v, R_v = proj(1024, 1536)
            k.op(ACT, [R_v], [RFj], lambda: nc.scalar.copy(out=VBa[qp][:, :, j, 0:64], in_=p_v[:, :].rearrange("p (h d) -> p h d", h=8)))
            yield
            k.op(DVE, [R_const], [RFj], lambda: nc.vector.tensor_scalar(out=VBa[qp][:, :, j, 64:128], in0=onesb[:], scalar1=kval[:, t:t + 1], scalar2=None, op0=ALU.mult))
            yield
            k.op(ACT, [R_const], [RFj], lambda: nc.scalar.activation(out=VBb[qp][:, :, j, 64:128], in_=onesb[:], func=AF.Copy, scale=kval[:, t:t + 1]))
            yield

            chk(5)
            p_cp, R_cp = proj(1792, 1952)
            k.op(ACT, [R_cp], [RcS], lambda: nc.scalar.copy(out=cS[:], in_=p_cp[:, 0:160]))
            yield
            p_c = cS; R_c = RcS
            rs2, Rr2 = rstd_of(p_c[:, 0:128], 128, [R_c], junk[:, 0:128], Rjunk)
            k.op(DVE, [R_c, Rr2], [Rckvn], lambda: nc.vector.tensor_scalar(out=ckvn[:], in0=p_c[:, 0:128], scalar1=rs2, scalar2=None, op0=ALU.mult))
            yield
            x1 = p_c[:, 128:144]; x2 = p_c[:, 144:160]
            co = csk[x4][:, 0:16]; si = csk[x4][:, 16:32]
            k.op(DVE, [R_c, Rcsk[x4]], [Rrt], lambda: nc.vector.tensor_tensor(out=rt[:, 0, 0:16], in0=x1, in1=co, op=ALU.mult))
            yield
            k.op(DVE, [R_c, Rcsk[x4]], [Rrt], lambda: nc.vector.tensor_tensor(out=rt[:, 1, 0:16], in0=x2, in1=si, op=ALU.mult))
            yield
            k.op(DVE, [R_c, Rcsk[x4]], [Rrt], lambda: nc.vector.tensor_tensor(out=rt[:, 2, 0:16], in0=x2, in1=co, op=ALU.mult))
            yield
            k.op(DVE, [R_c, Rcsk[x4]], [Rrt], lambda: nc.vector.tensor_tensor(out=rt[:, 3, 0:16], in0=x1, in1=si, op=ALU.mult))
            yield
            k.op(DVE, [Rrt], [Rkrr], lambda: nc.vector.tensor_tensor(out=krr[:, 0:16], in0=rt[:, 0, 0:16], in1=rt[:, 1, 0:16], op=ALU.subtract))
            yield
            k.op(DVE, [Rrt], [Rkrr], lambda: nc.vector.tensor_tensor(out=krr[:, 16:32], in0=rt[:, 2, 0:16], in1=rt[:, 3, 0:16], op=ALU.add))
            yield
            chk(6)
            pv, Rp = transposes([ckvn[:]], [Rckvn], None, None, 128, None)
            k.op(ACT, [Rp], [RckvnT], lambda: nc.scalar.copy(out=ckvnT[:], in_=pv[:, 0:128]))
            yield
            pv, Rp = transposes([krr[:]], [Rkrr], None, None, 32, None)
            k.op(ACT, [Rp], [RFj], lambda: nc.scalar.copy(out=krTb[qp][:, j * 128:(j + 1) * 128], in_=pv[0:32, 0:128]))
            yield
            chk(7)
            for hh in range(2):
                p, R = ps_next()
                k.op(PE, [RckvnT, RWukv], [R], lambda: nc.tensor.matmul(p[:, :], lhsT=ckvnT[:], rhs=Wukv[:, 0, hh * 512:(hh + 1) * 512], start=True, stop=True))
                yield
                p4 = p[:, :].rearrange("p (h two d) -> p h two d", h=4, two=2)
                k.op(ACT, [R], [RkB], lambda: nc.scalar.copy(out=kB[:, hh * 4:hh * 4 + 4, :], in_=p4[:, :, 0, :]))
                yield
                k.op(DVE, [R], [RFj], lambda: nc.vector.tensor_copy(out=VBb[qp][:, hh * 4:hh * 4 + 4, j, 0:64], in_=p4[:, :, 1, :]))
                yield
            pv, Rp = transposes([kB[:, h, :] for h in range(8)], [RkB], None, None, 64, None)
            pv3 = pv[0:64, :].rearrange("p (h t) -> p h t", h=8)
            k.op(ACT, [Rp], [RFj], lambda: nc.scalar.copy(out=kTbB[qp][:, :, j * 128:(j + 1) * 128], in_=pv3))
            yield

            chk(8)
            if isq:
                p_q, R_q = proj(0, 512)
                k.op(ACT, [R_q], [RqA], lambda: nc.scalar.activation(out=qA[:], in_=p_q[:, :], func=AF.Copy, scale=0.125))
                yield
                pv, Rp = transposes([qA[:, h * 64:(h + 1) * 64] for h in range(8)], [RqA], None, None, 64, None)
                pv3 = pv[0:64, :].rearrange("p (h t) -> p h t", h=8)
                k.op(ACT, [Rp], [RFj], lambda: nc.scalar.copy(out=qTbA[qp][0:64, :, j * 128:(j + 1) * 128], in_=pv3))
                yield
                chk(41)
                pg, Rg = ps_next()
                for h in range(8):
                    k.op(PE, [RFj, RkmT], [Rg], lambda h=h: nc.tensor.matmul(pg[:, h * 32:(h + 1) * 32], lhsT=qTbA[qp][0:64, h, j * 128:(j + 1) * 128], rhs=kmT[:, h, :], start=True, stop=True))
                chk(42)
                k.op(DVE, [Rg, R_const], [RgS], lambda: nc.vector.tensor_tensor(out=gateS[:].rearrange("p h n -> p (h n)"), in0=pg[:, 0:256], in1=gbias[:], op=ALU.add))
                yield
                if nblk < 32:
                    k.op(DVE, [], [RgS], lambda: nc.vector.memset(gateS[:, :, nblk:32], NEG))
                chk(43)
                for h in range(8):
                    k.op(DVE, [RgS], [Rtop], lambda h=h: nc.vector.max(out=top8[:, h * 8:(h + 1) * 8], in_=gateS[:, h, :]))
                chk(44)
                for h in range(8):
                    k.op(DVE, [RgS, Rtop], [RMf], lambda h=h: nc.vector.tensor_scalar(out=Mfull[:, h, 64:96], in0=gateS[:, h, :], scalar1=top8[:, h * 8 + 2:h * 8 + 3], scalar2=NEG, op0=ALU.is_lt, op1=ALU.mult))
                k.op(DVE, [], [RMf], lambda: nc.vector.memset(Mfull[:, :, 64 + nblk:65 + nblk], 0.0))
                yield
                chk(45)
                pv, Rp = transposes([Mfull[:, h, :] for h in range(8)], [RMf], None, None, 96, None)
                pv3 = pv[64:96, :].rearrange("p (h t) -> p h t", h=8)
                k.op(ACT, [Rp], [RFj], lambda: nc.scalar.copy(out=qTbA[qp][64:96, :, j * 128:(j + 1) * 128], in_=pv3))
                yield
                chk(46)
                p_cq, R_cq = proj(1536, 1792)
                rs3, Rr3 = rstd_of(p_cq[:, 0:256], 256, [R_cq], junk[:, 0:256], Rjunk)
                k.op(DVE, [R_cq, Rr3], [Rcqn], lambda: nc.vector.tensor_scalar(out=cqn[:], in0=p_cq[:, 0:256], scalar1=rs3, scalar2=None, op0=ALU.mult))
                yield
                pv, Rp = transposes([cqn[:, c * 128:(c + 1) * 128] for c in range(2)], [Rcqn], None, None, 128, None)
                k.op(ACT, [Rp], [RcqnT], lambda: nc.scalar.copy(out=cqnT[:].rearrange("p c t -> p (c t)"), in_=pv[:, 0:256]))
                yield
                chk(47)
                for hh in range(2):
                    p, R = ps_next()
                    for c in range(2):
                        k.op(PE, [RcqnT, RWuq], [R], lambda c=c: nc.tensor.matmul(p[:, 0:384], lhsT=cqnT[:, c, :], rhs=Wuq[:, c, hh * 384:(hh + 1) * 384], start=(c == 0), stop=(c == 1)))
                    p3 = p[:, 0:384].rearrange("p (h d) -> p h d", h=4)
                    hs_ = slice(hh * 4, hh * 4 + 4)
                    k.op(ACT, [R], [RqB], lambda: nc.scalar.activation(out=qB[:, hs_, 0:64], in_=p3[:, :, 0:64], func=AF.Copy, scale=96.0 ** -0.5))
                    chk(48)
                    k.op(ACT, [R], [Rqf], lambda: nc.scalar.copy(out=qf[:], in_=p[:, 0:384]))
                    q3 = qf[:].rearrange("p (h d) -> p h d", h=4)
                    x1 = q3[:, :, 64:80]; x2 = q3[:, :, 80:96]
                    co = csq[x4][:, 0:128].rearrange("p (h f) -> p h f", h=8)[:, hs_, :]
                    si = csq[x4][:, 128:256].rearrange("p (h f) -> p h f", h=8)[:, hs_, :]
                    k.op(DVE, [Rqf, Rcsq[x4]], [Rrt], lambda: nc.vector.tensor_tensor(out=rt[:, :, 0:16], in0=x1, in1=co, op=ALU.mult))
                    k.op(DVE, [Rqf, Rcsq[x4]], [Rrt], lambda: nc.vector.tensor_tensor(out=rt[:, :, 16:32], in0=x2, in1=si, op=ALU.mult))
                    k.op(DVE, [Rqf, Rcsq[x4]], [Rrt], lambda: nc.vector.tensor_tensor(out=rt[:, :, 32:48], in0=x2, in1=co, op=ALU.mult))
                    k.op(DVE, [Rqf, Rcsq[x4]], [Rrt], lambda: nc.vector.tensor_tensor(out=rt[:, :, 48:64], in0=x1, in1=si, op=ALU.mult))
                    k.op(DVE, [Rrt], [RqB], lambda: nc.vector.tensor_tensor(out=qB[:, hs_, 64:80], in0=rt[:, :, 0:16], in1=rt[:, :, 16:32], op=ALU.subtract))
                    k.op(DVE, [Rrt], [RqB], lambda: nc.vector.tensor_tensor(out=qB[:, hs_, 80:96], in0=rt[:, :, 32:48], in1=rt[:, :, 48:64], op=ALU.add))
                chk(49)
                pv, Rp = transposes([qB[:, h, :] for h in range(8)], [RqB], None, None, 96, None)
                pv3 = pv[0:96, :].rearrange("p (h t) -> p h t", h=8)
                k.op(ACT, [Rp], [RFj], lambda: nc.scalar.copy(out=qTbB[qp][:, :, j * 128:(j + 1) * 128], in_=pv3))
                yield
                chk(50)

            if j == 3:
                rd = RF[qp]
                ks = slice(quad * 512, quad * 512 + 512)
                items = [
                    (KTa[:, :, ks].rearrange("h d k -> d h k"), kTbA[qp][:], rd, []),
                    (KTb[:, :, ks].rearrange("h d k -> d h k"), kTbB[qp][:], rd, []),
                    (KRT[:, ks], krTb[qp][:], rd, []),
                    (Va[:, :, quad * 4:quad * 4 + 4, :].rearrange("h p t c -> p h t c"), VBa[qp][:], rd, []),
                    (Vb[:, :, quad * 4:quad * 4 + 4, :].rearrange("h p t c -> p h t c"), VBb[qp][:], rd, []),
                ]
                if quad == 7:
                    items.append((QTa[:, :, 0:128].rearrange("h d k -> d h k"), qTbA[qp][:, :, 384:512], rd, []))
                    items.append((QTb[:, :, 0:128].rearrange("h d k -> d h k"), qTbB[qp][:, :, 384:512], rd, []))
                elif quad >= 8:
                    qs = slice(128 + (quad - 8) * 512, 128 + (quad - 8) * 512 + 512)
                    items.append((QTa[:, :, qs].rearrange("h d k -> d h k"), qTbA[qp][:], rd, []))
                    items.append((QTb[:, :, qs].rearrange("h d k -> d h k"), qTbB[qp][:], rd, []))
                k.dma(POOL, stsem[qp], items)
            if STOP == "P0a" and t == 3:
                raise _Stop()
        try:
            gens = []
            t_next = 0
            OFFSET = int(os.environ.get("KOFF", "8"))
            while gens or t_next < NT:
                if t_next < NT and len(gens) < 3 and (not gens or gens[-1][1] >= OFFSET):
                    gens.append([do_tile(t_next), 0])
                    t_next += 1
                for ge in list(gens):
                    try:
                        next(ge[0])
                        ge[1] += 1
                    except StopIteration:
                        gens.remove(ge)
        except _Stop:
            pass
        k.barrier()
    if STOP in ("P0", "P0a", "P0b"):
        return nc

    with ExitStack() as es:
        def tb(name, shape, dt):
            return es.enter_context(nc.sbuf_tensor(name, list(shape), dt))
        Kt = [tb(f"Kt{i}", [96, 8192], BF16) for i in range(2)]
        Vt = [tb(f"Vt{i}", [128, 64, 128], BF16) for i in range(2)]
        Qt = [tb(f"Qt{i}", [96, NQ], BF16) for i in range(2)]
        NF = [tb(f"NF{i}", [128, 6, 512], BF16) for i in range(2)]
        NFh = [tb(f"NFh{i}", [128, 4, 128], BF16) for i in range(2)]
        Rh = [Res(), Res()]
        hsem = [k.sem("h0"), k.sem("h1")]
        CM = tb("CM", [128, 4, 512], BF16)
        b31 = tb("b31_s", [128, 8], F32)
        Pt = [tb(f"Pt{i}", [128, 1024], BF16) for i in range(4)]
        RPt = [Res() for _ in range(4)]
        rsb = [tb(f"rsb{i}", [64, 512], F32) for i in range(2)]
        yTb = [tb(f"yTb{i}", [64, 512], BF16) for i in range(2)]
        Ry = [Res(), Res()]
        ysem = [k.sem("y0"), k.sem("y1")]
        k.dma(SP, k.sem("ld2"), [(CM[:].rearrange("p a b -> p (a b)"), cm_d[:, :], [], [R_const]),
                                 (b31[:], b31_d[:, :], [], [R_const])])

        def load_head(hh):
            i = hh % 2
            h = hh % 8
            moba = hh < 8
            items = [
                (Kt[i][0:64, :], (KTa if moba else KTb)[h], [], [Rh[i]]),
                (Kt[i][64:96, :], oh_d[:, :] if moba else KRT[:, :], [], [Rh[i]]),
                (Vt[i][:], (Va if moba else Vb)[h], [], [Rh[i]]),
                (Qt[i][:], (QTa if moba else QTb)[h], [], [Rh[i]]),
            ]
            if moba:
                items.append((NF[i][:].rearrange("p a b -> p (a b)"), nf_d[h], [], [Rh[i]]))
                items.append((NFh[i][:].rearrange("p a b -> p (a b)"), nfh_d[h], [], [Rh[i]]))
            k.dma(SP, hsem[i], items)

        load_head(0)
        pcnt = [0]
        gcnt = [0]
        for hh in range(16):
            i = hh % 2
            h = hh % 8
            moba = hh < 8
            if hh + 1 < 16:
                load_head(hh + 1)
            for gi in range(-1, 8):
                if gi < 0:
                    W, qc0, nvis = 128, 0, 32
                else:
                    W, qc0, nvis = 512, 128 + 512 * gi, 32 + 4 * gi + 4
                tabs = {}
                if moba:
                    if gi < 0:
                        for a in range(4):
                            tabs[28 + a] = NFh[i][:, a, :]
                    else:
                        for a in range(6):
                            tabs[30 + 4 * gi + a] = NF[i][:, a, :]
                else:
                    if gi < 0:
                        tabs[31] = CM[:, 0, 0:128]
                    else:
                        for a in range(4):
                            tabs[32 + 4 * gi + a] = CM[:, a, :]
                po, Ro = ps_next(6, 8)
                pend = []

                def emit_pv(kt, pi, first, last):
                    k.op(PE, [Rh[i], RPt[pi]], [Ro], lambda: nc.tensor.matmul(po[:, 0:W], lhsT=Vt[i][:, kt, :], rhs=Pt[pi][:, 0:W], start=first, stop=last))

                for kt in range(nvis):
                    p, R = ps_next(0, 6)
                    tab = tabs.get(kt)
                    k.op(PE, [Rh[i]], [R], lambda: nc.tensor.matmul(p[:, 0:W], lhsT=Kt[i][:, kt * 128:(kt + 1) * 128], rhs=Qt[i][:, qc0:qc0 + W], start=True, stop=(tab is None)))
                    if tab is not None:
                        k.op(PE, [Rh[i], R_const], [R], lambda: nc.tensor.matmul(p[:, 0:W], lhsT=idb[:], rhs=tab, start=False, stop=True))
                    pi = pcnt[0] % 4
                    pcnt[0] += 1
                    if moba and tab is None:
                        k.op(ACT, [R, R_const], [RPt[pi]], lambda: nc.scalar.activation(out=Pt[pi][:, 0:W], in_=p[:, 0:W], func=AF.Exp, bias=b31[:, h:h + 1]))
                    else:
                        k.op(ACT, [R], [RPt[pi]], lambda: nc.scalar.activation(out=Pt[pi][:, 0:W], in_=p[:, 0:W], func=AF.Exp))
                    pend.append((kt, pi))
                    if len(pend) > 2:
                        kt0, pi0 = pend.pop(0)
                        emit_pv(kt0, pi0, kt0 == 0, False)
                while pend:
                    kt0, pi0 = pend.pop(0)
                    emit_pv(kt0, pi0, kt0 == 0, kt0 == nvis - 1)
                yi = gcnt[0] % 2
                gcnt[0] += 1
                k.op(DVE, [Ro], [Ry[yi]], lambda: nc.vector.tensor_scalar(out=rsb[yi][:, 0:W], in0=po[64:128, 0:W], scalar1=1e-30, scalar2=None, op0=ALU.max))
                k.op(DVE, [Ry[yi]], [Ry[yi]], lambda: nc.vector.reciprocal(out=rsb[yi][:, 0:W], in_=rsb[yi][:, 0:W]))
                k.op(DVE, [Ro, Ry[yi]], [Ry[yi]], lambda: nc.vector.tensor_tensor(out=yTb[yi][:, 0:W], in0=po[0:64, 0:W], in1=rsb[yi][:, 0:W], op=ALU.mult))
                k.dma(POOL, ysem[yi], [(YT[hh, :, qc0:qc0 + W], yTb[yi][:, 0:W], [Ry[yi]], [])])
        k.barrier()
    if STOP == "P1":
        return nc

    with ExitStack() as es:
        def tb(name, shape, dt):
            return es.enter_context(nc.sbuf_tensor(name, list(shape), dt))
        Wg = tb("Wg", [128, 8, 2048], BF16); RWg = Res()
        Wm = tb("Wm", [64, 8, 1024], BF16); RWm = Res()
        Wl = tb("Wl", [64, 8, 1024], BF16); RWl = Res()
        Wo = tb("Wo", [128, 8, 1024], BF16); RWo = Res()
        gA = tb("gA2", [128, 8], F32)
        bg = tb("bg_s", [128, 16], F32)
        k.dma(SP, k.sem("ld3"), [(gA[:], gA_d[:, :], [], [R_const]), (bg[:], bg_d[:, :], [], [R_const])])
        with ExitStack() as es2:
            stage = [es2.enter_context(nc.sbuf_tensor(f"wstb{i}", [128, 2048], F32)) for i in range(2)]
            Rstage = [Res(), Res()]
            load_weight(Wg, RWg, w_in, 8, 2048, lambda c: gA[:, c:c + 1], stage, Rstage, wsem, c0=1952)
            load_weight(Wm, RWm, w_bm, 8, 1024, None, stage, Rstage, wsem, rows=64)
            load_weight(Wl, RWl, w_bl, 8, 1024, None, stage, Rstage, wsem, rows=64)
            load_weight(Wo, RWo, w_out, 8, 1024, None, stage, Rstage, wsem)
            k.barrier()
        WG2 = 256
        xs = [tb(f"xsB{i}", [128, 1024], F32) for i in range(4)]; Rxs = [Res() for _ in range(4)]
        xsem = [k.sem(f"xb{i}") for i in range(4)]
        hsem2 = [k.sem(f"hs{i}") for i in range(4)]
        junk = tb("junkB", [128, 1024], BF16); Rjunk = Res()
        xn2 = [tb(f"xnB{i}", [128, 1024], BF16) for i in range(2)]; Rxn2 = [Res(), Res()]
        xnT2 = [tb(f"xnTB{i}", [128, 8, WG2], BF16) for i in range(2)]; RxnT2 = [Res(), Res()]
        ysb2 = [tb(f"ysb{i}", [64, 16, WG2], BF16) for i in range(2)]; Rysb2 = [Res(), Res()]
        ysem2 = [k.sem("ysb0"), k.sem("ysb1")]
        gT2 = [tb(f"gT{i}", [128, 16, WG2], F32) for i in range(2)]; RgT2 = [Res(), Res()]
        t1 = [tb(f"t1{i}", [128, WG2], F32) for i in range(4)]
        t2 = [tb(f"t2{i}", [128, WG2], F32) for i in range(4)]
        Rt = [Res() for _ in range(4)]
        mixT2 = [tb(f"mixT{i}", [128, 8, WG2], BF16) for i in range(2)]; Rmix2 = [Res(), Res()]
        xcnt = [0]

        def do_group2(idx, gi, W, qc0, r0):
            par = idx % 2
            xn, Rxn, xnT, RxnT = xn2[par], Rxn2[par], xnT2[par], RxnT2[par]
            ysb, Rysb, gT, RgT, mixT, Rmix = ysb2[par], Rysb2[par], gT2[par], RgT2[par], mixT2[par], Rmix2[par]
            ntt = W // 128
            sl = []
            for tt in range(ntt):
                sl.append(xcnt[0] % 4)
                xcnt[0] += 1
            k.dma(SP, ysem2[par], [(ysb[:, :, 0:W], YT[:, :, qc0:qc0 + W].rearrange("h d k -> d h k"), [], [Rysb])])
            for tt in range(ntt):
                a_ = sl[tt]
                k.dma(SP, xsem[a_], [(xs[a_][:], xc[r0 + tt * 128:r0 + (tt + 1) * 128, :], [], [Rxs[a_]])])
            yield
            for tt in range(ntt):
                a_ = sl[tt]
                rs, Rr = rstd_of(xs[a_][:], 1024, [Rxs[a_]], junk[:], Rjunk)
                k.op(DVE, [Rxs[a_], Rr], [Rxn], lambda: nc.vector.tensor_scalar(out=xn[:], in0=xs[a_][:], scalar1=rs, scalar2=None, op0=ALU.mult))
                pv, Rp = transposes([xn[:, c * 128:(c + 1) * 128] for c in range(8)], [Rxn], None, None, 128, None)
                k.op(ACT, [Rp], [RxnT], lambda: nc.scalar.copy(out=xnT[:, :, tt * 128:(tt + 1) * 128], in_=pv[:, :].rearrange("p (c t) -> p c t", c=8)))
                yield
            for ct in range(16):
                p, R = ps_next()
                for c in range(8):
                    k.op(PE, [RxnT, RWg], [R], lambda c=c: nc.tensor.matmul(p[:, 0:W], lhsT=Wg[:, c, ct * 128:(ct + 1) * 128], rhs=xnT[:, c, 0:W], start=(c == 0), stop=(c == 7)))
                k.op(ACT, [R, R_const], [RgT], lambda: nc.scalar.activation(out=gT[:, ct, 0:W], in_=p[:, 0:W], func=AF.Sigmoid, bias=bg[:, ct:ct + 1]))
                yield
            for ct in range(8):
                pa, Ra = ps_next()
                for h in range(8):
                    k.op(PE, [Rysb, RWm], [Ra], lambda h=h: nc.tensor.matmul(pa[:, 0:W], lhsT=Wm[:, h, ct * 128:(ct + 1) * 128], rhs=ysb[:, h, 0:W], start=(h == 0), stop=(h == 7)))
                pb, Rb = ps_next()
                for h in range(8):
                    k.op(PE, [Rysb, RWl], [Rb], lambda h=h: nc.tensor.matmul(pb[:, 0:W], lhsT=Wl[:, h, ct * 128:(ct + 1) * 128], rhs=ysb[:, 8 + h, 0:W], start=(h == 0), stop=(h == 7)))
                ti = ct % 4
                k.op(DVE, [Ra, RgT], [Rt[ti]], lambda: nc.vector.tensor_tensor(out=t1[ti][:, 0:W], in0=pa[:, 0:W], in1=gT[:, ct, 0:W], op=ALU.mult))
                k.op(DVE, [Rb, RgT], [Rt[ti]], lambda: nc.vector.tensor_tensor(out=t2[ti][:, 0:W], in0=pb[:, 0:W], in1=gT[:, 8 + ct, 0:W], op=ALU.mult))
                k.op(DVE, [Rt[ti]], [Rmix], lambda: nc.vector.tensor_tensor(out=mixT[:, ct, 0:W], in0=t1[ti][:, 0:W], in1=t2[ti][:, 0:W], op=ALU.add))
                yield
            for tt in range(ntt):
                a_ = sl[tt]
                for hf in range(2):
                    p, R = ps_next()
                    for c in range(8):
                        k.op(PE, [Rmix, RWo], [R], lambda c=c: nc.tensor.matmul(p[:, :], lhsT=mixT[:, c, tt * 128:(tt + 1) * 128], rhs=Wo[:, c, hf * 512:(hf + 1) * 512], start=(c == 0), stop=(c == 7)))
                    k.op(DVE, [R, Rxs[a_]], [Rxs[a_]], lambda: nc.vector.tensor_tensor(out=xs[a_][:, hf * 512:(hf + 1) * 512], in0=p[:, :], in1=xs[a_][:, hf * 512:(hf + 1) * 512], op=ALU.add))
                k.dma(SP, hsem2[a_], [(H1[qc0 + tt * 128:qc0 + (tt + 1) * 128, :], xs[a_][:], [Rxs[a_]], [])])
                yield

        groups2 = [(-1, 128, 0, 31 * 128)] + [(g, WG2, 128 + WG2 * g, 4096 + WG2 * g) for g in range(4096 // WG2)]
        run_pipelined([do_group2(n_, *g_) for n_, g_ in enumerate(groups2)], int(os.environ.get("KOFF2", "14")))
        k.barrier()
    if STOP == "P2":
        return nc

    with ExitStack() as es:
        def tb(name, shape, dt):
            return es.enter_context(nc.sbuf_tensor(name, list(shape), dt))
        Wu = tb("Wu", [128, 8, 5632], BF16); RWu = Res()
        Wd = tb("Wd", [128, 22, 1024], BF16); RWd = Res()
        gF = tb("gF_s", [128, 8], F32)
        cw = tb("cw_s", [128, 44, 3], F32)
        cb = tb("cb_s", [128, 44], F32)
        gO = tb("gO_s", [128, 1024], F32)
        HB = tb("HB", [128, 44, 2], F32); RHB = Res()
        k.dma(SP, k.sem("ld4"), [(gF[:], gF_d[:, :], [], [R_const]), (cw[:].rearrange("p a b -> p (a b)"), cw_d[:, :], [], [R_const]),
                                 (cb[:], cb_d[:, :], [], [R_const]), (gO[:], gO_d[:, :], [], [R_const])])
        with ExitStack() as es2:
            stage = [es2.enter_context(nc.sbuf_tensor(f"wstc{i}", [128, 2048], F32)) for i in range(2)]
            Rstage = [Res(), Res()]
            load_weight(Wu, RWu, w_up, 8, 5632, lambda c: gF[:, c:c + 1], stage, Rstage, wsem)
            load_weight(Wd, RWd, w_dn, 22, 1024, None, stage, Rstage, wsem)
            k.barrier()
        WG = 256
        hs = [tb(f"hs{i}", [128, 1024], F32) for i in range(4)]; Rhs = [Res() for _ in range(4)]
        lsem = [k.sem(f"l{i}") for i in range(4)]
        osem = [k.sem(f"o{i}") for i in range(4)]
        junk = tb("junkC", [128, 1024], BF16); Rjunk = Res()
        hn2 = [tb(f"hnC{i}", [128, 1024], BF16) for i in range(2)]; Rhn2 = [Res(), Res()]
        hnT2 = [tb(f"hnT{i}", [128, 8, WG], BF16) for i in range(2)]; RhnT2 = [Res(), Res()]
        actT2 = [tb(f"actT{i}", [128, 22, WG], BF16) for i in range(2)]; RactT2 = [Res(), Res()]
        upw = [tb(f"upw{i}", [128, 2 + WG], F32) for i in range(6)]; Rupw = [Res() for _ in range(6)]
        acc = [tb(f"acc{i}", [128, WG], F32) for i in range(6)]; Racc = [Res() for _ in range(6)]
        sg = [tb(f"sg{i}", [128, WG], F32) for i in range(3)]; Rsg = [Res() for _ in range(3)]
        uc = [0]
        hc = [0]
        groups = [(-1, 128, 0)] + [(g, WG, 128 + WG * g) for g in range(4096 // WG)]

        def do_group3(idx, gi, W, qc0):
            par = idx % 2
            hn, Rhn, hnT, RhnT, actT, RactT = hn2[par], Rhn2[par], hnT2[par], RhnT2[par], actT2[par], RactT2[par]
            ntt = W // 128
            hidx = []
            for tt in range(ntt):
                a = hc[0] % 4
                hc[0] += 1
                hidx.append(a)
                k.dma(SP, lsem[a], [(hs[a][:], H1[qc0 + tt * 128:qc0 + (tt + 1) * 128, :], [], [Rhs[a]])])
            yield
            for tt in range(ntt):
                a = hidx[tt]
                rs, Rr = rstd_of(hs[a][:], 1024, [Rhs[a]], junk[:], Rjunk)
                k.op(DVE, [Rhs[a], Rr], [Rhn], lambda: nc.vector.tensor_scalar(out=hn[:], in0=hs[a][:], scalar1=rs, scalar2=None, op0=ALU.mult))
                pv, Rp = transposes([hn[:, c * 128:(c + 1) * 128] for c in range(8)], [Rhn], None, None, 128, None)
                k.op(ACT, [Rp], [RhnT], lambda: nc.scalar.copy(out=hnT[:, :, tt * 128:(tt + 1) * 128], in_=pv[:, :].rearrange("p (c t) -> p c t", c=8)))
                yield
            for c in range(22):
                accs = []
                for part, ch in ((0, c), (1, 22 + c)):
                    p, R = ps_next()
                    col = ch * 128
                    for d in range(8):
                        k.op(PE, [RhnT, RWu], [R], lambda d=d: nc.tensor.matmul(p[:, 0:W], lhsT=Wu[:, d, col:col + 128], rhs=hnT[:, d, 0:W], start=(d == 0), stop=(d == 7)))
                    if gi < 0:
                        k.op(ACT, [R, R_const], [RHB], lambda: nc.scalar.activation(out=HB[:, ch, :], in_=p[:, W - 2:W], func=AF.Copy, scale=hflag[:, 0:1]))
                        continue
                    u = uc[0] % 6
                    uc[0] += 1
                    k.op(ACT, [R], [Rupw[u]], lambda: nc.scalar.copy(out=upw[u][:, 2:2 + W], in_=p[:, 0:W]))
                    k.op(DVE, [RHB], [Rupw[u]], lambda: nc.vector.tensor_copy(out=upw[u][:, 0:2], in_=HB[:, ch, :]))
                    k.op(ACT, [R, R_const], [Racc[u]], lambda: nc.scalar.activation(out=acc[u][:, 0:W], in_=p[:, 0:W], func=AF.Identity, scale=cw[:, ch, 2:3], bias=cb[:, ch:ch + 1]))
                    k.op(DVE, [Rupw[u], R_const, Racc[u]], [Racc[u]], lambda: nc.vector.scalar_tensor_tensor(out=acc[u][:, 0:W], in0=upw[u][:, 1:1 + W], scalar=cw[:, ch, 1:2], in1=acc[u][:, 0:W], op0=ALU.mult, op1=ALU.add))
                    k.op(DVE, [Rupw[u], R_const, Racc[u]], [Racc[u]], lambda: nc.vector.scalar_tensor_tensor(out=acc[u][:, 0:W], in0=upw[u][:, 0:W], scalar=cw[:, ch, 0:1], in1=acc[u][:, 0:W], op0=ALU.mult, op1=ALU.add))
                    k.op(POOL, [Rupw[u]], [RHB], lambda: nc.gpsimd.tensor_copy(out=HB[:, ch, :], in_=upw[u][:, W:W + 2]))
                    accs.append(u)
                if gi >= 0:
                    ug, uv = accs
                    si = c % 3
                    k.op(ACT, [Racc[ug]], [Rsg[si]], lambda: nc.scalar.activation(out=sg[si][:, 0:W], in_=acc[ug][:, 0:W], func=AF.Silu))
                    k.op(POOL, [Rsg[si], Racc[uv]], [RactT], lambda: nc.gpsimd.tensor_tensor(out=actT[:, c, 0:W], in0=sg[si][:, 0:W], in1=acc[uv][:, 0:W], op=ALU.mult))
                yield
            if gi >= 0:
                for tt in range(ntt):
                    a = hidx[tt]
                    for hf in range(2):
                        p, R = ps_next()
                        for c in range(22):
                            k.op(PE, [RactT, RWd], [R], lambda c=c: nc.tensor.matmul(p[:, :], lhsT=actT[:, c, tt * 128:(tt + 1) * 128], rhs=Wd[:, c, hf * 512:(hf + 1) * 512], start=(c == 0), stop=(c == 21)))
                        k.op(DVE, [R, Rhs[a]], [Rhs[a]], lambda: nc.vector.tensor_tensor(out=hs[a][:, hf * 512:(hf + 1) * 512], in0=p[:, :], in1=hs[a][:, hf * 512:(hf + 1) * 512], op=ALU.add))
                    rs, Rr = rstd_of(hs[a][:], 1024, [Rhs[a]], junk[:], Rjunk)
                    k.op(DVE, [Rhs[a], Rr, R_const], [Rhs[a]], lambda: nc.vector.scalar_tensor_tensor(out=hs[a][:], in0=hs[a][:], scalar=rs, in1=gO[:], op0=ALU.mult, op1=ALU.mult))
                    row = qc0 - 128 + tt * 128
                    k.dma(SP, osem[a], [(out_d[row:row + 128, :], hs[a][:], [Rhs[a]], [])])
                    yield

        run_pipelined([do_group3(n_, *g_) for n_, g_ in enumerate(groups)], int(os.environ.get("KOFF3", "13")))
        k.barrier()
    return nc


def _t5_bucket(rel):
    n = np.maximum(rel, 0)
    nf = np.maximum(n, 1).astype(np.float32)
    large = 16 + (np.log(nf / np.float32(16)) / np.float32(math.log(128 / 16)) * np.float32(16)).astype(np.int32)
    large = np.minimum(large, 31)
    return np.where(n < 16, n, large)


def _tables():
    kk = np.arange(128)[:, None]
    qq = np.arange(256)[None, :]
    T0, T1 = [], []
    for i in range(2):
        kb = i * 128 + kk
        rel = qq - kb
        T0.append(np.where(rel >= 0, _t5_bucket(rel), 32))
        T1.append(_t5_bucket(qq + 256 - kb))
    far = np.full((128, 256), 31)
    msk = np.full((128, 256), 32)
    nf = [np.concatenate([T1[0], far], 1), np.concatenate([T1[1], far], 1),
          np.concatenate([T0[0], T1[0]], 1), np.concatenate([T0[1], T1[1]], 1),
          np.concatenate([msk, T0[0]], 1), np.concatenate([msk, T0[1]], 1)]
    nfh = [T1[0][:, 128:], T1[1][:, 128:], T0[0][:, 128:], T0[1][:, 128:]]
    return np.stack(nf, 1), np.stack(nfh, 1)


_NC_CACHE = {}


def kernel(x, norm_attn_g, w_in, b_gate, q_norm_g, w_uq, kv_norm_g, w_ukv, rel_bias,
           w_branch_moba, w_branch_mla, w_out, norm_ffn_g, w_up, conv_w, conv_b, w_down,
           norm_final_g):
    f32 = np.float32
    bf = ml_dtypes.bfloat16
    x = np.asarray(x, f32)
    c = lambda a: np.ascontiguousarray(np.asarray(a, f32))
    nfi, nfhi = _tables()
    rb_ext = np.concatenate([np.asarray(rel_bias, f32), np.full((1, 8), NEG, f32)], 0)
    nf = np.ascontiguousarray(np.transpose(rb_ext[nfi], (3, 0, 1, 2)).reshape(8, 128, 6 * 512)).astype(bf)
    nfh = np.ascontiguousarray(np.transpose(rb_ext[nfhi], (3, 0, 1, 2)).reshape(8, 128, 4 * 128)).astype(bf)
    b31 = np.ascontiguousarray(np.broadcast_to(np.asarray(rel_bias, f32)[31][None, :], (128, 8)))
    kk = np.arange(128)[:, None]
    cm = np.stack([np.where(i * 128 + kk <= np.arange(512)[None, :], 0.0, NEG) for i in range(4)], 1)
    cm = np.ascontiguousarray(cm.reshape(128, 2048).astype(f32)).astype(bf)
    oh = (np.arange(8192)[None, :] // 256 == np.arange(32)[:, None]).astype(f32).astype(bf)
    idb = np.eye(128, dtype=f32).astype(bf)
    inv_freq = (np.float32(10000.0) ** (-np.arange(0, 32, 2, dtype=f32) / np.float32(32))).astype(f32)
    common = {
        "w_in": c(w_in[0]), "w_uq": c(w_uq[0]), "w_ukv": c(w_ukv[0]), "w_bm": c(w_branch_moba[0]),
        "w_bl": c(w_branch_mla[0]), "w_out": c(w_out[0]), "w_up": c(w_up[0]), "w_dn": c(w_down[0]),
        "gA": c(np.asarray(norm_attn_g, f32)[0].reshape(8, 128).T),
        "gF": c(np.asarray(norm_ffn_g, f32)[0].reshape(8, 128).T),
        "gQ": c(np.asarray(q_norm_g, f32)[0].reshape(2, 128).T),
        "gKV": c(np.asarray(kv_norm_g, f32)[0].reshape(1, 128).T),
        "bg": c(np.asarray(b_gate, f32)[0].reshape(16, 128).T),
        "cw": c(np.transpose(np.asarray(conv_w, f32)[0].reshape(3, 44, 128), (2, 1, 0)).reshape(128, 132)),
        "cb": c(np.asarray(conv_b, f32)[0].reshape(44, 128).T),
        "gO": c(np.broadcast_to(np.asarray(norm_final_g, f32)[None, :], (128, 1024))),
        "b31": b31, "oh": oh, "nf": nf, "nfh": nfh, "cm": cm, "idb": idb,
    }
    in_maps = []
    for core in range(8):
        b, half = core // 2, core % 2
        xcat = np.zeros((8192, 1024), f32)
        if half == 1:
            xcat[:4096] = x[b, :4096]
        xcat[4096:] = x[b, half * 4096:(half + 1) * 4096]
        kval = np.ones((128, 64), f32)
        kval[:, :32] = float(half)
        gbias = np.zeros((128, 8, 32), f32)
        if half == 0:
            gbias[:, :, :16] = NEG
        pos = (np.arange(8192) if half == 1 else np.concatenate([np.arange(4096), np.arange(4096)])).astype(f32)
        ang = pos[:, None] * inv_freq[None, :]
        cs, sn = np.cos(ang).astype(f32), np.sin(ang).astype(f32)
        csk = np.concatenate([cs, sn], 1).reshape(64, 128, 32)
        s = np.float32(96.0 ** -0.5)
        csq = np.concatenate([np.tile(cs[31 * 128:], (1, 8)) * s, np.tile(sn[31 * 128:], (1, 8)) * s], 1).reshape(NQT, 128, 256)
        m = dict(common)
        m.update({"xc": xcat, "kval": kval, "gbias": c(gbias.reshape(128, 256)),
                  "hflag": np.full((128, 1), float(half), f32), "csk": c(csk), "csq": c(csq)})
        in_maps.append(m)
    if "nc" not in _NC_CACHE:
        _NC_CACHE["nc"] = build_program()
    nc = _NC_CACHE["nc"]
    ncores = int(os.environ.get("KCORES", "8"))
    res = run_bass_kernel_spmd(nc, in_maps[:ncores], core_ids=list(range(ncores)))
    out = np.empty((4, 8192, 1024), f32)
    for core in range(ncores):
        b, half = core // 2, core % 2
        out[b, half * 4096:(half + 1) * 4096] = res.results[core]["out"]
    if DEBUG:
        kernel.last = res
    return out
```

```python
import math
import os
from contextlib import ExitStack

import ml_dtypes
import numpy as np

import concourse.bass as bass
import concourse.mybir as mybir
from concourse.bass_utils import run_bass_kernel_spmd

F32 = mybir.dt.float32
BF16 = mybir.dt.bfloat16
AF = mybir.ActivationFunctionType
ALU = mybir.AluOpType
AX = mybir.AxisListType

NEG = -30000.0
EPS = 1e-6
NT = 64
NQT = 33
NQ = NQT * 128
DEBUG = False
STOP = None
STRICT = int(os.environ.get('KSTRICT', '2'))


class Sem:
    def __init__(self, nc, name):
        self.h = nc.alloc_semaphore(name)
        self.v = 0


class Res:
    __slots__ = ("w", "r", "excl")

    def __init__(self, excl=False):
        self.w = None
        self.r = {}
        self.excl = excl


class Eng:
    def __init__(self, eng, sem):
        self.e = eng
        self.sem = sem
        self.seen = {}

    def wait(self, sem, val):
        if self.seen.get(id(sem), 0) >= val:
            return
        self.e.wait_ge(sem.h, val)
        self.seen[id(sem)] = val


class K:
    def __init__(self, nc):
        self.nc = nc
        self.sems = []
        self.pe = Eng(nc.tensor, self.sem("pe"))
        self.act = Eng(nc.scalar, self.sem("act"))
        self.dve = Eng(nc.vector, self.sem("dve"))
        self.pool = Eng(nc.gpsimd, self.sem("pool"))
        self.sp = Eng(nc.sync, self.sem("sp"))
        self.engs = [self.pe, self.act, self.dve, self.pool, self.sp]

    def sem(self, name):
        s = Sem(self.nc, name)
        self.sems.append(s)
        return s

    def _deps(self, eng, reads, writes):
        for r in reads:
            if r.w is not None:
                eng.wait(*r.w)
        strict = STRICT == 1 or (STRICT == 2 and eng is not self.pe)
        for w in writes:
            if w.w is not None and (strict or w.w[0] is not eng.sem):
                eng.wait(*w.w)
            for s, v in w.r.values():
                if strict or s is not eng.sem:
                    eng.wait(s, v)

    def _commit(self, ev, reads, writes):
        for w in writes:
            w.w = ev
            w.r = {}
        for r in reads:
            if r not in writes:
                r.r[id(ev[0])] = ev

    def op(self, eng, reads, writes, fn):
        ex = [r for r in reads if r.excl and r not in writes]
        if ex:
            writes = list(writes) + ex
        self._deps(eng, reads, writes)
        ins = fn()
        eng.sem.v += 1
        ins.then_inc(eng.sem.h, 1)
        self._commit((eng.sem, eng.sem.v), reads, writes)

    def dma(self, q, sem, items):
        for o, i, reads, writes in items:
            self._deps(q, reads, writes)
        for o, i, reads, writes in items:
            q.e.dma_start(out=o, in_=i).then_inc(sem.h, 16)
            sem.v += 16
        ev = (sem, sem.v)
        for o, i, reads, writes in items:
            self._commit(ev, reads, writes)

    def barrier(self):
        for e in self.engs:
            for s in self.sems:
                if s.v > 0:
                    e.wait(s, s.v)


class _Stop(Exception):
    pass


def chk(n):
    if STOP in ("P0a", "P0b") and int(os.environ.get("KSTEP", "99")) == n:
        raise _Stop()


def run_pipelined(gen_list, offset):
    gens = []
    nxt = 0
    while gens or nxt < len(gen_list):
        if nxt < len(gen_list) and len(gens) < 2 and (not gens or gens[-1][1] >= offset):
            gens.append([gen_list[nxt], 0])
            nxt += 1
        for ge in list(gens):
            try:
                next(ge[0])
                ge[1] += 1
            except StopIteration:
                gens.remove(ge)


def build_program():
    nc = bass.Bass("TRN2", target_bir_lowering=False)
    k = K(nc)
    PE, ACT, DVE, POOL, SP = k.pe, k.act, k.dve, k.pool, k.sp

    def din(name, shape, dt=F32):
        return nc.dram_tensor(name, list(shape), dt, kind="ExternalInput").ap()

    def dscr(name, shape, dt):
        kind = "ExternalOutput" if DEBUG else "Internal"
        return nc.dram_tensor(name, list(shape), dt, kind=kind).ap()

    xc = din("xc", [8192, 1024])
    w_in = din("w_in", [1024, 4000])
    w_uq = din("w_uq", [256, 768])
    w_ukv = din("w_ukv", [128, 1024])
    w_bm = din("w_bm", [512, 1024])
    w_bl = din("w_bl", [512, 1024])
    w_out = din("w_out", [1024, 1024])
    w_up = din("w_up", [1024, 5632])
    w_dn = din("w_dn", [2816, 1024])
    gA_d = din("gA", [128, 8])
    gF_d = din("gF", [128, 8])
    gQ_d = din("gQ", [128, 2])
    gKV_d = din("gKV", [128, 1])
    bg_d = din("bg", [128, 16])
    cw_d = din("cw", [128, 44 * 3])
    cb_d = din("cb", [128, 44])
    gO_d = din("gO", [128, 1024])
    kval_d = din("kval", [128, 64])
    gbias_d = din("gbias", [128, 256])
    hflag_d = din("hflag", [128, 1])
    b31_d = din("b31", [128, 8])
    csk_d = din("csk", [64, 128, 32])
    csq_d = din("csq", [NQT, 128, 256])
    oh_d = din("oh", [32, 8192], BF16)
    nf_d = din("nf", [8, 128, 6 * 512], BF16)
    nfh_d = din("nfh", [8, 128, 4 * 128], BF16)
    cm_d = din("cm", [128, 4 * 512], BF16)
    idb_d = din("idb", [128, 128], BF16)
    out_d = nc.dram_tensor("out", [4096, 1024], F32, kind="ExternalOutput").ap()

    KTa = dscr("KTa", [8, 64, 8192], BF16)
    KTb = dscr("KTb", [8, 64, 8192], BF16)
    KRT = dscr("KRT", [32, 8192], BF16)
    Va = dscr("Va", [8, 128, 64, 128], BF16)
    Vb = dscr("Vb", [8, 128, 64, 128], BF16)
    QTa = dscr("QTa", [8, 96, NQ], BF16)
    QTb = dscr("QTb", [8, 96, NQ], BF16)
    YT = dscr("YT", [16, 64, NQ], BF16)
    H1 = dscr("H1", [NQ, 1024], F32)

    psBig = nc.alloc_psum_tensor("psbig", [128, 4096], F32)
    psT = [psBig[:, i * 512:(i + 1) * 512] for i in range(8)]
    psR = [Res(excl=True) for _ in range(8)]
    pctr = [0]

    def ps_next(lo=0, hi=8):
        i = lo + pctr[0] % (hi - lo)
        pctr[0] += 1
        return psT[i], psR[i]

    def psbf(p):
        return p[:, :].bitcast(BF16)

    def sb(name, shape, dt):
        return nc.alloc_sbuf_tensor(name, list(shape), dt)

    idb = sb("idb_s", [128, 128], BF16)
    epsb = sb("epsb", [128, 1], F32)
    stat = sb("stat", [128, 16], F32)
    kval = sb("kval_s", [128, 64], F32)
    hflag = sb("hflag_s", [128, 1], F32)
    R_const = Res()
    R_stat = [Res() for _ in range(4)]
    sc = [0]

    k.dma(SP, k.sem("ld0"), [
        (idb[:], idb_d[:, :], [], [R_const]),
        (kval[:], kval_d[:, :], [], [R_const]),
        (hflag[:], hflag_d[:, :], [], [R_const]),
    ])
    k.op(POOL, [], [R_const], lambda: nc.gpsimd.memset(epsb[:], EPS))
    if STOP == "W0":
        k.barrier()
        return nc

    def rstd_of(src_ap, n, reads, junk, Rjunk):
        i = sc[0] % 4
        sc[0] += 1
        R = R_stat[i]
        ss = stat[:, 4 * i:4 * i + 1]
        sd = stat[:, 4 * i + 1:4 * i + 2]
        rs = stat[:, 4 * i + 2:4 * i + 3]
        k.op(ACT, reads, [R, Rjunk], lambda: nc.scalar.activation(out=junk, in_=src_ap, func=AF.Square, accum_out=ss))
        k.op(ACT, [R, R_const], [R], lambda: nc.scalar.activation(out=sd, in_=ss, func=AF.Sqrt, scale=1.0 / n, bias=epsb[:, 0:1]))
        k.op(DVE, [R], [R], lambda: nc.vector.reciprocal(out=rs, in_=sd))
        return rs, R

    def transposes(src_list, reads, dst_ap_fn, dst_writes, rows, copy_eng, alloc=None):
        p, R = (alloc or ps_next)()
        pv = psbf(p)
        n = len(src_list)
        for j, s in enumerate(src_list):
            k.op(PE, reads + [R_const], [R], lambda s=s, j=j: nc.tensor.transpose(out=pv[0:rows, j * 128:(j + 1) * 128], in_=s, identity=idb[:]))
        return pv, R

    def load_weight(dst, Rdst, src, nchunks, cols, scale_ap_fn, stage, Rstage, ssem, c0=0, rows=128):
        for c in range(nchunks):
            for off in range(0, cols, 2048):
                w = min(2048, cols - off)
                i = load_weight.ctr % 2
                load_weight.ctr += 1
                k.dma(SP, ssem[i], [(stage[i][0:rows, 0:w], src[c * rows:(c + 1) * rows, c0 + off:c0 + off + w], [], [Rstage[i]])])
                sap = scale_ap_fn(c) if scale_ap_fn else None
                if sap is not None:
                    if load_weight.ctr % 2:
                        k.op(ACT, [Rstage[i], R_const], [Rdst], lambda i=i, c=c, off=off, w=w, sap=sap: nc.scalar.activation(out=dst[0:rows, c, off:off + w], in_=stage[i][0:rows, 0:w], func=AF.Copy, scale=sap))
                    else:
                        k.op(DVE, [Rstage[i], R_const], [Rdst], lambda i=i, c=c, off=off, w=w, sap=sap: nc.vector.tensor_scalar(out=dst[0:rows, c, off:off + w], in0=stage[i][0:rows, 0:w], scalar1=sap, scalar2=None, op0=ALU.mult))
                else:
                    if load_weight.ctr % 2:
                        k.op(ACT, [Rstage[i]], [Rdst], lambda i=i, c=c, off=off, w=w: nc.scalar.copy(out=dst[0:rows, c, off:off + w], in_=stage[i][0:rows, 0:w]))
                    else:
                        k.op(DVE, [Rstage[i]], [Rdst], lambda i=i, c=c, off=off, w=w: nc.vector.tensor_copy(out=dst[0:rows, c, off:off + w], in_=stage[i][0:rows, 0:w]))
    load_weight.ctr = 0
    transposes_g = transposes
    wsem = [k.sem("ws0"), k.sem("ws1")]

    with ExitStack() as es:
        def tb(name, shape, dt):
            return es.enter_context(nc.sbuf_tensor(name, list(shape), dt))

        W0 = tb("W0", [128, 8, 1952], BF16); RW0 = Res()
        Wuq = tb("Wuq", [128, 2, 768], BF16); RWuq = Res()
        Wukv = tb("Wukv", [128, 1, 1024], BF16); RWukv = Res()
        gA = tb("gA_s", [128, 8], F32)
        gQ = tb("gQ_s", [128, 2], F32)
        gKV = tb("gKV_s", [128, 1], F32)
        gbias = tb("gbias_s", [128, 256], F32)
        k.dma(SP, k.sem("ld1"), [
            (gA[:], gA_d[:, :], [], [R_const]), (gQ[:], gQ_d[:, :], [], [R_const]),
            (gKV[:], gKV_d[:, :], [], [R_const]), (gbias[:], gbias_d[:, :], [], [R_const]),
        ])
        with ExitStack() as es0:
            stage = [es0.enter_context(nc.sbuf_tensor(f"wst{i}", [128, 2048], F32)) for i in range(2)]
            Rstage = [Res(), Res()]
            load_weight(W0, RW0, w_in, 8, 1952, lambda c: gA[:, c:c + 1], stage, Rstage, wsem)
            load_weight(Wuq, RWuq, w_uq, 2, 768, lambda c: gQ[:, c:c + 1], stage, Rstage, wsem)
            load_weight(Wukv, RWukv, w_ukv, 1, 1024, lambda c: gKV[:, 0:1], stage, Rstage, wsem)
            k.barrier()
        if STOP == "W":
            k.barrier()
            return nc

        xs = [tb(f"xs{i}", [128, 1024], F32) for i in range(3)]; Rxs = [Res() for _ in range(3)]
        xsem = [k.sem(f"x{i}") for i in range(3)]
        cqsem = [k.sem(f"cq{i}") for i in range(4)]
        cksem = [k.sem(f"ck{i}") for i in range(4)]
        csk = [tb(f"csk{i}", [128, 32], F32) for i in range(4)]; Rcsk = [Res() for _ in range(4)]
        csq = [tb(f"csq{i}", [128, 256], F32) for i in range(4)]; Rcsq = [Res() for _ in range(4)]
        junk = tb("junk", [128, 1024], BF16); Rjunk = Res()
        cS2 = [tb(f"cS{i}", [128, 160], F32) for i in range(3)]; RcS2 = [Res() for _ in range(3)]
        xn2 = [tb(f"xn{i}", [128, 1024], BF16) for i in range(3)]; Rxn2 = [Res() for _ in range(3)]
        xnT2 = [tb(f"xnT{i}", [128, 8, 128], BF16) for i in range(3)]; RxnT2 = [Res() for _ in range(3)]
        kA2 = [tb(f"kA{i}", [128, 512], BF16) for i in range(3)]; RkA2 = [Res() for _ in range(3)]
        kB2 = [tb(f"kB{i}", [128, 8, 64], BF16) for i in range(3)]; RkB2 = [Res() for _ in range(3)]
        qA2 = [tb(f"qA{i}", [128, 512], BF16) for i in range(3)]; RqA2 = [Res() for _ in range(3)]
        qB2 = [tb(f"qB{i}", [128, 8, 96], BF16) for i in range(3)]; RqB2 = [Res() for _ in range(3)]
        Mfull2 = [tb(f"Mfull{i}", [128, 8, 96], BF16) for i in range(3)]; RMf2 = [Res() for _ in range(3)]
        ckvn2 = [tb(f"ckvn{i}", [128, 128], BF16) for i in range(3)]; Rckvn2 = [Res() for _ in range(3)]
        ckvnT2 = [tb(f"ckvnT{i}", [128, 128], BF16) for i in range(3)]; RckvnT2 = [Res() for _ in range(3)]
        cqn2 = [tb(f"cqn{i}", [128, 256], BF16) for i in range(3)]; Rcqn2 = [Res() for _ in range(3)]
        cqnT2 = [tb(f"cqnT{i}", [128, 2, 128], BF16) for i in range(3)]; RcqnT2 = [Res() for _ in range(3)]
        krr2 = [tb(f"krr{i}", [128, 32], BF16) for i in range(3)]; Rkrr2 = [Res() for _ in range(3)]
        rt2 = [tb(f"rt{i}", [128, 4, 64], F32) for i in range(3)]; Rrt2 = [Res() for _ in range(3)]
        qf2 = [tb(f"qf{i}", [128, 384], F32) for i in range(3)]; Rqf2 = [Res() for _ in range(3)]
        ksum = tb("ksum", [64, 8, 32], F32); Rksum = Res()
        kpart2 = [tb(f"kpart{i}", [64, 16], F32) for i in range(2)]; Rkpart2 = [Res() for _ in range(3)]; Rksum2 = [Res(), Res()]
        kmT = tb("kmT", [64, 8, 32], BF16); RkmT = Res()
        gateS2 = [tb(f"gateS{i}", [128, 8, 32], F32) for i in range(3)]; RgS2 = [Res() for _ in range(3)]
        top82 = [tb(f"top8{i}", [128, 64], F32) for i in range(3)]; Rtop2 = [Res() for _ in range(3)]
        onesb = tb("onesb", [128, 8, 64], BF16)
        kTbA = [tb(f"kTbA{i}", [64, 8, 512], BF16) for i in range(2)]
        kTbB = [tb(f"kTbB{i}", [64, 8, 512], BF16) for i in range(2)]
        krTb = [tb(f"krTb{i}", [32, 512], BF16) for i in range(2)]
        VBa = [tb(f"VBa{i}", [128, 8, 4, 128], BF16) for i in range(2)]
        VBb = [tb(f"VBb{i}", [128, 8, 4, 128], BF16) for i in range(2)]
        qTbA = [tb(f"qTbA{i}", [96, 8, 512], BF16) for i in range(2)]
        qTbB = [tb(f"qTbB{i}", [96, 8, 512], BF16) for i in range(2)]
        RF = [[Res() for _ in range(4)] for _ in range(2)]
        stsem = [k.sem("st0"), k.sem("st1")]

        k.op(POOL, [], [R_const], lambda: nc.gpsimd.memset(onesb[:], 1.0))
        for i in range(3):
            k.op(POOL, [], [RMf2[i]], lambda: nc.gpsimd.memset(Mfull2[i][:], 0.0))
        k.op(POOL, [], [RkmT], lambda: nc.gpsimd.memset(kmT[:], 0.0))

        def issue_x(t):
            i = t % 3
            i4 = t % 4
            k.dma(SP, xsem[i], [(xs[i][:], xc[t * 128:(t + 1) * 128, :], [], [Rxs[i]])])
            k.dma(SP, cksem[i4], [(csk[i4][:], csk_d[t], [], [Rcsk[i4]])])
            if t >= 31:
                k.dma(SP, cqsem[i4], [(csq[i4][:], csq_d[t - 31], [], [Rcsq[i4]])])

        issue_x(0)

        def do_tile(t):
            i = t % 2
            x3 = t % 3
            x4 = t % 4
            sx = t % 3
            quad, j = t // 4, t % 4
            qp = quad % 2
            isq = t >= 31
            nblk = t // 2
            RFj = RF[qp][j]
            xn = xn2[sx]; Rxn = Rxn2[sx]
            xnT = xnT2[sx]; RxnT = RxnT2[sx]
            kA = kA2[sx]; RkA = RkA2[sx]
            kB = kB2[sx]; RkB = RkB2[sx]
            qA = qA2[sx]; RqA = RqA2[sx]
            qB = qB2[sx]; RqB = RqB2[sx]
            Mfull = Mfull2[sx]; RMf = RMf2[sx]
            ckvn = ckvn2[sx]; Rckvn = Rckvn2[sx]
            ckvnT = ckvnT2[sx]; RckvnT = RckvnT2[sx]
            cqn = cqn2[sx]; Rcqn = Rcqn2[sx]
            cqnT = cqnT2[sx]; RcqnT = RcqnT2[sx]
            krr = krr2[sx]; Rkrr = Rkrr2[sx]
            rt = rt2[sx]; Rrt = Rrt2[sx]
            qf = qf2[sx]; Rqf = Rqf2[sx]
            gateS = gateS2[sx]; RgS = RgS2[sx]
            top8 = top82[sx]; Rtop = Rtop2[sx]
            kpart = kpart2[nblk % 2]; Rkpart = Rkpart2[nblk % 2]; Rksum = Rksum2[nblk % 2]
            cS = cS2[sx]; RcS = RcS2[sx]
            cnt = [0]

            def ps_next():
                bnk = 2 * sx + cnt[0] % 2
                cnt[0] += 1
                return psT[bnk], psR[bnk]

            def transposes(src_list, reads, a_, b_, rows, c_):
                return transposes_g(src_list, reads, a_, b_, rows, c_, alloc=ps_next)
            if t + 1 < NT:
                issue_x(t + 1)
            rs, Rr = rstd_of(xs[x3][:], 1024, [Rxs[x3]], junk[:], Rjunk)
            k.op(DVE, [Rxs[x3], Rr], [Rxn], lambda: nc.vector.tensor_scalar(out=xn[:], in0=xs[x3][:], scalar1=rs, scalar2=None, op0=ALU.mult))
            yield
            chk(1)
            pv, Rp = transposes([xn[:, c * 128:(c + 1) * 128] for c in range(8)], [Rxn], None, None, 128, None)
            k.op(ACT, [Rp], [RxnT], lambda: nc.scalar.copy(out=xnT[:].rearrange("p c t -> p (c t)"), in_=pv[:, :]))
            yield
            chk(2)

            def proj(c0, c1):
                p, R = ps_next()
                for c in range(8):
                    k.op(PE, [RxnT, RW0], [R], lambda c=c: nc.tensor.matmul(p[:, 0:c1 - c0], lhsT=xnT[:, c, :], rhs=W0[:, c, c0:c1], start=(c == 0), stop=(c == 7)))
                return p, R
            p_k, R_k = proj(512, 1024)

            chk(3)
            k.op(ACT, [R_k], [RkA], lambda: nc.scalar.copy(out=kA[:], in_=p_k[:, :]))
            yield
            pv, Rp = transposes([kA[:, h * 64:(h + 1) * 64] for h in range(8)], [RkA], None, None, 64, None)
            pv3 = pv[0:64, :].rearrange("p (h t) -> p h t", h=8)
            chk(31)
            k.op(ACT, [Rp], [RFj], lambda: nc.scalar.copy(out=kTbA[qp][:, :, j * 128:(j + 1) * 128], in_=pv3))
            yield
            chk(32)
            ksrc = kTbA[qp][:, :, j * 128:(j + 1) * 128]
            if t % 2 == 0:
                k.op(DVE, [RFj], [Rksum], lambda: nc.vector.reduce_sum(out=kpart[:, 0:8], in_=ksrc, axis=AX.X))
                yield
            else:
                k.op(DVE, [RFj], [Rkpart], lambda: nc.vector.reduce_sum(out=kpart[:, 8:16], in_=ksrc, axis=AX.X))
                yield
                k.op(DVE, [Rkpart, Rksum], [RkmT], lambda: nc.vector.tensor_tensor(out=kmT[:, :, nblk], in0=kpart[:, 0:8], in1=kpart[:, 8:16], op=ALU.add))
                yield
            chk(33)
            chk(4)
            p_v, R_v = proj(1024, 1536)
            k.op(ACT, [R_v], [RFj], lambda: nc.scalar.copy(out=VBa[qp][:, :, j, 0:64], in_=p_v[:, :].rearrange("p (h d) -> p h d", h=8)))
            yield
            k.op(DVE, [R_const], [RFj], lambda: nc.vector.tensor_scalar(out=VBa[qp][:, :, j, 64:128], in0=onesb[:], scalar1=kval[:, t:t + 1], scalar2=None, op0=ALU.mult))
            yield
            k.op(ACT, [R_const], [RFj], lambda: nc.scalar.activation(out=VBb[qp][:, :, j, 64:128], in_=onesb[:], func=AF.Copy, scale=kval[:, t:t + 1]))
            yield

            chk(5)
            p_cp, R_cp = proj(1792, 1952)
            k.op(ACT, [R_cp], [RcS], lambda: nc.scalar.copy(out=cS[:], in_=p_cp[:, 0:160]))
            yield
            p_c = cS; R_c = RcS
            rs2, Rr2 = rstd_of(p_c[:, 0:128], 128, [R_c], junk[:, 0:128], Rjunk)
            k.op(DVE, [R_c, Rr2], [Rckvn], lambda: nc.vector.tensor_scalar(out=ckvn[:], in0=p_c[:, 0:128], scalar1=rs2, scalar2=None, op0=ALU.mult))
            yield
            x1 = p_c[:, 128:144]; x2 = p_c[:, 144:160]
            co = csk[x4][:, 0:16]; si = csk[x4][:, 16:32]
            k.op(DVE, [R_c, Rcsk[x4]], [Rrt], lambda: nc.vector.tensor_tensor(out=rt[:, 0, 0:16], in0=x1, in1=co, op=ALU.mult))
            yield
            k.op(DVE, [R_c, Rcsk[x4]], [Rrt], lambda: nc.vector.tensor_tensor(out=rt[:, 1, 0:16], in0=x2, in1=si, op=ALU.mult))
            yield
            k.op(DVE, [R_c, Rcsk[x4]], [Rrt], lambda: nc.vector.tensor_tensor(out=rt[:, 2, 0:16], in0=x2, in1=co, op=ALU.mult))
            yield
            k.op(DVE, [R_c, Rcsk[x4]], [Rrt], lambda: nc.vector.tensor_tensor(out=rt[:, 3, 0:16], in0=x1, in1=si, op=ALU.mult))
            yield
            k.op(DVE, [Rrt], [Rkrr], lambda: nc.vector.tensor_tensor(out=krr[:, 0:16], in0=rt[:, 0, 0:16], in1=rt[:, 1, 0:16], op=ALU.subtract))
            yield
            k.op(DVE, [Rrt], [Rkrr], lambda: nc.vector.tensor_tensor(out=krr[:, 16:32], in0=rt[:, 2, 0:16], in1=rt[:, 3, 0:16], op=ALU.add))
            yield
            chk(6)
            pv, Rp = transposes([ckvn[:]], [Rckvn], None, None, 128, None)
            k.op(ACT, [Rp], [RckvnT], lambda: nc.scalar.copy(out=ckvnT[:], in_=pv[:, 0:128]))
            yield
            pv, Rp = transposes([krr[:]], [Rkrr], None, None, 32, None)
            k.op(ACT, [Rp], [RFj], lambda: nc.scalar.copy(out=krTb[qp][:, j * 128:(j + 1) * 128], in_=pv[0:32, 0:128]))
            yield
            chk(7)
            for hh in range(2):
                p, R = ps_next()
                k.op(PE, [RckvnT, RWukv], [R], lambda: nc.tensor.matmul(p[:, :], lhsT=ckvnT[:], rhs=Wukv[:, 0, hh * 512:(hh + 1) * 512], start=True, stop=True))
                yield
                p4 = p[:, :].rearrange("p (h two d) -> p h two d", h=4, two=2)
                k.op(ACT, [R], [RkB], lambda: nc.scalar.copy(out=kB[:, hh * 4:hh * 4 + 4, :], in_=p4[:, :, 0, :]))
                yield
                k.op(DVE, [R], [RFj], lambda: nc.vector.tensor_copy(out=VBb[qp][:, hh * 4:hh * 4 + 4, j, 0:64], in_=p4[:, :, 1, :]))
                yield
            pv, Rp = transposes([kB[:, h, :] for h in range(8)], [RkB], None, None, 64, None)
            pv3 = pv[0:64, :].rearrange("p (h t) -> p h t", h=8)
            k.op(ACT, [Rp], [RFj], lambda: nc.scalar.copy(out=kTbB[qp][:, :, j * 128:(j + 1) * 128], in_=pv3))
            yield

            chk(8)
            if isq:
                p_q, R_q = proj(0, 512)
                k.op(ACT, [R_q], [RqA], lambda: nc.scalar.activation(out=qA[:], in_=p_q[:, :], func=AF.Copy, scale=0.125))
                yield
                pv, Rp = transposes([qA[:, h * 64:(h + 1) * 64] for h in range(8)], [RqA], None, None, 64, None)
                pv3 = pv[0:64, :].rearrange("p (h t) -> p h t", h=8)
                k.op(ACT, [Rp], [RFj], lambda: nc.scalar.copy(out=qTbA[qp][0:64, :, j * 128:(j + 1) * 128], in_=pv3))
                yield
                chk(41)
                pg, Rg = ps_next()
                for h in range(8):
                    k.op(PE, [RFj, RkmT], [Rg], lambda h=h: nc.tensor.matmul(pg[:, h * 32:(h + 1) * 32], lhsT=qTbA[qp][0:64, h, j * 128:(j + 1) * 128], rhs=kmT[:, h, :], start=True, stop=True))
                chk(42)
                k.op(DVE, [Rg, R_const], [RgS], lambda: nc.vector.tensor_tensor(out=gateS[:].rearrange("p h n -> p (h n)"), in0=pg[:, 0:256], in1=gbias[:], op=ALU.add))
                yield
                if nblk < 32:
                    k.op(DVE, [], [RgS], lambda: nc.vector.memset(gateS[:, :, nblk:32], NEG))
                chk(43)
                for h in range(8):
                    k.op(DVE, [RgS], [Rtop], lambda h=h: nc.vector.max(out=top8[:, h * 8:(h + 1) * 8], in_=gateS[:, h, :]))
                chk(44)
                for h in range(8):
                    k.op(DVE, [RgS, Rtop], [RMf], lambda h=h: nc.vector.tensor_scalar(out=Mfull[:, h, 64:96], in0=gateS[:, h, :], scalar1=top8[:, h * 8 + 2:h * 8 + 3], scalar2=NEG, op0=ALU.is_lt, op1=ALU.mult))
                k.op(DVE, [], [RMf], lambda: nc.vector.memset(Mfull[:, :, 64 + nblk:65 + nblk], 0.0))
                yield
                chk(45)
                pv, Rp = transposes([Mfull[:, h, :] for h in range(8)], [RMf], None, None, 96, None)
                pv3 = pv[64:96, :].rearrange("p (h t) -> p h t", h=8)
                k.op(ACT, [Rp], [RFj], lambda: nc.scalar.copy(out=qTbA[qp][64:96, :, j * 128:(j + 1) * 128], in_=pv3))
                yield
                chk(46)
                p_cq, R_cq = proj(1536, 1792)
                rs3, Rr3 = rstd_of(p_cq[:, 0:256], 256, [R_cq], junk[:, 0:256], Rjunk)
                k.op(DVE, [R_cq, Rr3], [Rcqn], lambda: nc.vector.tensor_scalar(out=cqn[:], in0=p_cq[:, 0:256], scalar1=rs3, scalar2=None, op0=ALU.mult))
                yield
                pv, Rp = transposes([cqn[:, c * 128:(c + 1) * 128] for c in range(2)], [Rcqn], None, None, 128, None)
                k.op(ACT, [Rp], [RcqnT], lambda: nc.scalar.copy(out=cqnT[:].rearrange("p c t -> p (c t)"), in_=pv[:, 0:256]))
                yield
                chk(47)
                for hh in range(2):
                    p, R = ps_next()
                    for c in range(2):
                        k.op(PE, [RcqnT, RWuq], [R], lambda c=c: nc.tensor.matmul(p[:, 0:384], lhsT=cqnT[:, c, :], rhs=Wuq[:, c, hh * 384:(hh + 1) * 384], start=(c == 0), stop=(c == 1)))
                    p3 = p[:, 0:384].rearrange("p (h d) -> p h d", h=4)
                    hs_ = slice(hh * 4, hh * 4 + 4)
                    k.op(ACT, [R], [RqB], lambda: nc.scalar.activation(out=qB[:, hs_, 0:64], in_=p3[:, :, 0:64], func=AF.Copy, scale=96.0 ** -0.5))
                    chk(48)
                    k.op(ACT, [R], [Rqf], lambda: nc.scalar.copy(out=qf[:], in_=p[:, 0:384]))
                    q3 = qf[:].rearrange("p (h d) -> p h d", h=4)
                    x1 = q3[:, :, 64:80]; x2 = q3[:, :, 80:96]
                    co = csq[x4][:, 0:128].rearrange("p (h f) -> p h f", h=8)[:, hs_, :]
                    si = csq[x4][:, 128:256].rearrange("p (h f) -> p h f", h=8)[:, hs_, :]
                    k.op(DVE, [Rqf, Rcsq[x4]], [Rrt], lambda: nc.vector.tensor_tensor(out=rt[:, :, 0:16], in0=x1, in1=co, op=ALU.mult))
                    k.op(DVE, [Rqf, Rcsq[x4]], [Rrt], lambda: nc.vector.tensor_tensor(out=rt[:, :, 16:32], in0=x2, in1=si, op=ALU.mult))
                    k.op(DVE, [Rqf, Rcsq[x4]], [Rrt], lambda: nc.vector.tensor_tensor(out=rt[:, :, 32:48], in0=x2, in1=co, op=ALU.mult))
                    k.op(DVE, [Rqf, Rcsq[x4]], [Rrt], lambda: nc.vector.tensor_tensor(out=rt[:, :, 48:64], in0=x1, in1=si, op=ALU.mult))
                    k.op(DVE, [Rrt], [RqB], lambda: nc.vector.tensor_tensor(out=qB[:, hs_, 64:80], in0=rt[:, :, 0:16], in1=rt[:, :, 16:32], op=ALU.subtract))
                    k.op(DVE, [Rrt], [RqB], lambda: nc.vector.tensor_tensor(out=qB[:, hs_, 80:96], in0=rt[:, :, 32:48], in1=rt[:, :, 48:64], op=ALU.add))
                chk(49)
                pv, Rp = transposes([qB[:, h, :] for h in range(8)], [RqB], None, None, 96, None)
                pv3 = pv[0:96, :].rearrange("p (h t) -> p h t", h=8)
                k.op(ACT, [Rp], [RFj], lambda: nc.scalar.copy(out=qTbB[qp][:, :, j * 128:(j + 1) * 128], in_=pv3))
                yield
                chk(50)

            if j == 3:
                rd = RF[qp]
                ks = slice(quad * 512, quad * 512 + 512)
                items = [
                    (KTa[:, :, ks].rearrange("h d k -> d h k"), kTbA[qp][:], rd, []),
                    (KTb[:, :, ks].rearrange("h d k -> d h k"), kTbB[qp][:], rd, []),
                    (KRT[:, ks], krTb[qp][:], rd, []),
                    (Va[:, :, quad * 4:quad * 4 + 4, :].rearrange("h p t c -> p h t c"), VBa[qp][:], rd, []),
                    (Vb[:, :, quad * 4:quad * 4 + 4, :].rearrange("h p t c -> p h t c"), VBb[qp][:], rd, []),
                ]
                if quad == 7:
                    items.append((QTa[:, :, 0:128].rearrange("h d k -> d h k"), qTbA[qp][:, :, 384:512], rd, []))
                    items.append((QTb[:, :, 0:128].rearrange("h d k -> d h k"), qTbB[qp][:, :, 384:512], rd, []))
                elif quad >= 8:
                    qs = slice(128 + (quad - 8) * 512, 128 + (quad - 8) * 512 + 512)
                    items.append((QTa[:, :, qs].rearrange("h d k -> d h k"), qTbA[qp][:], rd, []))
                    items.append((QTb[:, :, qs].rearrange("h d k -> d h k"), qTbB[qp][:], rd, []))
                k.dma(POOL, stsem[qp], items)
            if STOP == "P0a" and t == 3:
                raise _Stop()
        try:
            gens = []
            t_next = 0
            OFFSET = int(os.environ.get("KOFF", "2"))
            while gens or t_next < NT:
                if t_next < NT and len(gens) < 3 and (not gens or gens[-1][1] >= OFFSET):
                    gens.append([do_tile(t_next), 0])
                    t_next += 1
                for ge in list(gens):
                    try:
                        next(ge[0])
                        ge[1] += 1
                    except StopIteration:
                        gens.remove(ge)
        except _Stop:
            pass
        k.barrier()
    if STOP in ("P0", "P0a", "P0b"):
        return nc

    with ExitStack() as es:
        def tb(name, shape, dt):
            return es.enter_context(nc.sbuf_tensor(name, list(shape), dt))
        Kt = [tb(f"Kt{i}", [96, 8192], BF16) for i in range(2)]
        Vt = [tb(f"Vt{i}", [128, 64, 128], BF16) for i in range(2)]
        Qt = [tb(f"Qt{i}", [96, NQ], BF16) for i in range(2)]
        NF = [tb(f"NF{i}", [128, 6, 512], BF16) for i in range(2)]
        NFh = [tb(f"NFh{i}", [128, 4, 128], BF16) for i in range(2)]
        Rh = [Res(), Res()]
        hsem = [k.sem("h0"), k.sem("h1")]
        CM = tb("CM", [128, 4, 512], BF16)
        b31 = tb("b31_s", [128, 8], F32)
        Pt = [tb(f"Pt{i}", [128, 1024], BF16) for i in range(4)]
        RPt = [Res() for _ in range(4)]
        rsb = [tb(f"rsb{i}", [64, 512], F32) for i in range(2)]
        yTb = [tb(f"yTb{i}", [64, 512], BF16) for i in range(2)]
        Ry = [Res(), Res()]
        ysem = [k.sem("y0"), k.sem("y1")]
        k.dma(SP, k.sem("ld2"), [(CM[:].rearrange("p a b -> p (a b)"), cm_d[:, :], [], [R_const]),
                                 (b31[:], b31_d[:, :], [], [R_const])])

        def load_head(hh):
            i = hh % 2
            h = hh % 8
            moba = hh < 8
            items = [
                (Kt[i][0:64, :], (KTa if moba else KTb)[h], [], [Rh[i]]),
                (Kt[i][64:96, :], oh_d[:, :] if moba else KRT[:, :], [], [Rh[i]]),
                (Vt[i][:], (Va if moba else Vb)[h], [], [Rh[i]]),
                (Qt[i][:], (QTa if moba else QTb)[h], [], [Rh[i]]),
            ]
            if moba:
                items.append((NF[i][:].rearrange("p a b -> p (a b)"), nf_d[h], [], [Rh[i]]))
                items.append((NFh[i][:].rearrange("p a b -> p (a b)"), nfh_d[h], [], [Rh[i]]))
            k.dma(SP, hsem[i], items)

        load_head(0)
        pcnt = [0]
        gcnt = [0]
        for hh in range(16):
            i = hh % 2
            h = hh % 8
            moba = hh < 8
            if hh + 1 < 16:
                load_head(hh + 1)
            for gi in range(-1, 8):
                if gi < 0:
                    W, qc0, nvis = 128, 0, 32
                else:
                    W, qc0, nvis = 512, 128 + 512 * gi, 32 + 4 * gi + 4
                tabs = {}
                if moba:
                    if gi < 0:
                        for a in range(4):
                            tabs[28 + a] = NFh[i][:, a, :]
                    else:
                        for a in range(6):
                            tabs[30 + 4 * gi + a] = NF[i][:, a, :]
                else:
                    if gi < 0:
                        tabs[31] = CM[:, 0, 0:128]
                    else:
                        for a in range(4):
                            tabs[32 + 4 * gi + a] = CM[:, a, :]
                po, Ro = ps_next(6, 8)
                pend = []

                def emit_pv(kt, pi, first, last):
                    k.op(PE, [Rh[i], RPt[pi]], [Ro], lambda: nc.tensor.matmul(po[:, 0:W], lhsT=Vt[i][:, kt, :], rhs=Pt[pi][:, 0:W], start=first, stop=last))

                for kt in range(nvis):
                    p, R = ps_next(0, 6)
                    tab = tabs.get(kt)
                    k.op(PE, [Rh[i]], [R], lambda: nc.tensor.matmul(p[:, 0:W], lhsT=Kt[i][:, kt * 128:(kt + 1) * 128], rhs=Qt[i][:, qc0:qc0 + W], start=True, stop=(tab is None)))
                    if tab is not None:
                        k.op(PE, [Rh[i], R_const], [R], lambda: nc.tensor.matmul(p[:, 0:W], lhsT=idb[:], rhs=tab, start=False, stop=True))
                    pi = pcnt[0] % 4
                    pcnt[0] += 1
                    if moba and tab is None:
                        k.op(ACT, [R, R_const], [RPt[pi]], lambda: nc.scalar.activation(out=Pt[pi][:, 0:W], in_=p[:, 0:W], func=AF.Exp, bias=b31[:, h:h + 1]))
                    else:
                        k.op(ACT, [R], [RPt[pi]], lambda: nc.scalar.activation(out=Pt[pi][:, 0:W], in_=p[:, 0:W], func=AF.Exp))
                    pend.append((kt, pi))
                    if len(pend) > 2:
                        kt0, pi0 = pend.pop(0)
                        emit_pv(kt0, pi0, kt0 == 0, False)
                while pend:
                    kt0, pi0 = pend.pop(0)
                    emit_pv(kt0, pi0, kt0 == 0, kt0 == nvis - 1)
                yi = gcnt[0] % 2
                gcnt[0] += 1
                k.op(DVE, [Ro], [Ry[yi]], lambda: nc.vector.tensor_scalar(out=rsb[yi][:, 0:W], in0=po[64:128, 0:W], scalar1=1e-30, scalar2=None, op0=ALU.max))
                k.op(DVE, [Ry[yi]], [Ry[yi]], lambda: nc.vector.reciprocal(out=rsb[yi][:, 0:W], in_=rsb[yi][:, 0:W]))
                k.op(DVE, [Ro, Ry[yi]], [Ry[yi]], lambda: nc.vector.tensor_tensor(out=yTb[yi][:, 0:W], in0=po[0:64, 0:W], in1=rsb[yi][:, 0:W], op=ALU.mult))
                k.dma(POOL, ysem[yi], [(YT[hh, :, qc0:qc0 + W], yTb[yi][:, 0:W], [Ry[yi]], [])])
        k.barrier()
    if STOP == "P1":
        return nc

    with ExitStack() as es:
        def tb(name, shape, dt):
            return es.enter_context(nc.sbuf_tensor(name, list(shape), dt))
        Wg = tb("Wg", [128, 8, 2048], BF16); RWg = Res()
        Wm = tb("Wm", [64, 8, 1024], BF16); RWm = Res()
        Wl = tb("Wl", [64, 8, 1024], BF16); RWl = Res()
        Wo = tb("Wo", [128, 8, 1024], BF16); RWo = Res()
        gA = tb("gA2", [128, 8], F32)
        bg = tb("bg_s", [128, 16], F32)
        k.dma(SP, k.sem("ld3"), [(gA[:], gA_d[:, :], [], [R_const]), (bg[:], bg_d[:, :], [], [R_const])])
        with ExitStack() as es2:
            stage = [es2.enter_context(nc.sbuf_tensor(f"wstb{i}", [128, 2048], F32)) for i in range(2)]
            Rstage = [Res(), Res()]
            load_weight(Wg, RWg, w_in, 8, 2048, lambda c: gA[:, c:c + 1], stage, Rstage, wsem, c0=1952)
            load_weight(Wm, RWm, w_bm, 8, 1024, None, stage, Rstage, wsem, rows=64)
            load_weight(Wl, RWl, w_bl, 8, 1024, None, stage, Rstage, wsem, rows=64)
            load_weight(Wo, RWo, w_out, 8, 1024, None, stage, Rstage, wsem)
            k.barrier()
        WG2 = 256
        xs = [tb(f"xsB{i}", [128, 1024], F32) for i in range(4)]; Rxs = [Res() for _ in range(4)]
        xsem = [k.sem(f"xb{i}") for i in range(4)]
        hsem2 = [k.sem(f"hs{i}") for i in range(4)]
        junk = tb("junkB", [128, 1024], BF16); Rjunk = Res()
        xn2 = [tb(f"xnB{i}", [128, 1024], BF16) for i in range(2)]; Rxn2 = [Res(), Res()]
        xnT2 = [tb(f"xnTB{i}", [128, 8, WG2], BF16) for i in range(2)]; RxnT2 = [Res(), Res()]
        ysb2 = [tb(f"ysb{i}", [64, 16, WG2], BF16) for i in range(2)]; Rysb2 = [Res(), Res()]
        ysem2 = [k.sem("ysb0"), k.sem("ysb1")]
        gT2 = [tb(f"gT{i}", [128, 16, WG2], F32) for i in range(2)]; RgT2 = [Res(), Res()]
        t1 = [tb(f"t1{i}", [128, WG2], F32) for i in range(4)]
        t2 = [tb(f"t2{i}", [128, WG2], F32) for i in range(4)]
        Rt = [Res() for _ in range(4)]
        mixT2 = [tb(f"mixT{i}", [128, 8, WG2], BF16) for i in range(2)]; Rmix2 = [Res(), Res()]
        xcnt = [0]

        def do_group2(idx, gi, W, qc0, r0):
            par = idx % 2
            xn, Rxn, xnT, RxnT = xn2[par], Rxn2[par], xnT2[par], RxnT2[par]
            ysb, Rysb, gT, RgT, mixT, Rmix = ysb2[par], Rysb2[par], gT2[par], RgT2[par], mixT2[par], Rmix2[par]
            ntt = W // 128
            sl = []
            for tt in range(ntt):
                sl.append(xcnt[0] % 4)
                xcnt[0] += 1
            k.dma(SP, ysem2[par], [(ysb[:, :, 0:W], YT[:, :, qc0:qc0 + W].rearrange("h d k -> d h k"), [], [Rysb])])
            for tt in range(ntt):
                a_ = sl[tt]
                k.dma(SP, xsem[a_], [(xs[a_][:], xc[r0 + tt * 128:r0 + (tt + 1) * 128, :], [], [Rxs[a_]])])
            yield
            for tt in range(ntt):
                a_ = sl[tt]
                rs, Rr = rstd_of(xs[a_][:], 1024, [Rxs[a_]], junk[:], Rjunk)
                k.op(DVE, [Rxs[a_], Rr], [Rxn], lambda: nc.vector.tensor_scalar(out=xn[:], in0=xs[a_][:], scalar1=rs, scalar2=None, op0=ALU.mult))
                pv, Rp = transposes([xn[:, c * 128:(c + 1) * 128] for c in range(8)], [Rxn], None, None, 128, None)
                k.op(ACT, [Rp], [RxnT], lambda: nc.scalar.copy(out=xnT[:, :, tt * 128:(tt + 1) * 128], in_=pv[:, :].rearrange("p (c t) -> p c t", c=8)))
                yield
            for ct in range(16):
                p, R = ps_next()
                for c in range(8):
                    k.op(PE, [RxnT, RWg], [R], lambda c=c: nc.tensor.matmul(p[:, 0:W], lhsT=Wg[:, c, ct * 128:(ct + 1) * 128], rhs=xnT[:, c, 0:W], start=(c == 0), stop=(c == 7)))
                k.op(ACT, [R, R_const], [RgT], lambda: nc.scalar.activation(out=gT[:, ct, 0:W], in_=p[:, 0:W], func=AF.Sigmoid, bias=bg[:, ct:ct + 1]))
                yield
            for ct in range(8):
                pa, Ra = ps_next()
                for h in range(8):
                    k.op(PE, [Rysb, RWm], [Ra], lambda h=h: nc.tensor.matmul(pa[:, 0:W], lhsT=Wm[:, h, ct * 128:(ct + 1) * 128], rhs=ysb[:, h, 0:W], start=(h == 0), stop=(h == 7)))
                pb, Rb = ps_next()
                for h in range(8):
                    k.op(PE, [Rysb, RWl], [Rb], lambda h=h: nc.tensor.matmul(pb[:, 0:W], lhsT=Wl[:, h, ct * 128:(ct + 1) * 128], rhs=ysb[:, 8 + h, 0:W], start=(h == 0), stop=(h == 7)))
                ti = ct % 4
                k.op(DVE, [Ra, RgT], [Rt[ti]], lambda: nc.vector.tensor_tensor(out=t1[ti][:, 0:W], in0=pa[:, 0:W], in1=gT[:, ct, 0:W], op=ALU.mult))
                k.op(DVE, [Rb, RgT], [Rt[ti]], lambda: nc.vector.tensor_tensor(out=t2[ti][:, 0:W], in0=pb[:, 0:W], in1=gT[:, 8 + ct, 0:W], op=ALU.mult))
                k.op(DVE, [Rt[ti]], [Rmix], lambda: nc.vector.tensor_tensor(out=mixT[:, ct, 0:W], in0=t1[ti][:, 0:W], in1=t2[ti][:, 0:W], op=ALU.add))
                yield
            for tt in range(ntt):
                a_ = sl[tt]
                for hf in range(2):
                    p, R = ps_next()
                    for c in range(8):
                        k.op(PE, [Rmix, RWo], [R], lambda c=c: nc.tensor.matmul(p[:, :], lhsT=mixT[:, c, tt * 128:(tt + 1) * 128], rhs=Wo[:, c, hf * 512:(hf + 1) * 512], start=(c == 0), stop=(c == 7)))
                    k.op(DVE, [R, Rxs[a_]], [Rxs[a_]], lambda: nc.vector.tensor_tensor(out=xs[a_][:, hf * 512:(hf + 1) * 512], in0=p[:, :], in1=xs[a_][:, hf * 512:(hf + 1) * 512], op=ALU.add))
                k.dma(SP, hsem2[a_], [(H1[qc0 + tt * 128:qc0 + (tt + 1) * 128, :], xs[a_][:], [Rxs[a_]], [])])
                yield

        groups2 = [(-1, 128, 0, 31 * 128)] + [(g, WG2, 128 + WG2 * g, 4096 + WG2 * g) for g in range(4096 // WG2)]
        run_pipelined([do_group2(n_, *g_) for n_, g_ in enumerate(groups2)], int(os.environ.get("KOFF2", "2")))
        k.barrier()
    if STOP == "P2":
        return nc

    with ExitStack() as es:
        def tb(name, shape, dt):
            return es.enter_context(nc.sbuf_tensor(name, list(shape), dt))
        Wu = tb("Wu", [128, 8, 5632], BF16); RWu = Res()
        Wd = tb("Wd", [128, 22, 1024], BF16); RWd = Res()
        gF = tb("gF_s", [128, 8], F32)
        cw = tb("cw_s", [128, 44, 3], F32)
        cb = tb("cb_s", [128, 44], F32)
        gO = tb("gO_s", [128, 1024], F32)
        HB = tb("HB", [128, 44, 2], F32); RHB = Res()
        k.dma(SP, k.sem("ld4"), [(gF[:], gF_d[:, :], [], [R_const]), (cw[:].rearrange("p a b -> p (a b)"), cw_d[:, :], [], [R_const]),
                                 (cb[:], cb_d[:, :], [], [R_const]), (gO[:], gO_d[:, :], [], [R_const])])
        with ExitStack() as es2:
            stage = [es2.enter_context(nc.sbuf_tensor(f"wstc{i}", [128, 2048], F32)) for i in range(2)]
            Rstage = [Res(), Res()]
            load_weight(Wu, RWu, w_up, 8, 5632, lambda c: gF[:, c:c + 1], stage, Rstage, wsem)
            load_weight(Wd, RWd, w_dn, 22, 1024, None, stage, Rstage, wsem)
            k.barrier()
        WG = 256
        hs = [tb(f"hs{i}", [128, 1024], F32) for i in range(4)]; Rhs = [Res() for _ in range(4)]
        lsem = [k.sem(f"l{i}") for i in range(4)]
        osem = [k.sem(f"o{i}") for i in range(4)]
        junk = tb("junkC", [128, 1024], BF16); Rjunk = Res()
        hn2 = [tb(f"hnC{i}", [128, 1024], BF16) for i in range(2)]; Rhn2 = [Res(), Res()]
        hnT2 = [tb(f"hnT{i}", [128, 8, WG], BF16) for i in range(2)]; RhnT2 = [Res(), Res()]
        actT2 = [tb(f"actT{i}", [128, 22, WG], BF16) for i in range(2)]; RactT2 = [Res(), Res()]
        upw = [tb(f"upw{i}", [128, 2 + WG], F32) for i in range(6)]; Rupw = [Res() for _ in range(6)]
        acc = [tb(f"acc{i}", [128, WG], F32) for i in range(6)]; Racc = [Res() for _ in range(6)]
        sg = [tb(f"sg{i}", [128, WG], F32) for i in range(3)]; Rsg = [Res() for _ in range(3)]
        uc = [0]
        hc = [0]
        groups = [(-1, 128, 0)] + [(g, WG, 128 + WG * g) for g in range(4096 // WG)]

        def do_group3(idx, gi, W, qc0):
            par = idx % 2
            hn, Rhn, hnT, RhnT, actT, RactT = hn2[par], Rhn2[par], hnT2[par], RhnT2[par], actT2[par], RactT2[par]
            ntt = W // 128
            hidx = []
            for tt in range(ntt):
                a = hc[0] % 4
                hc[0] += 1
                hidx.append(a)
                k.dma(SP, lsem[a], [(hs[a][:], H1[qc0 + tt * 128:qc0 + (tt + 1) * 128, :], [], [Rhs[a]])])
            yield
            for tt in range(ntt):
                a = hidx[tt]
                rs, Rr = rstd_of(hs[a][:], 1024, [Rhs[a]], junk[:], Rjunk)
                k.op(DVE, [Rhs[a], Rr], [Rhn], lambda: nc.vector.tensor_scalar(out=hn[:], in0=hs[a][:], scalar1=rs, scalar2=None, op0=ALU.mult))
                pv, Rp = transposes([hn[:, c * 128:(c + 1) * 128] for c in range(8)], [Rhn], None, None, 128, None)
                k.op(ACT, [Rp], [RhnT], lambda: nc.scalar.copy(out=hnT[:, :, tt * 128:(tt + 1) * 128], in_=pv[:, :].rearrange("p (c t) -> p c t", c=8)))
                yield
            for c in range(22):
                accs = []
                for part, ch in ((0, c), (1, 22 + c)):
                    p, R = ps_next()
                    col = ch * 128
                    for d in range(8):
                        k.op(PE, [RhnT, RWu], [R], lambda d=d: nc.tensor.matmul(p[:, 0:W], lhsT=Wu[:, d, col:col + 128], rhs=hnT[:, d, 0:W], start=(d == 0), stop=(d == 7)))
                    if gi < 0:
                        k.op(ACT, [R, R_const], [RHB], lambda: nc.scalar.activation(out=HB[:, ch, :], in_=p[:, W - 2:W], func=AF.Copy, scale=hflag[:, 0:1]))
                        continue
                    u = uc[0] % 6
                    uc[0] += 1
                    k.op(ACT, [R], [Rupw[u]], lambda: nc.scalar.copy(out=upw[u][:, 2:2 + W], in_=p[:, 0:W]))
                    k.op(DVE, [RHB], [Rupw[u]], lambda: nc.vector.tensor_copy(out=upw[u][:, 0:2], in_=HB[:, ch, :]))
                    k.op(ACT, [R, R_const], [Racc[u]], lambda: nc.scalar.activation(out=acc[u][:, 0:W], in_=p[:, 0:W], func=AF.Identity, scale=cw[:, ch, 2:3], bias=cb[:, ch:ch + 1]))
                    k.op(DVE, [Rupw[u], R_const, Racc[u]], [Racc[u]], lambda: nc.vector.scalar_tensor_tensor(out=acc[u][:, 0:W], in0=upw[u][:, 1:1 + W], scalar=cw[:, ch, 1:2], in1=acc[u][:, 0:W], op0=ALU.mult, op1=ALU.add))
                    k.op(DVE, [Rupw[u], R_const, Racc[u]], [Racc[u]], lambda: nc.vector.scalar_tensor_tensor(out=acc[u][:, 0:W], in0=upw[u][:, 0:W], scalar=cw[:, ch, 0:1], in1=acc[u][:, 0:W], op0=ALU.mult, op1=ALU.add))
                    k.op(POOL, [Rupw[u]], [RHB], lambda: nc.gpsimd.tensor_copy(out=HB[:, ch, :], in_=upw[u][:, W:W + 2]))
                    accs.append(u)
                if gi >= 0:
                    ug, uv = accs
                    si = c % 3
                    k.op(ACT, [Racc[ug]], [Rsg[si]], lambda: nc.scalar.activation(out=sg[si][:, 0:W], in_=acc[ug][:, 0:W], func=AF.Silu))
                    k.op(POOL, [Rsg[si], Racc[uv]], [RactT], lambda: nc.gpsimd.tensor_tensor(out=actT[:, c, 0:W], in0=sg[si][:, 0:W], in1=acc[uv][:, 0:W], op=ALU.mult))
                yield
            if gi >= 0:
                for tt in range(ntt):
                    a = hidx[tt]
                    for hf in range(2):
                        p, R = ps_next()
                        for c in range(22):
                            k.op(PE, [RactT, RWd], [R], lambda c=c: nc.tensor.matmul(p[:, :], lhsT=actT[:, c, tt * 128:(tt + 1) * 128], rhs=Wd[:, c, hf * 512:(hf + 1) * 512], start=(c == 0), stop=(c == 21)))
                        k.op(DVE, [R, Rhs[a]], [Rhs[a]], lambda: nc.vector.tensor_tensor(out=hs[a][:, hf * 512:(hf + 1) * 512], in0=p[:, :], in1=hs[a][:, hf * 512:(hf + 1) * 512], op=ALU.add))
                    rs, Rr = rstd_of(hs[a][:], 1024, [Rhs[a]], junk[:], Rjunk)
                    k.op(DVE, [Rhs[a], Rr, R_const], [Rhs[a]], lambda: nc.vector.scalar_tensor_tensor(out=hs[a][:], in0=hs[a][:], scalar=rs, in1=gO[:], op0=ALU.mult, op1=ALU.mult))
                    row = qc0 - 128 + tt * 128
                    k.dma(SP, osem[a], [(out_d[row:row + 128, :], hs[a][:], [Rhs[a]], [])])
                    yield

        run_pipelined([do_group3(n_, *g_) for n_, g_ in enumerate(groups)], int(os.environ.get("KOFF3", "3")))
        k.barrier()
    return nc


def _t5_bucket(rel):
    n = np.maximum(rel, 0)
    nf = np.maximum(n, 1).astype(np.float32)
    large = 16 + (np.log(nf / np.float32(16)) / np.float32(math.log(128 / 16)) * np.float32(16)).astype(np.int32)
    large = np.minimum(large, 31)
    return np.where(n < 16, n, large)


def _tables():
    kk = np.arange(128)[:, None]
    qq = np.arange(256)[None, :]
    T0, T1 = [], []
    for i in range(2):
        kb = i * 128 + kk
        rel = qq - kb
        T0.append(np.where(rel >= 0, _t5_bucket(rel), 32))
        T1.append(_t5_bucket(qq + 256 - kb))
    far = np.full((128, 256), 31)
    msk = np.full((128, 256), 32)
    nf = [np.concatenate([T1[0], far], 1), np.concatenate([T1[1], far], 1),
          np.concatenate([T0[0], T1[0]], 1), np.concatenate([T0[1], T1[1]], 1),
          np.concatenate([msk, T0[0]], 1), np.concatenate([msk, T0[1]], 1)]
    nfh = [T1[0][:, 128:], T1[1][:, 128:], T0[0][:, 128:], T0[1][:, 128:]]
    return np.stack(nf, 1), np.stack(nfh, 1)


_NC_CACHE = {}


def kernel(x, norm_attn_g, w_in, b_gate, q_norm_g, w_uq, kv_norm_g, w_ukv, rel_bias,
           w_branch_moba, w_branch_mla, w_out, norm_ffn_g, w_up, conv_w, conv_b, w_down,
           norm_final_g):
    f32 = np.float32
    bf = ml_dtypes.bfloat16
    x = np.asarray(x, f32)
    c = lambda a: np.ascontiguousarray(np.asarray(a, f32))
    nfi, nfhi = _tables()
    rb_ext = np.concatenate([np.asarray(rel_bias, f32), np.full((1, 8), NEG, f32)], 0)
    nf = np.ascontiguousarray(np.transpose(rb_ext[nfi], (3, 0, 1, 2)).reshape(8, 128, 6 * 512)).astype(bf)
    nfh = np.ascontiguousarray(np.transpose(rb_ext[nfhi], (3, 0, 1, 2)).reshape(8, 128, 4 * 128)).astype(bf)
    b31 = np.ascontiguousarray(np.broadcast_to(np.asarray(rel_bias, f32)[31][None, :], (128, 8)))
    kk = np.arange(128)[:, None]
    cm = np.stack([np.where(i * 128 + kk <= np.arange(512)[None, :], 0.0, NEG) for i in range(4)], 1)
    cm = np.ascontiguousarray(cm.reshape(128, 2048).astype(f32)).astype(bf)
    oh = (np.arange(8192)[None, :] // 256 == np.arange(32)[:, None]).astype(f32).astype(bf)
    idb = np.eye(128, dtype=f32).astype(bf)
    inv_freq = (np.float32(10000.0) ** (-np.arange(0, 32, 2, dtype=f32) / np.float32(32))).astype(f32)
    common = {
        "w_in": c(w_in[0]), "w_uq": c(w_uq[0]), "w_ukv": c(w_ukv[0]), "w_bm": c(w_branch_moba[0]),
        "w_bl": c(w_branch_mla[0]), "w_out": c(w_out[0]), "w_up": c(w_up[0]), "w_dn": c(w_down[0]),
        "gA": c(np.asarray(norm_attn_g, f32)[0].reshape(8, 128).T),
        "gF": c(np.asarray(norm_ffn_g, f32)[0].reshape(8, 128).T),
        "gQ": c(np.asarray(q_norm_g, f32)[0].reshape(2, 128).T),
        "gKV": c(np.asarray(kv_norm_g, f32)[0].reshape(1, 128).T),
        "bg": c(np.asarray(b_gate, f32)[0].reshape(16, 128).T),
        "cw": c(np.transpose(np.asarray(conv_w, f32)[0].reshape(3, 44, 128), (2, 1, 0)).reshape(128, 132)),
        "cb": c(np.asarray(conv_b, f32)[0].reshape(44, 128).T),
        "gO": c(np.broadcast_to(np.asarray(norm_final_g, f32)[None, :], (128, 1024))),
        "b31": b31, "oh": oh, "nf": nf, "nfh": nfh, "cm": cm, "idb": idb,
    }
    in_maps = []
    for core in range(8):
        b, half = core // 2, core % 2
        xcat = np.zeros((8192, 1024), f32)
        if half == 1:
            xcat[:4096] = x[b, :4096]
        xcat[4096:] = x[b, half * 4096:(half + 1) * 4096]
        kval = np.ones((128, 64), f32)
        kval[:, :32] = float(half)
        gbias = np.zeros((128, 8, 32), f32)
        if half == 0:
            gbias[:, :, :16] = NEG
        pos = (np.arange(8192) if half == 1 else np.concatenate([np.arange(4096), np.arange(4096)])).astype(f32)
        ang = pos[:, None] * inv_freq[None, :]
        cs, sn = np.cos(ang).astype(f32), np.sin(ang).astype(f32)
        csk = np.concatenate([cs, sn], 1).reshape(64, 128, 32)
        s = np.float32(96.0 ** -0.5)
        csq = np.concatenate([np.tile(cs[31 * 128:], (1, 8)) * s, np.tile(sn[31 * 128:], (1, 8)) * s], 1).reshape(NQT, 128, 256)
        m = dict(common)
        m.update({"xc": xcat, "kval": kval, "gbias": c(gbias.reshape(128, 256)),
                  "hflag": np.full((128, 1), float(half), f32), "csk": c(csk), "csq": c(csq)})
        in_maps.append(m)
    if "nc" not in _NC_CACHE:
        _NC_CACHE["nc"] = build_program()
    nc = _NC_CACHE["nc"]
    ncores = int(os.environ.get("KCORES", "8"))
    res = run_bass_kernel_spmd(nc, in_maps[:ncores], core_ids=list(range(ncores)))
    out = np.empty((4, 8192, 1024), f32)
    for core in range(ncores):
        b, half = core // 2, core % 2
        out[b, half * 4096:(half + 1) * 4096] = res.results[core]["out"]
    if DEBUG:
        kernel.last = res
    return out
```

```python
import math
import os
from contextlib import ExitStack

import ml_dtypes
import numpy as np

import concourse.bass as bass
import concourse.mybir as mybir
from concourse.bass_utils import run_bass_kernel_spmd

F32 = mybir.dt.float32
BF16 = mybir.dt.bfloat16
AF = mybir.ActivationFunctionType
ALU = mybir.AluOpType
AX = mybir.AxisListType

NEG = -30000.0
EPS = 1e-6
NT = 64
NQT = 33
NQ = NQT * 128
DEBUG = False
STOP = None
STRICT = int(os.environ.get('KSTRICT', '0'))


class Sem:
    def __init__(self, nc, name):
        self.h = nc.alloc_semaphore(name)
        self.v = 0


class Res:
    __slots__ = ("w", "r", "excl")

    def __init__(self, excl=False):
        self.w = None
        self.r = {}
        self.excl = excl


class Eng:
    def __init__(self, eng, sem):
        self.e = eng
        self.sem = sem
        self.seen = {}

    def wait(self, sem, val):
        if self.seen.get(id(sem), 0) >= val:
            return
        self.e.wait_ge(sem.h, val)
        self.seen[id(sem)] = val


class K:
    def __init__(self, nc):
        self.nc = nc
        self.sems = []
        self.pe = Eng(nc.tensor, self.sem("pe"))
        self.act = Eng(nc.scalar, self.sem("act"))
        self.dve = Eng(nc.vector, self.sem("dve"))
        self.pool = Eng(nc.gpsimd, self.sem("pool"))
        self.sp = Eng(nc.sync, self.sem("sp"))
        self.engs = [self.pe, self.act, self.dve, self.pool, self.sp]

    def sem(self, name):
        s = Sem(self.nc, name)
        self.sems.append(s)
        return s

    def _deps(self, eng, reads, writes):
        for r in reads:
            if r.w is not None:
                eng.wait(*r.w)
        strict = STRICT == 1 or (STRICT == 2 and eng is not self.pe)
        for w in writes:
            if w.w is not None and (strict or w.w[0] is not eng.sem):
                eng.wait(*w.w)
            for s, v in w.r.values():
                if strict or s is not eng.sem:
                    eng.wait(s, v)

    def _commit(self, ev, reads, writes):
        for w in writes:
            w.w = ev
            w.r = {}
        for r in reads:
            if r not in writes:
                r.r[id(ev[0])] = ev

    def op(self, eng, reads, writes, fn):
        ex = [r for r in reads if r.excl and r not in writes]
        if ex:
            writes = list(writes) + ex
        self._deps(eng, reads, writes)
        ins = fn()
        eng.sem.v += 1
        ins.then_inc(eng.sem.h, 1)
        self._commit((eng.sem, eng.sem.v), reads, writes)

    def dma(self, q, sem, items):
        for o, i, reads, writes in items:
            self._deps(q, reads, writes)
        for o, i, reads, writes in items:
            q.e.dma_start(out=o, in_=i).then_inc(sem.h, 16)
            sem.v += 16
        ev = (sem, sem.v)
        for o, i, reads, writes in items:
            self._commit(ev, reads, writes)

    def barrier(self):
        for e in self.engs:
            for s in self.sems:
                if s.v > 0:
                    e.wait(s, s.v)


class _Stop(Exception):
    pass


def chk(n):
    if STOP in ("P0a", "P0b") and int(os.environ.get("KSTEP", "99")) == n:
        raise _Stop()


def run_pipelined(gen_list, offset):
    gens = []
    nxt = 0
    while gens or nxt < len(gen_list):
        if nxt < len(gen_list) and len(gens) < 2 and (not gens or gens[-1][1] >= offset):
            gens.append([gen_list[nxt], 0])
            nxt += 1
        for ge in list(gens):
            try:
                next(ge[0])
                ge[1] += 1
            except StopIteration:
                gens.remove(ge)


def build_program():
    nc = bass.Bass("TRN2", target_bir_lowering=False)
    k = K(nc)
    PE, ACT, DVE, POOL, SP = k.pe, k.act, k.dve, k.pool, k.sp

    def din(name, shape, dt=F32):
        return nc.dram_tensor(name, list(shape), dt, kind="ExternalInput").ap()

    def dscr(name, shape, dt):
        kind = "ExternalOutput" if DEBUG else "Internal"
        return nc.dram_tensor(name, list(shape), dt, kind=kind).ap()

    xc = din("xc", [8192, 1024])
    w_in = din("w_in", [1024, 4000])
    w_uq = din("w_uq", [256, 768])
    w_ukv = din("w_ukv", [128, 1024])
    w_bm = din("w_bm", [512, 1024])
    w_bl = din("w_bl", [512, 1024])
    w_out = din("w_out", [1024, 1024])
    w_up = din("w_up", [1024, 5632])
    w_dn = din("w_dn", [2816, 1024])
    gA_d = din("gA", [128, 8])
    gF_d = din("gF", [128, 8])
    gQ_d = din("gQ", [128, 2])
    gKV_d = din("gKV", [128, 1])
    bg_d = din("bg", [128, 16])
    cw_d = din("cw", [128, 44 * 3])
    cb_d = din("cb", [128, 44])
    gO_d = din("gO", [128, 1024])
    kval_d = din("kval", [128, 64])
    gbias_d = din("gbias", [128, 256])
    hflag_d = din("hflag", [128, 1])
    b31_d = din("b31", [128, 8])
    csk_d = din("csk", [64, 128, 32])
    csq_d = din("csq", [NQT, 128, 256])
    oh_d = din("oh", [32, 8192], BF16)
    nf_d = din("nf", [8, 128, 6 * 512], BF16)
    nfh_d = din("nfh", [8, 128, 4 * 128], BF16)
    cm_d = din("cm", [128, 4 * 512], BF16)
    idb_d = din("idb", [128, 128], BF16)
    out_d = nc.dram_tensor("out", [4096, 1024], F32, kind="ExternalOutput").ap()

    KTa = dscr("KTa", [8, 64, 8192], BF16)
    KTb = dscr("KTb", [8, 64, 8192], BF16)
    KRT = dscr("KRT", [32, 8192], BF16)
    Va = dscr("Va", [8, 128, 64, 128], BF16)
    Vb = dscr("Vb", [8, 128, 64, 128], BF16)
    QTa = dscr("QTa", [8, 96, NQ], BF16)
    QTb = dscr("QTb", [8, 96, NQ], BF16)
    YT = dscr("YT", [16, 64, NQ], BF16)
    H1 = dscr("H1", [NQ, 1024], F32)

    psBig = nc.alloc_psum_tensor("psbig", [128, 4096], F32)
    psT = [psBig[:, i * 512:(i + 1) * 512] for i in range(8)]
    psR = [Res(excl=True) for _ in range(8)]
    pctr = [0]

    def ps_next(lo=0, hi=8):
        i = lo + pctr[0] % (hi - lo)
        pctr[0] += 1
        return psT[i], psR[i]

    def psbf(p):
        return p[:, :].bitcast(BF16)

    def sb(name, shape, dt):
        return nc.alloc_sbuf_tensor(name, list(shape), dt)

    idb = sb("idb_s", [128, 128], BF16)
    epsb = sb("epsb", [128, 1], F32)
    stat = sb("stat", [128, 16], F32)
    kval = sb("kval_s", [128, 64], F32)
    hflag = sb("hflag_s", [128, 1], F32)
    R_const = Res()
    R_stat = [Res() for _ in range(4)]
    sc = [0]

    k.dma(SP, k.sem("ld0"), [
        (idb[:], idb_d[:, :], [], [R_const]),
        (kval[:], kval_d[:, :], [], [R_const]),
        (hflag[:], hflag_d[:, :], [], [R_const]),
    ])
    k.op(POOL, [], [R_const], lambda: nc.gpsimd.memset(epsb[:], EPS))
    if STOP == "W0":
        k.barrier()
        return nc

    def rstd_of(src_ap, n, reads, junk, Rjunk):
        i = sc[0] % 4
        sc[0] += 1
        R = R_stat[i]
        ss = stat[:, 4 * i:4 * i + 1]
        sd = stat[:, 4 * i + 1:4 * i + 2]
        rs = stat[:, 4 * i + 2:4 * i + 3]
        k.op(ACT, reads, [R, Rjunk], lambda: nc.scalar.activation(out=junk, in_=src_ap, func=AF.Square, accum_out=ss))
        k.op(ACT, [R, R_const], [R], lambda: nc.scalar.activation(out=sd, in_=ss, func=AF.Sqrt, scale=1.0 / n, bias=epsb[:, 0:1]))
        k.op(DVE, [R], [R], lambda: nc.vector.reciprocal(out=rs, in_=sd))
        return rs, R

    def transposes(src_list, reads, dst_ap_fn, dst_writes, rows, copy_eng, alloc=None):
        p, R = (alloc or ps_next)()
        pv = psbf(p)
        n = len(src_list)
        for j, s in enumerate(src_list):
            k.op(PE, reads + [R_const], [R], lambda s=s, j=j: nc.tensor.transpose(out=pv[0:rows, j * 128:(j + 1) * 128], in_=s, identity=idb[:]))
        return pv, R

    def load_weight(dst, Rdst, src, nchunks, cols, scale_ap_fn, stage, Rstage, ssem, c0=0, rows=128):
        for c in range(nchunks):
            for off in range(0, cols, 2048):
                w = min(2048, cols - off)
                i = load_weight.ctr % 2
                load_weight.ctr += 1
                k.dma(SP, ssem[i], [(stage[i][0:rows, 0:w], src[c * rows:(c + 1) * rows, c0 + off:c0 + off + w], [], [Rstage[i]])])
                sap = scale_ap_fn(c) if scale_ap_fn else None
                if sap is not None:
                    if load_weight.ctr % 2:
                        k.op(ACT, [Rstage[i], R_const], [Rdst], lambda i=i, c=c, off=off, w=w, sap=sap: nc.scalar.activation(out=dst[0:rows, c, off:off + w], in_=stage[i][0:rows, 0:w], func=AF.Copy, scale=sap))
                    else:
                        k.op(DVE, [Rstage[i], R_const], [Rdst], lambda i=i, c=c, off=off, w=w, sap=sap: nc.vector.tensor_scalar(out=dst[0:rows, c, off:off + w], in0=stage[i][0:rows, 0:w], scalar1=sap, scalar2=None, op0=ALU.mult))
                else:
                    if load_weight.ctr % 2:
                        k.op(ACT, [Rstage[i]], [Rdst], lambda i=i, c=c, off=off, w=w: nc.scalar.copy(out=dst[0:rows, c, off:off + w], in_=stage[i][0:rows, 0:w]))
                    else:
                        k.op(DVE, [Rstage[i]], [Rdst], lambda i=i, c=c, off=off, w=w: nc.vector.tensor_copy(out=dst[0:rows, c, off:off + w], in_=stage[i][0:rows, 0:w]))
    load_weight.ctr = 0
    transposes_g = transposes
    wsem = [k.sem("ws0"), k.sem("ws1")]

    with ExitStack() as es:
        def tb(name, shape, dt):
            return es.enter_context(nc.sbuf_tensor(name, list(shape), dt))

        W0 = tb("W0", [128, 8, 1952], BF16); RW0 = Res()
        Wuq = tb("Wuq", [128, 2, 768], BF16); RWuq = Res()
        Wukv = tb("Wukv", [128, 1, 1024], BF16); RWukv = Res()
        gA = tb("gA_s", [128, 8], F32)
        gQ = tb("gQ_s", [128, 2], F32)
        gKV = tb("gKV_s", [128, 1], F32)
        gbias = tb("gbias_s", [128, 256], F32)
        k.dma(SP, k.sem("ld1"), [
            (gA[:], gA_d[:, :], [], [R_const]), (gQ[:], gQ_d[:, :], [], [R_const]),
            (gKV[:], gKV_d[:, :], [], [R_const]), (gbias[:], gbias_d[:, :], [], [R_const]),
        ])
        with ExitStack() as es0:
            stage = [es0.enter_context(nc.sbuf_tensor(f"wst{i}", [128, 2048], F32)) for i in range(2)]
            Rstage = [Res(), Res()]
            load_weight(W0, RW0, w_in, 8, 1952, lambda c: gA[:, c:c + 1], stage, Rstage, wsem)
            load_weight(Wuq, RWuq, w_uq, 2, 768, lambda c: gQ[:, c:c + 1], stage, Rstage, wsem)
            load_weight(Wukv, RWukv, w_ukv, 1, 1024, lambda c: gKV[:, 0:1], stage, Rstage, wsem)
            k.barrier()
        if STOP == "W":
            k.barrier()
            return nc

        xs = [tb(f"xs{i}", [128, 1024], F32) for i in range(3)]; Rxs = [Res() for _ in range(3)]
        xsem = [k.sem(f"x{i}") for i in range(3)]
        cqsem = [k.sem(f"cq{i}") for i in range(4)]
        cksem = [k.sem(f"ck{i}") for i in range(4)]
        csk = [tb(f"csk{i}", [128, 32], F32) for i in range(4)]; Rcsk = [Res() for _ in range(4)]
        csq = [tb(f"csq{i}", [128, 256], F32) for i in range(4)]; Rcsq = [Res() for _ in range(4)]
        junk = tb("junk", [128, 1024], BF16); Rjunk = Res()
        cS2 = [tb(f"cS{i}", [128, 160], F32) for i in range(3)]; RcS2 = [Res() for _ in range(3)]
        xn2 = [tb(f"xn{i}", [128, 1024], BF16) for i in range(3)]; Rxn2 = [Res() for _ in range(3)]
        xnT2 = [tb(f"xnT{i}", [128, 8, 128], BF16) for i in range(3)]; RxnT2 = [Res() for _ in range(3)]
        kA2 = [tb(f"kA{i}", [128, 512], BF16) for i in range(3)]; RkA2 = [Res() for _ in range(3)]
        kB2 = [tb(f"kB{i}", [128, 8, 64], BF16) for i in range(3)]; RkB2 = [Res() for _ in range(3)]
        qA2 = [tb(f"qA{i}", [128, 512], BF16) for i in range(3)]; RqA2 = [Res() for _ in range(3)]
        qB2 = [tb(f"qB{i}", [128, 8, 96], BF16) for i in range(3)]; RqB2 = [Res() for _ in range(3)]
        Mfull2 = [tb(f"Mfull{i}", [128, 8, 96], BF16) for i in range(3)]; RMf2 = [Res() for _ in range(3)]
        ckvn2 = [tb(f"ckvn{i}", [128, 128], BF16) for i in range(3)]; Rckvn2 = [Res() for _ in range(3)]
        ckvnT2 = [tb(f"ckvnT{i}", [128, 128], BF16) for i in range(3)]; RckvnT2 = [Res() for _ in range(3)]
        cqn2 = [tb(f"cqn{i}", [128, 256], BF16) for i in range(3)]; Rcqn2 = [Res() for _ in range(3)]
        cqnT2 = [tb(f"cqnT{i}", [128, 2, 128], BF16) for i in range(3)]; RcqnT2 = [Res() for _ in range(3)]
        krr2 = [tb(f"krr{i}", [128, 32], BF16) for i in range(3)]; Rkrr2 = [Res() for _ in range(3)]
        rt2 = [tb(f"rt{i}", [128, 4, 64], F32) for i in range(3)]; Rrt2 = [Res() for _ in range(3)]
        qf2 = [tb(f"qf{i}", [128, 384], F32) for i in range(3)]; Rqf2 = [Res() for _ in range(3)]
        ksum = tb("ksum", [64, 8, 32], F32); Rksum = Res()
        kpart2 = [tb(f"kpart{i}", [64, 16], F32) for i in range(2)]; Rkpart2 = [Res() for _ in range(3)]; Rksum2 = [Res(), Res()]
        kmT = tb("kmT", [64, 8, 32], BF16); RkmT = Res()
        gateS2 = [tb(f"gateS{i}", [128, 8, 32], F32) for i in range(3)]; RgS2 = [Res() for _ in range(3)]
        top82 = [tb(f"top8{i}", [128, 64], F32) for i in range(3)]; Rtop2 = [Res() for _ in range(3)]
        onesb = tb("onesb", [128, 8, 64], BF16)
        kTbA = [tb(f"kTbA{i}", [64, 8, 512], BF16) for i in range(2)]
        kTbB = [tb(f"kTbB{i}", [64, 8, 512], BF16) for i in range(2)]
        krTb = [tb(f"krTb{i}", [32, 512], BF16) for i in range(2)]
        VBa = [tb(f"VBa{i}", [128, 8, 4, 128], BF16) for i in range(2)]
        VBb = [tb(f"VBb{i}", [128, 8, 4, 128], BF16) for i in range(2)]
        qTbA = [tb(f"qTbA{i}", [96, 8, 512], BF16) for i in range(2)]
        qTbB = [tb(f"qTbB{i}", [96, 8, 512], BF16) for i in range(2)]
        RF = [[Res() for _ in range(4)] for _ in range(2)]
        stsem = [k.sem("st0"), k.sem("st1")]

        k.op(POOL, [], [R_const], lambda: nc.gpsimd.memset(onesb[:], 1.0))
        for i in range(3):
            k.op(POOL, [], [RMf2[i]], lambda: nc.gpsimd.memset(Mfull2[i][:], 0.0))
        k.op(POOL, [], [RkmT], lambda: nc.gpsimd.memset(kmT[:], 0.0))

        def issue_x(t):
            i = t % 3
            i4 = t % 4
            k.dma(SP, xsem[i], [(xs[i][:], xc[t * 128:(t + 1) * 128, :], [], [Rxs[i]])])
            k.dma(SP, cksem[i4], [(csk[i4][:], csk_d[t], [], [Rcsk[i4]])])
            if t >= 31:
                k.dma(SP, cqsem[i4], [(csq[i4][:], csq_d[t - 31], [], [Rcsq[i4]])])

        issue_x(0)

        def do_tile(t):
            i = t % 2
            x3 = t % 3
            x4 = t % 4
            sx = t % 3
            quad, j = t // 4, t % 4
            qp = quad % 2
            isq = t >= 31
            nblk = t // 2
            RFj = RF[qp][j]
            xn = xn2[sx]; Rxn = Rxn2[sx]
            xnT = xnT2[sx]; RxnT = RxnT2[sx]
            kA = kA2[sx]; RkA = RkA2[sx]
            kB = kB2[sx]; RkB = RkB2[sx]
            qA = qA2[sx]; RqA = RqA2[sx]
            qB = qB2[sx]; RqB = RqB2[sx]
            Mfull = Mfull2[sx]; RMf = RMf2[sx]
            ckvn = ckvn2[sx]; Rckvn = Rckvn2[sx]
            ckvnT = ckvnT2[sx]; RckvnT = RckvnT2[sx]
            cqn = cqn2[sx]; Rcqn = Rcqn2[sx]
            cqnT = cqnT2[sx]; RcqnT = RcqnT2[sx]
            krr = krr2[sx]; Rkrr = Rkrr2[sx]
            rt = rt2[sx]; Rrt = Rrt2[sx]
            qf = qf2[sx]; Rqf = Rqf2[sx]
            gateS = gateS2[sx]; RgS = RgS2[sx]
            top8 = top82[sx]; Rtop = Rtop2[sx]
            kpart = kpart2[nblk % 2]; Rkpart = Rkpart2[nblk % 2]; Rksum = Rksum2[nblk % 2]
            cS = cS2[sx]; RcS = RcS2[sx]
            cnt = [0]

            def ps_next():
                bnk = 2 * sx + cnt[0] % 2
                cnt[0] += 1
                return psT[bnk], psR[bnk]

            def transposes(src_list, reads, a_, b_, rows, c_):
                return transposes_g(src_list, reads, a_, b_, rows, c_, alloc=ps_next)
            if t + 1 < NT:
                issue_x(t + 1)
            rs, Rr = rstd_of(xs[x3][:], 1024, [Rxs[x3]], junk[:], Rjunk)
            k.op(DVE, [Rxs[x3], Rr], [Rxn], lambda: nc.vector.tensor_scalar(out=xn[:], in0=xs[x3][:], scalar1=rs, scalar2=None, op0=ALU.mult))
            yield
            chk(1)
            pv, Rp = transposes([xn[:, c * 128:(c + 1) * 128] for c in range(8)], [Rxn], None, None, 128, None)
            k.op(ACT, [Rp], [RxnT], lambda: nc.scalar.copy(out=xnT[:].rearrange("p c t -> p (c t)"), in_=pv[:, :]))
            yield
            chk(2)

            def proj(c0, c1):
                p, R = ps_next()
                for c in range(8):
                    k.op(PE, [RxnT, RW0], [R], lambda c=c: nc.tensor.matmul(p[:, 0:c1 - c0], lhsT=xnT[:, c, :], rhs=W0[:, c, c0:c1], start=(c == 0), stop=(c == 7)))
                return p, R
            p_k, R_k = proj(512, 1024)

            chk(3)
            k.op(ACT, [R_k], [RkA], lambda: nc.scalar.copy(out=kA[:], in_=p_k[:, :]))
            yield
            pv, Rp = transposes([kA[:, h * 64:(h + 1) * 64] for h in range(8)], [RkA], None, None, 64, None)
            pv3 = pv[0:64, :].rearrange("p (h t) -> p h t", h=8)
            chk(31)
            k.op(ACT, [Rp], [RFj], lambda: nc.scalar.copy(out=kTbA[qp][:, :, j * 128:(j + 1) * 128], in_=pv3))
            yield
            chk(32)
            ksrc = kTbA[qp][:, :, j * 128:(j + 1) * 128]
            if t % 2 == 0:
                k.op(DVE, [RFj], [Rksum], lambda: nc.vector.reduce_sum(out=kpart[:, 0:8], in_=ksrc, axis=AX.X))
                yield
            else:
                k.op(DVE, [RFj], [Rkpart], lambda: nc.vector.reduce_sum(out=kpart[:, 8:16], in_=ksrc, axis=AX.X))
                yield
                k.op(DVE, [Rkpart, Rksum], [RkmT], lambda: nc.vector.tensor_tensor(out=kmT[:, :, nblk], in0=kpart[:, 0:8], in1=kpart[:, 8:16], op=ALU.add))
                yield
            chk(33)
            chk(4)
            p_v, R_v = proj(1024, 1536)
            k.op(ACT, [R_v], [RFj], lambda: nc.scalar.copy(out=VBa[qp][:, :, j, 0:64], in_=p_v[:, :].rearrange("p (h d) -> p h d", h=8)))
            yield
            k.op(DVE, [R_const], [RFj], lambda: nc.vector.tensor_scalar(out=VBa[qp][:, :, j, 64:128], in0=onesb[:], scalar1=kval[:, t:t + 1], scalar2=None, op0=ALU.mult))
            yield
            k.op(ACT, [R_const], [RFj], lambda: nc.scalar.activation(out=VBb[qp][:, :, j, 64:128], in_=onesb[:], func=AF.Copy, scale=kval[:, t:t + 1]))
            yield

            chk(5)
            p_cp, R_cp = proj(1792, 1952)
            k.op(ACT, [R_cp], [RcS], lambda: nc.scalar.copy(out=cS[:], in_=p_cp[:, 0:160]))
            yield
            p_c = cS; R_c = RcS
            rs2, Rr2 = rstd_of(p_c[:, 0:128], 128, [R_c], junk[:, 0:128], Rjunk)
            k.op(DVE, [R_c, Rr2], [Rckvn], lambda: nc.vector.tensor_scalar(out=ckvn[:], in0=p_c[:, 0:128], scalar1=rs2, scalar2=None, op0=ALU.mult))
            yield
            x1 = p_c[:, 128:144]; x2 = p_c[:, 144:160]
            co = csk[x4][:, 0:16]; si = csk[x4][:, 16:32]
            k.op(DVE, [R_c, Rcsk[x4]], [Rrt], lambda: nc.vector.tensor_tensor(out=rt[:, 0, 0:16], in0=x1, in1=co, op=ALU.mult))
            yield
            k.op(DVE, [R_c, Rcsk[x4]], [Rrt], lambda: nc.vector.tensor_tensor(out=rt[:, 1, 0:16], in0=x2, in1=si, op=ALU.mult))
            yield
            k.op(DVE, [R_c, Rcsk[x4]], [Rrt], lambda: nc.vector.tensor_tensor(out=rt[:, 2, 0:16], in0=x2, in1=co, op=ALU.mult))
            yield
            k.op(DVE, [R_c, Rcsk[x4]], [Rrt], lambda: nc.vector.tensor_tensor(out=rt[:, 3, 0:16], in0=x1, in1=si, op=ALU.mult))
            yield
            k.op(DVE, [Rrt], [Rkrr], lambda: nc.vector.tensor_tensor(out=krr[:, 0:16], in0=rt[:, 0, 0:16], in1=rt[:, 1, 0:16], op=ALU.subtract))
            yield
            k.op(DVE, [Rrt], [Rkrr], lambda: nc.vector.tensor_tensor(out=krr[:, 16:32], in0=rt[:, 2, 0:16], in1=rt[:, 3, 0:16], op=ALU.add))
            yield
            chk(6)
            pv, Rp = transposes([ckvn[:]], [Rckvn], None, None, 128, None)
            k.op(ACT, [Rp], [RckvnT], lambda: nc.scalar.copy(out=ckvnT[:], in_=pv[:, 0:128]))
            yield
            pv, Rp = transposes([krr[:]], [Rkrr], None, None, 32, None)
            k.op(ACT, [Rp], [RFj], lambda: nc.scalar.copy(out=krTb[qp][:, j * 128:(j + 1) * 128], in_=pv[0:32, 0:128]))
            yield
            chk(7)
            for hh in range(2):
                p, R = ps_next()
                k.op(PE, [RckvnT, RWukv], [R], lambda: nc.tensor.matmul(p[:, :], lhsT=ckvnT[:], rhs=Wukv[:, 0, hh * 512:(hh + 1) * 512], start=True, stop=True))
                yield
                p4 = p[:, :].rearrange("p (h two d) -> p h two d", h=4, two=2)
                k.op(ACT, [R], [RkB], lambda: nc.scalar.copy(out=kB[:, hh * 4:hh * 4 + 4, :], in_=p4[:, :, 0, :]))
                yield
                k.op(DVE, [R], [RFj], lambda: nc.vector.tensor_copy(out=VBb[qp][:, hh * 4:hh * 4 + 4, j, 0:64], in_=p4[:, :, 1, :]))
                yield
            pv, Rp = transposes([kB[:, h, :] for h in range(8)], [RkB], None, None, 64, None)
            pv3 = pv[0:64, :].rearrange("p (h t) -> p h t", h=8)
            k.op(ACT, [Rp], [RFj], lambda: nc.scalar.copy(out=kTbB[qp][:, :, j * 128:(j + 1) * 128], in_=pv3))
            yield

            chk(8)
            if isq:
                p_q, R_q = proj(0, 512)
                k.op(ACT, [R_q], [RqA], lambda: nc.scalar.activation(out=qA[:], in_=p_q[:, :], func=AF.Copy, scale=0.125))
                yield
                pv, Rp = transposes([qA[:, h * 64:(h + 1) * 64] for h in range(8)], [RqA], None, None, 64, None)
                pv3 = pv[0:64, :].rearrange("p (h t) -> p h t", h=8)
                k.op(ACT, [Rp], [RFj], lambda: nc.scalar.copy(out=qTbA[qp][0:64, :, j * 128:(j + 1) * 128], in_=pv3))
                yield
                chk(41)
                pg, Rg = ps_next()
                for h in range(8):
                    k.op(PE, [RFj, RkmT], [Rg], lambda h=h: nc.tensor.matmul(pg[:, h * 32:(h + 1) * 32], lhsT=qTbA[qp][0:64, h, j * 128:(j + 1) * 128], rhs=kmT[:, h, :], start=True, stop=True))
                chk(42)
                k.op(DVE, [Rg, R_const], [RgS], lambda: nc.vector.tensor_tensor(out=gateS[:].rearrange("p h n -> p (h n)"), in0=pg[:, 0:256], in1=gbias[:], op=ALU.add))
                yield
                if nblk < 32:
                    k.op(DVE, [], [RgS], lambda: nc.vector.memset(gateS[:, :, nblk:32], NEG))
                chk(43)
                for h in range(8):
                    k.op(DVE, [RgS], [Rtop], lambda h=h: nc.vector.max(out=top8[:, h * 8:(h + 1) * 8], in_=gateS[:, h, :]))
                chk(44)
                for h in range(8):
                    k.op(DVE, [RgS, Rtop], [RMf], lambda h=h: nc.vector.tensor_scalar(out=Mfull[:, h, 64:96], in0=gateS[:, h, :], scalar1=top8[:, h * 8 + 2:h * 8 + 3], scalar2=NEG, op0=ALU.is_lt, op1=ALU.mult))
                k.op(DVE, [], [RMf], lambda: nc.vector.memset(Mfull[:, :, 64 + nblk:65 + nblk], 0.0))
                yield
                chk(45)
                pv, Rp = transposes([Mfull[:, h, :] for h in range(8)], [RMf], None, None, 96, None)
                pv3 = pv[64:96, :].rearrange("p (h t) -> p h t", h=8)
                k.op(ACT, [Rp], [RFj], lambda: nc.scalar.copy(out=qTbA[qp][64:96, :, j * 128:(j + 1) * 128], in_=pv3))
                yield
                chk(46)
                p_cq, R_cq = proj(1536, 1792)
                rs3, Rr3 = rstd_of(p_cq[:, 0:256], 256, [R_cq], junk[:, 0:256], Rjunk)
                k.op(DVE, [R_cq, Rr3], [Rcqn], lambda: nc.vector.tensor_scalar(out=cqn[:], in0=p_cq[:, 0:256], scalar1=rs3, scalar2=None, op0=ALU.mult))
                yield
                pv, Rp = transposes([cqn[:, c * 128:(c + 1) * 128] for c in range(2)], [Rcqn], None, None, 128, None)
                k.op(ACT, [Rp], [RcqnT], lambda: nc.scalar.copy(out=cqnT[:].rearrange("p c t -> p (c t)"), in_=pv[:, 0:256]))
                yield
                chk(47)
                for hh in range(2):
                    p, R = ps_next()
                    for c in range(2):
                        k.op(PE, [RcqnT, RWuq], [R], lambda c=c: nc.tensor.matmul(p[:, 0:384], lhsT=cqnT[:, c, :], rhs=Wuq[:, c, hh * 384:(hh + 1) * 384], start=(c == 0), stop=(c == 1)))
                    p3 = p[:, 0:384].rearrange("p (h d) -> p h d", h=4)
                    hs_ = slice(hh * 4, hh * 4 + 4)
                    k.op(ACT, [R], [RqB], lambda: nc.scalar.activation(out=qB[:, hs_, 0:64], in_=p3[:, :, 0:64], func=AF.Copy, scale=96.0 ** -0.5))
                    chk(48)
                    k.op(ACT, [R], [Rqf], lambda: nc.scalar.copy(out=qf[:], in_=p[:, 0:384]))
                    q3 = qf[:].rearrange("p (h d) -> p h d", h=4)
                    x1 = q3[:, :, 64:80]; x2 = q3[:, :, 80:96]
                    co = csq[x4][:, 0:128].rearrange("p (h f) -> p h f", h=8)[:, hs_, :]
                    si = csq[x4][:, 128:256].rearrange("p (h f) -> p h f", h=8)[:, hs_, :]
                    k.op(DVE, [Rqf, Rcsq[x4]], [Rrt], lambda: nc.vector.tensor_tensor(out=rt[:, :, 0:16], in0=x1, in1=co, op=ALU.mult))
                    k.op(DVE, [Rqf, Rcsq[x4]], [Rrt], lambda: nc.vector.tensor_tensor(out=rt[:, :, 16:32], in0=x2, in1=si, op=ALU.mult))
                    k.op(DVE, [Rqf, Rcsq[x4]], [Rrt], lambda: nc.vector.tensor_tensor(out=rt[:, :, 32:48], in0=x2, in1=co, op=ALU.mult))
                    k.op(DVE, [Rqf, Rcsq[x4]], [Rrt], lambda: nc.vector.tensor_tensor(out=rt[:, :, 48:64], in0=x1, in1=si, op=ALU.mult))
                    k.op(DVE, [Rrt], [RqB], lambda: nc.vector.tensor_tensor(out=qB[:, hs_, 64:80], in0=rt[:, :, 0:16], in1=rt[:, :, 16:32], op=ALU.subtract))
                    k.op(DVE, [Rrt], [RqB], lambda: nc.vector.tensor_tensor(out=qB[:, hs_, 80:96], in0=rt[:, :, 32:48], in1=rt[:, :, 48:64], op=ALU.add))
                chk(49)
                pv, Rp = transposes([qB[:, h, :] for h in range(8)], [RqB], None, None, 96, None)
                pv3 = pv[0:96, :].rearrange("p (h t) -> p h t", h=8)
                k.op(ACT, [Rp], [RFj], lambda: nc.scalar.copy(out=qTbB[qp][:, :, j * 128:(j + 1) * 128], in_=pv3))
                yield
                chk(50)

            if j == 3:
                rd = RF[qp]
                ks = slice(quad * 512, quad * 512 + 512)
                items = [
                    (KTa[:, :, ks].rearrange("h d k -> d h k"), kTbA[qp][:], rd, []),
                    (KTb[:, :, ks].rearrange("h d k -> d h k"), kTbB[qp][:], rd, []),
                    (KRT[:, ks], krTb[qp][:], rd, []),
                    (Va[:, :, quad * 4:quad * 4 + 4, :].rearrange("h p t c -> p h t c"), VBa[qp][:], rd, []),
                    (Vb[:, :, quad * 4:quad * 4 + 4, :].rearrange("h p t c -> p h t c"), VBb[qp][:], rd, []),
                ]
                if quad == 7:
                    items.append((QTa[:, :, 0:128].rearrange("h d k -> d h k"), qTbA[qp][:, :, 384:512], rd, []))
                    items.append((QTb[:, :, 0:128].rearrange("h d k -> d h k"), qTbB[qp][:, :, 384:512], rd, []))
                elif quad >= 8:
                    qs = slice(128 + (quad - 8) * 512, 128 + (quad - 8) * 512 + 512)
                    items.append((QTa[:, :, qs].rearrange("h d k -> d h k"), qTbA[qp][:], rd, []))
                    items.append((QTb[:, :, qs].rearrange("h d k -> d h k"), qTbB[qp][:], rd, []))
                k.dma(POOL, stsem[qp], items)
            if STOP == "P0a" and t == 3:
                raise _Stop()
        try:
            gens = []
            t_next = 0
            OFFSET = int(os.environ.get("KOFF", "2"))
            while gens or t_next < NT:
                if t_next < NT and len(gens) < 3 and (not gens or gens[-1][1] >= OFFSET):
                    gens.append([do_tile(t_next), 0])
                    t_next += 1
                for ge in list(gens):
                    try:
                        next(ge[0])
                        ge[1] += 1
                    except StopIteration:
                        gens.remove(ge)
        except _Stop:
            pass
        k.barrier()
    if STOP in ("P0", "P0a", "P0b"):
        return nc

    with ExitStack() as es:
        def tb(name, shape, dt):
            return es.enter_context(nc.sbuf_tensor(name, list(shape), dt))
        Kt = [tb(f"Kt{i}", [96, 8192], BF16) for i in range(2)]
        Vt = [tb(f"Vt{i}", [128, 64, 128], BF16) for i in range(2)]
        Qt = [tb(f"Qt{i}", [96, NQ], BF16) for i in range(2)]
        NF = [tb(f"NF{i}", [128, 6, 512], BF16) for i in range(2)]
        NFh = [tb(f"NFh{i}", [128, 4, 128], BF16) for i in range(2)]
        Rh = [Res(), Res()]
        hsem = [k.sem("h0"), k.sem("h1")]
        CM = tb("CM", [128, 4, 512], BF16)
        b31 = tb("b31_s", [128, 8], F32)
        Pt = [tb(f"Pt{i}", [128, 1024], BF16) for i in range(4)]
        RPt = [Res() for _ in range(4)]
        rsb = [tb(f"rsb{i}", [64, 512], F32) for i in range(2)]
        yTb = [tb(f"yTb{i}", [64, 512], BF16) for i in range(2)]
        Ry = [Res(), Res()]
        ysem = [k.sem("y0"), k.sem("y1")]
        k.dma(SP, k.sem("ld2"), [(CM[:].rearrange("p a b -> p (a b)"), cm_d[:, :], [], [R_const]),
                                 (b31[:], b31_d[:, :], [], [R_const])])

        def load_head(hh):
            i = hh % 2
            h = hh % 8
            moba = hh < 8
            items = [
                (Kt[i][0:64, :], (KTa if moba else KTb)[h], [], [Rh[i]]),
                (Kt[i][64:96, :], oh_d[:, :] if moba else KRT[:, :], [], [Rh[i]]),
                (Vt[i][:], (Va if moba else Vb)[h], [], [Rh[i]]),
                (Qt[i][:], (QTa if moba else QTb)[h], [], [Rh[i]]),
            ]
            if moba:
                items.append((NF[i][:].rearrange("p a b -> p (a b)"), nf_d[h], [], [Rh[i]]))
                items.append((NFh[i][:].rearrange("p a b -> p (a b)"), nfh_d[h], [], [Rh[i]]))
            k.dma(SP, hsem[i], items)

        load_head(0)
        pcnt = [0]
        gcnt = [0]
        for hh in range(16):
            i = hh % 2
            h = hh % 8
            moba = hh < 8
            if hh + 1 < 16:
                load_head(hh + 1)
            for gi in range(-1, 8):
                if gi < 0:
                    W, qc0, nvis = 128, 0, 32
                else:
                    W, qc0, nvis = 512, 128 + 512 * gi, 32 + 4 * gi + 4
                tabs = {}
                if moba:
                    if gi < 0:
                        for a in range(4):
                            tabs[28 + a] = NFh[i][:, a, :]
                    else:
                        for a in range(6):
                            tabs[30 + 4 * gi + a] = NF[i][:, a, :]
                else:
                    if gi < 0:
                        tabs[31] = CM[:, 0, 0:128]
                    else:
                        for a in range(4):
                            tabs[32 + 4 * gi + a] = CM[:, a, :]
                po, Ro = ps_next(6, 8)
                pend = []

                def emit_pv(kt, pi, first, last):
                    k.op(PE, [Rh[i], RPt[pi]], [Ro], lambda: nc.tensor.matmul(po[:, 0:W], lhsT=Vt[i][:, kt, :], rhs=Pt[pi][:, 0:W], start=first, stop=last))

                for kt in range(nvis):
                    p, R = ps_next(0, 6)
                    tab = tabs.get(kt)
                    k.op(PE, [Rh[i]], [R], lambda: nc.tensor.matmul(p[:, 0:W], lhsT=Kt[i][:, kt * 128:(kt + 1) * 128], rhs=Qt[i][:, qc0:qc0 + W], start=True, stop=(tab is None)))
                    if tab is not None:
                        k.op(PE, [Rh[i], R_const], [R], lambda: nc.tensor.matmul(p[:, 0:W], lhsT=idb[:], rhs=tab, start=False, stop=True))
                    pi = pcnt[0] % 4
                    pcnt[0] += 1
                    if moba and tab is None:
                        k.op(ACT, [R, R_const], [RPt[pi]], lambda: nc.scalar.activation(out=Pt[pi][:, 0:W], in_=p[:, 0:W], func=AF.Exp, bias=b31[:, h:h + 1]))
                    else:
                        k.op(ACT, [R], [RPt[pi]], lambda: nc.scalar.activation(out=Pt[pi][:, 0:W], in_=p[:, 0:W], func=AF.Exp))
                    pend.append((kt, pi))
                    if len(pend) > 2:
                        kt0, pi0 = pend.pop(0)
                        emit_pv(kt0, pi0, kt0 == 0, False)
                while pend:
                    kt0, pi0 = pend.pop(0)
                    emit_pv(kt0, pi0, kt0 == 0, kt0 == nvis - 1)
                yi = gcnt[0] % 2
                gcnt[0] += 1
                k.op(DVE, [Ro], [Ry[yi]], lambda: nc.vector.tensor_scalar(out=rsb[yi][:, 0:W], in0=po[64:128, 0:W], scalar1=1e-30, scalar2=None, op0=ALU.max))
                k.op(DVE, [Ry[yi]], [Ry[yi]], lambda: nc.vector.reciprocal(out=rsb[yi][:, 0:W], in_=rsb[yi][:, 0:W]))
                k.op(DVE, [Ro, Ry[yi]], [Ry[yi]], lambda: nc.vector.tensor_tensor(out=yTb[yi][:, 0:W], in0=po[0:64, 0:W], in1=rsb[yi][:, 0:W], op=ALU.mult))
                k.dma(POOL, ysem[yi], [(YT[hh, :, qc0:qc0 + W], yTb[yi][:, 0:W], [Ry[yi]], [])])
        k.barrier()
    if STOP == "P1":
        return nc

    with ExitStack() as es:
        def tb(name, shape, dt):
            return es.enter_context(nc.sbuf_tensor(name, list(shape), dt))
        Wg = tb("Wg", [128, 8, 2048], BF16); RWg = Res()
        Wm = tb("Wm", [64, 8, 1024], BF16); RWm = Res()
        Wl = tb("Wl", [64, 8, 1024], BF16); RWl = Res()
        Wo = tb("Wo", [128, 8, 1024], BF16); RWo = Res()
        gA = tb("gA2", [128, 8], F32)
        bg = tb("bg_s", [128, 16], F32)
        k.dma(SP, k.sem("ld3"), [(gA[:], gA_d[:, :], [], [R_const]), (bg[:], bg_d[:, :], [], [R_const])])
        stage = [tb(f"wstb{i}", [128, 2048], F32) for i in range(2)]
        Rstage = [Res(), Res()]
        load_weight(Wg, RWg, w_in, 8, 2048, lambda c: gA[:, c:c + 1], stage, Rstage, wsem, c0=1952)
        load_weight(Wm, RWm, w_bm, 8, 1024, None, stage, Rstage, wsem, rows=64)
        load_weight(Wl, RWl, w_bl, 8, 1024, None, stage, Rstage, wsem, rows=64)
        load_weight(Wo, RWo, w_out, 8, 1024, None, stage, Rstage, wsem)
        WG2 = 256
        xs = [tb(f"xsB{i}", [128, 1024], F32) for i in range(4)]; Rxs = [Res() for _ in range(4)]
        xsem = [k.sem(f"xb{i}") for i in range(4)]
        hsem2 = [k.sem(f"hs{i}") for i in range(4)]
        junk = tb("junkB", [128, 1024], BF16); Rjunk = Res()
        xn2 = [tb(f"xnB{i}", [128, 1024], BF16) for i in range(2)]; Rxn2 = [Res(), Res()]
        xnT2 = [tb(f"xnTB{i}", [128, 8, WG2], BF16) for i in range(2)]; RxnT2 = [Res(), Res()]
        ysb2 = [tb(f"ysb{i}", [64, 16, WG2], BF16) for i in range(2)]; Rysb2 = [Res(), Res()]
        ysem2 = [k.sem("ysb0"), k.sem("ysb1")]
        gT2 = [tb(f"gT{i}", [128, 16, WG2], F32) for i in range(2)]; RgT2 = [Res(), Res()]
        t1 = [tb(f"t1{i}", [128, WG2], F32) for i in range(4)]
        t2 = [tb(f"t2{i}", [128, WG2], F32) for i in range(4)]
        Rt = [Res() for _ in range(4)]
        mixT2 = [tb(f"mixT{i}", [128, 8, WG2], BF16) for i in range(2)]; Rmix2 = [Res(), Res()]
        xcnt = [0]

        def do_group2(idx, gi, W, qc0, r0):
            par = idx % 2
            xn, Rxn, xnT, RxnT = xn2[par], Rxn2[par], xnT2[par], RxnT2[par]
            ysb, Rysb, gT, RgT, mixT, Rmix = ysb2[par], Rysb2[par], gT2[par], RgT2[par], mixT2[par], Rmix2[par]
            ntt = W // 128
            sl = []
            for tt in range(ntt):
                sl.append(xcnt[0] % 4)
                xcnt[0] += 1
            k.dma(SP, ysem2[par], [(ysb[:, :, 0:W], YT[:, :, qc0:qc0 + W].rearrange("h d k -> d h k"), [], [Rysb])])
            for tt in range(ntt):
                a_ = sl[tt]
                k.dma(SP, xsem[a_], [(xs[a_][:], xc[r0 + tt * 128:r0 + (tt + 1) * 128, :], [], [Rxs[a_]])])
            yield
            for tt in range(ntt):
                a_ = sl[tt]
                rs, Rr = rstd_of(xs[a_][:], 1024, [Rxs[a_]], junk[:], Rjunk)
                k.op(DVE, [Rxs[a_], Rr], [Rxn], lambda: nc.vector.tensor_scalar(out=xn[:], in0=xs[a_][:], scalar1=rs, scalar2=None, op0=ALU.mult))
                pv, Rp = transposes([xn[:, c * 128:(c + 1) * 128] for c in range(8)], [Rxn], None, None, 128, None)
                k.op(ACT, [Rp], [RxnT], lambda: nc.scalar.copy(out=xnT[:, :, tt * 128:(tt + 1) * 128], in_=pv[:, :].rearrange("p (c t) -> p c t", c=8)))
                yield
            for ct in range(16):
                p, R = ps_next()
                for c in range(8):
                    k.op(PE, [RxnT, RWg], [R], lambda c=c: nc.tensor.matmul(p[:, 0:W], lhsT=Wg[:, c, ct * 128:(ct + 1) * 128], rhs=xnT[:, c, 0:W], start=(c == 0), stop=(c == 7)))
                k.op(ACT, [R, R_const], [RgT], lambda: nc.scalar.activation(out=gT[:, ct, 0:W], in_=p[:, 0:W], func=AF.Sigmoid, bias=bg[:, ct:ct + 1]))
                yield
            for ct in range(8):
                pa, Ra = ps_next()
                for h in range(8):
                    k.op(PE, [Rysb, RWm], [Ra], lambda h=h: nc.tensor.matmul(pa[:, 0:W], lhsT=Wm[:, h, ct * 128:(ct + 1) * 128], rhs=ysb[:, h, 0:W], start=(h == 0), stop=(h == 7)))
                pb, Rb = ps_next()
                for h in range(8):
                    k.op(PE, [Rysb, RWl], [Rb], lambda h=h: nc.tensor.matmul(pb[:, 0:W], lhsT=Wl[:, h, ct * 128:(ct + 1) * 128], rhs=ysb[:, 8 + h, 0:W], start=(h == 0), stop=(h == 7)))
                ti = ct % 4
                k.op(DVE, [Ra, RgT], [Rt[ti]], lambda: nc.vector.tensor_tensor(out=t1[ti][:, 0:W], in0=pa[:, 0:W], in1=gT[:, ct, 0:W], op=ALU.mult))
                k.op(DVE, [Rb, RgT], [Rt[ti]], lambda: nc.vector.tensor_tensor(out=t2[ti][:, 0:W], in0=pb[:, 0:W], in1=gT[:, 8 + ct, 0:W], op=ALU.mult))
                k.op(DVE, [Rt[ti]], [Rmix], lambda: nc.vector.tensor_tensor(out=mixT[:, ct, 0:W], in0=t1[ti][:, 0:W], in1=t2[ti][:, 0:W], op=ALU.add))
                yield
            for tt in range(ntt):
                a_ = sl[tt]
                for hf in range(2):
                    p, R = ps_next()
                    for c in range(8):
                        k.op(PE, [Rmix, RWo], [R], lambda c=c: nc.tensor.matmul(p[:, :], lhsT=mixT[:, c, tt * 128:(tt + 1) * 128], rhs=Wo[:, c, hf * 512:(hf + 1) * 512], start=(c == 0), stop=(c == 7)))
                    k.op(DVE, [R, Rxs[a_]], [Rxs[a_]], lambda: nc.vector.tensor_tensor(out=xs[a_][:, hf * 512:(hf + 1) * 512], in0=p[:, :], in1=xs[a_][:, hf * 512:(hf + 1) * 512], op=ALU.add))
                k.dma(SP, hsem2[a_], [(H1[qc0 + tt * 128:qc0 + (tt + 1) * 128, :], xs[a_][:], [Rxs[a_]], [])])
                yield

        groups2 = [(-1, 128, 0, 31 * 128)] + [(g, WG2, 128 + WG2 * g, 4096 + WG2 * g) for g in range(4096 // WG2)]
        run_pipelined([do_group2(n_, *g_) for n_, g_ in enumerate(groups2)], int(os.environ.get("KOFF2", "2")))
        k.barrier()
    if STOP == "P2":
        return nc

    with ExitStack() as es:
        def tb(name, shape, dt):
            return es.enter_context(nc.sbuf_tensor(name, list(shape), dt))
        Wu = tb("Wu", [128, 8, 5632], BF16); RWu = Res()
        Wd = tb("Wd", [128, 22, 1024], BF16); RWd = Res()
        gF = tb("gF_s", [128, 8], F32)
        cw = tb("cw_s", [128, 44, 3], F32)
        cb = tb("cb_s", [128, 44], F32)
        gO = tb("gO_s", [128, 1024], F32)
        HB = tb("HB", [128, 44, 2], F32); RHB = Res()
        k.dma(SP, k.sem("ld4"), [(gF[:], gF_d[:, :], [], [R_const]), (cw[:].rearrange("p a b -> p (a b)"), cw_d[:, :], [], [R_const]),
                                 (cb[:], cb_d[:, :], [], [R_const]), (gO[:], gO_d[:, :], [], [R_const])])
        with ExitStack() as es2:
            stage = [es2.enter_context(nc.sbuf_tensor(f"wstc{i}", [128, 2048], F32)) for i in range(2)]
            Rstage = [Res(), Res()]
            load_weight(Wu, RWu, w_up, 8, 5632, lambda c: gF[:, c:c + 1], stage, Rstage, wsem)
            load_weight(Wd, RWd, w_dn, 22, 1024, None, stage, Rstage, wsem)
            k.barrier()
        WG = 256
        hs = [tb(f"hs{i}", [128, 1024], F32) for i in range(4)]; Rhs = [Res() for _ in range(4)]
        lsem = [k.sem(f"l{i}") for i in range(4)]
        osem = [k.sem(f"o{i}") for i in range(4)]
        junk = tb("junkC", [128, 1024], BF16); Rjunk = Res()
        hn2 = [tb(f"hnC{i}", [128, 1024], BF16) for i in range(2)]; Rhn2 = [Res(), Res()]
        hnT2 = [tb(f"hnT{i}", [128, 8, WG], BF16) for i in range(2)]; RhnT2 = [Res(), Res()]
        actT2 = [tb(f"actT{i}", [128, 22, WG], BF16) for i in range(2)]; RactT2 = [Res(), Res()]
        upw = [tb(f"upw{i}", [128, 2 + WG], F32) for i in range(6)]; Rupw = [Res() for _ in range(6)]
        acc = [tb(f"acc{i}", [128, WG], F32) for i in range(6)]; Racc = [Res() for _ in range(6)]
        sg = [tb(f"sg{i}", [128, WG], F32) for i in range(3)]; Rsg = [Res() for _ in range(3)]
        uc = [0]
        hc = [0]
        groups = [(-1, 128, 0)] + [(g, WG, 128 + WG * g) for g in range(4096 // WG)]

        def do_group3(idx, gi, W, qc0):
            par = idx % 2
            hn, Rhn, hnT, RhnT, actT, RactT = hn2[par], Rhn2[par], hnT2[par], RhnT2[par], actT2[par], RactT2[par]
            ntt = W // 128
            hidx = []
            for tt in range(ntt):
                a = hc[0] % 4
                hc[0] += 1
                hidx.append(a)
                k.dma(SP, lsem[a], [(hs[a][:], H1[qc0 + tt * 128:qc0 + (tt + 1) * 128, :], [], [Rhs[a]])])
            yield
            for tt in range(ntt):
                a = hidx[tt]
                rs, Rr = rstd_of(hs[a][:], 1024, [Rhs[a]], junk[:], Rjunk)
                k.op(DVE, [Rhs[a], Rr], [Rhn], lambda: nc.vector.tensor_scalar(out=hn[:], in0=hs[a][:], scalar1=rs, scalar2=None, op0=ALU.mult))
                pv, Rp = transposes([hn[:, c * 128:(c + 1) * 128] for c in range(8)], [Rhn], None, None, 128, None)
                k.op(ACT, [Rp], [RhnT], lambda: nc.scalar.copy(out=hnT[:, :, tt * 128:(tt + 1) * 128], in_=pv[:, :].rearrange("p (c t) -> p c t", c=8)))
                yield
            for c in range(22):
                accs = []
                for part, ch in ((0, c), (1, 22 + c)):
                    p, R = ps_next()
                    col = ch * 128
                    for d in range(8):
                        k.op(PE, [RhnT, RWu], [R], lambda d=d: nc.tensor.matmul(p[:, 0:W], lhsT=Wu[:, d, col:col + 128], rhs=hnT[:, d, 0:W], start=(d == 0), stop=(d == 7)))
                    if gi < 0:
                        k.op(ACT, [R, R_const], [RHB], lambda: nc.scalar.activation(out=HB[:, ch, :], in_=p[:, W - 2:W], func=AF.Copy, scale=hflag[:, 0:1]))
                        continue
                    u = uc[0] % 6
                    uc[0] += 1
                    k.op(ACT, [R], [Rupw[u]], lambda: nc.scalar.copy(out=upw[u][:, 2:2 + W], in_=p[:, 0:W]))
                    k.op(DVE, [RHB], [Rupw[u]], lambda: nc.vector.tensor_copy(out=upw[u][:, 0:2], in_=HB[:, ch, :]))
                    k.op(ACT, [R, R_const], [Racc[u]], lambda: nc.scalar.activation(out=acc[u][:, 0:W], in_=p[:, 0:W], func=AF.Identity, scale=cw[:, ch, 2:3], bias=cb[:, ch:ch + 1]))
                    k.op(DVE, [Rupw[u], R_const, Racc[u]], [Racc[u]], lambda: nc.vector.scalar_tensor_tensor(out=acc[u][:, 0:W], in0=upw[u][:, 1:1 + W], scalar=cw[:, ch, 1:2], in1=acc[u][:, 0:W], op0=ALU.mult, op1=ALU.add))
                    k.op(DVE, [Rupw[u], R_const, Racc[u]], [Racc[u]], lambda: nc.vector.scalar_tensor_tensor(out=acc[u][:, 0:W], in0=upw[u][:, 0:W], scalar=cw[:, ch, 0:1], in1=acc[u][:, 0:W], op0=ALU.mult, op1=ALU.add))
                    k.op(POOL, [Rupw[u]], [RHB], lambda: nc.gpsimd.tensor_copy(out=HB[:, ch, :], in_=upw[u][:, W:W + 2]))
                    accs.append(u)
                if gi >= 0:
                    ug, uv = accs
                    si = c % 3
                    k.op(ACT, [Racc[ug]], [Rsg[si]], lambda: nc.scalar.activation(out=sg[si][:, 0:W], in_=acc[ug][:, 0:W], func=AF.Silu))
                    k.op(POOL, [Rsg[si], Racc[uv]], [RactT], lambda: nc.gpsimd.tensor_tensor(out=actT[:, c, 0:W], in0=sg[si][:, 0:W], in1=acc[uv][:, 0:W], op=ALU.mult))
                yield
            if gi >= 0:
                for tt in range(ntt):
                    a = hidx[tt]
                    for hf in range(2):
                        p, R = ps_next()
                        for c in range(22):
                            k.op(PE, [RactT, RWd], [R], lambda c=c: nc.tensor.matmul(p[:, :], lhsT=actT[:, c, tt * 128:(tt + 1) * 128], rhs=Wd[:, c, hf * 512:(hf + 1) * 512], start=(c == 0), stop=(c == 21)))
                        k.op(DVE, [R, Rhs[a]], [Rhs[a]], lambda: nc.vector.tensor_tensor(out=hs[a][:, hf * 512:(hf + 1) * 512], in0=p[:, :], in1=hs[a][:, hf * 512:(hf + 1) * 512], op=ALU.add))
                    rs, Rr = rstd_of(hs[a][:], 1024, [Rhs[a]], junk[:], Rjunk)
                    k.op(DVE, [Rhs[a], Rr, R_const], [Rhs[a]], lambda: nc.vector.scalar_tensor_tensor(out=hs[a][:], in0=hs[a][:], scalar=rs, in1=gO[:], op0=ALU.mult, op1=ALU.mult))
                    row = qc0 - 128 + tt * 128
                    k.dma(SP, osem[a], [(out_d[row:row + 128, :], hs[a][:], [Rhs[a]], [])])
                    yield

        run_pipelined([do_group3(n_, *g_) for n_, g_ in enumerate(groups)], int(os.environ.get("KOFF3", "3")))
        k.barrier()
    return nc


def _t5_bucket(rel):
    n = np.maximum(rel, 0)
    nf = np.maximum(n, 1).astype(np.float32)
    large = 16 + (np.log(nf / np.float32(16)) / np.float32(math.log(128 / 16)) * np.float32(16)).astype(np.int32)
    large = np.minimum(large, 31)
    return np.where(n < 16, n, large)


def _tables():
    kk = np.arange(128)[:, None]
    qq = np.arange(256)[None, :]
    T0, T1 = [], []
    for i in range(2):
        kb = i * 128 + kk
        rel = qq - kb
        T0.append(np.where(rel >= 0, _t5_bucket(rel), 32))
        T1.append(_t5_bucket(qq + 256 - kb))
    far = np.full((128, 256), 31)
    msk = np.full((128, 256), 32)
    nf = [np.concatenate([T1[0], far], 1), np.concatenate([T1[1], far], 1),
          np.concatenate([T0[0], T1[0]], 1), np.concatenate([T0[1], T1[1]], 1),
          np.concatenate([msk, T0[0]], 1), np.concatenate([msk, T0[1]], 1)]
    nfh = [T1[0][:, 128:], T1[1][:, 128:], T0[0][:, 128:], T0[1][:, 128:]]
    return np.stack(nf, 1), np.stack(nfh, 1)


_NC_CACHE = {}


def kernel(x, norm_attn_g, w_in, b_gate, q_norm_g, w_uq, kv_norm_g, w_ukv, rel_bias,
           w_branch_moba, w_branch_mla, w_out, norm_ffn_g, w_up, conv_w, conv_b, w_down,
           norm_final_g):
    f32 = np.float32
    bf = ml_dtypes.bfloat16
    x = np.asarray(x, f32)
    c = lambda a: np.ascontiguousarray(np.asarray(a, f32))
    nfi, nfhi = _tables()
    rb_ext = np.concatenate([np.asarray(rel_bias, f32), np.full((1, 8), NEG, f32)], 0)
    nf = np.ascontiguousarray(np.transpose(rb_ext[nfi], (3, 0, 1, 2)).reshape(8, 128, 6 * 512)).astype(bf)
    nfh = np.ascontiguousarray(np.transpose(rb_ext[nfhi], (3, 0, 1, 2)).reshape(8, 128, 4 * 128)).astype(bf)
    b31 = np.ascontiguousarray(np.broadcast_to(np.asarray(rel_bias, f32)[31][None, :], (128, 8)))
    kk = np.arange(128)[:, None]
    cm = np.stack([np.where(i * 128 + kk <= np.arange(512)[None, :], 0.0, NEG) for i in range(4)], 1)
    cm = np.ascontiguousarray(cm.reshape(128, 2048).astype(f32)).astype(bf)
    oh = (np.arange(8192)[None, :] // 256 == np.arange(32)[:, None]).astype(f32).astype(bf)
    idb = np.eye(128, dtype=f32).astype(bf)
    inv_freq = (np.float32(10000.0) ** (-np.arange(0, 32, 2, dtype=f32) / np.float32(32))).astype(f32)
    common = {
        "w_in": c(w_in[0]), "w_uq": c(w_uq[0]), "w_ukv": c(w_ukv[0]), "w_bm": c(w_branch_moba[0]),
        "w_bl": c(w_branch_mla[0]), "w_out": c(w_out[0]), "w_up": c(w_up[0]), "w_dn": c(w_down[0]),
        "gA": c(np.asarray(norm_attn_g, f32)[0].reshape(8, 128).T),
        "gF": c(np.asarray(norm_ffn_g, f32)[0].reshape(8, 128).T),
        "gQ": c(np.asarray(q_norm_g, f32)[0].reshape(2, 128).T),
        "gKV": c(np.asarray(kv_norm_g, f32)[0].reshape(1, 128).T),
        "bg": c(np.asarray(b_gate, f32)[0].reshape(16, 128).T),
        "cw": c(np.transpose(np.asarray(conv_w, f32)[0].reshape(3, 44, 128), (2, 1, 0)).reshape(128, 132)),
        "cb": c(np.asarray(conv_b, f32)[0].reshape(44, 128).T),
        "gO": c(np.broadcast_to(np.asarray(norm_final_g, f32)[None, :], (128, 1024))),
        "b31": b31, "oh": oh, "nf": nf, "nfh": nfh, "cm": cm, "idb": idb,
    }
    in_maps = []
    for core in range(8):
        b, half = core // 2, core % 2
        xcat = np.zeros((8192, 1024), f32)
        if half == 1:
            xcat[:4096] = x[b, :4096]
        xcat[4096:] = x[b, half * 4096:(half + 1) * 4096]
        kval = np.ones((128, 64), f32)
        kval[:, :32] = float(half)
        gbias = np.zeros((128, 8, 32), f32)
        if half == 0:
            gbias[:, :, :16] = NEG
        pos = (np.arange(8192) if half == 1 else np.concatenate([np.arange(4096), np.arange(4096)])).astype(f32)
        ang = pos[:, None] * inv_freq[None, :]
        cs, sn = np.cos(ang).astype(f32), np.sin(ang).astype(f32)
        csk = np.concatenate([cs, sn], 1).reshape(64, 128, 32)
        s = np.float32(96.0 ** -0.5)
        csq = np.concatenate([np.tile(cs[31 * 128:], (1, 8)) * s, np.tile(sn[31 * 128:], (1, 8)) * s], 1).reshape(NQT, 128, 256)
        m = dict(common)
        m.update({"xc": xcat, "kval": kval, "gbias": c(gbias.reshape(128, 256)),
                  "hflag": np.full((128, 1), float(half), f32), "csk": c(csk), "csq": c(csq)})
        in_maps.append(m)
    if "nc" not in _NC_CACHE:
        _NC_CACHE["nc"] = build_program()
    nc = _NC_CACHE["nc"]
    ncores = int(os.environ.get("KCORES", "8"))
    res = run_bass_kernel_spmd(nc, in_maps[:ncores], core_ids=list(range(ncores)))
    out = np.empty((4, 8192, 1024), f32)
    for core in range(ncores):
        b, half = core // 2, core % 2
        out[b, half * 4096:(half + 1) * 4096] = res.results[core]["out"]
    if DEBUG:
        kernel.last = res
    return out
```
